# Optimizing a Trainium2 kernel written in Bass

```python
import jax, jax.numpy as jnp
from jax import lax
import numpy as np

D_MODEL = 1024
BATCH = 4
SEQ = 4096
DEPTH = 1

GRID_W = 64
NA_HEADS = 8
NA_HEAD_DIM = 64
NA_WIDTH = NA_HEADS * NA_HEAD_DIM
NA_WIN_ROWS = 8
NA_WIN_COLS = 16
F_GROUPS = 4
F_GROUP_DIM = 128
F_WIDTH = F_GROUPS * F_GROUP_DIM
MEM_TOKENS = 256
MEM_HEADS = 4
MEM_HEAD_DIM = 128
MEM_WIDTH = MEM_HEADS * MEM_HEAD_DIM
N_BRANCHES = 3
D_FF = 4 * D_MODEL
IN_WIDTH = 3 * NA_WIDTH + F_WIDTH + MEM_WIDTH + N_BRANCHES * D_MODEL
EPS = 1e-6
NEG_INF = -1e30

kernel_name = "hybrid_na_fourier_memory_gated_encoder"


def rmsnorm(x, g):
    xf = x.astype(jnp.float32)
    y = xf * lax.rsqrt(jnp.mean(xf * xf, axis=-1, keepdims=True) + EPS)
    return (y * g.astype(jnp.float32)).astype(x.dtype)


def neighbourhood_attention_2d(q, k, v, rpb):
    B, S, H, d = q.shape
    rows = S // GRID_W
    wr = min(NA_WIN_ROWS, rows)
    r = jnp.arange(rows)
    rs = jnp.clip(r - wr // 2, 0, rows - wr)
    row_idx = rs[:, None] + jnp.arange(wr)[None, :]
    c = jnp.arange(GRID_W)
    cs = jnp.clip(c - NA_WIN_COLS // 2, 0, GRID_W - NA_WIN_COLS)
    kc = jnp.arange(GRID_W)
    col_valid = (kc[None, :] >= cs[:, None]) & (kc[None, :] < cs[:, None] + NA_WIN_COLS)
    dr_idx = row_idx - r[:, None] + (NA_WIN_ROWS - 1)
    dc_idx = jnp.clip(kc[None, :] - c[:, None], -(NA_WIN_COLS - 1), NA_WIN_COLS - 1) + (NA_WIN_COLS - 1)
    bias = rpb[:, dr_idx[:, None, :, None], dc_idx[None, :, None, :]]

    qg = q.reshape(B, rows, GRID_W, H, d)
    kb = k.reshape(B, rows, GRID_W, H, d)[:, row_idx]
    vb = v.reshape(B, rows, GRID_W, H, d)[:, row_idx]
    scale = 1.0 / np.sqrt(d).astype(np.float32)
    s = jnp.einsum('brchd,brwkhd->bhrcwk', qg, kb).astype(jnp.float32) * scale
    s = s + bias[None].astype(jnp.float32)
    s = jnp.where(col_valid[:, None, :], s, NEG_INF)
    p = jax.nn.softmax(s, axis=(-2, -1)).astype(v.dtype)
    o = jnp.einsum('bhrcwk,brwkhd->brchd', p, vb)
    return o.reshape(B, S, H * d)


def fourier_mix(u):
    B, S, _ = u.shape
    ug = u.reshape(B, S, F_GROUPS, F_GROUP_DIM).astype(jnp.float32)
    yf = jnp.fft.fft2(ug, axes=(1, 3), norm='ortho').real
    return yf.reshape(B, S, F_WIDTH).astype(u.dtype)


def memory_cross_attention(q, k, v):
    B, S, H, d = q.shape
    scale = 1.0 / np.sqrt(d).astype(np.float32)
    s = jnp.einsum('bshd,bmhd->bhsm', q, k).astype(jnp.float32) * scale
    p = jax.nn.softmax(s, axis=-1).astype(v.dtype)
    o = jnp.einsum('bhsm,bmhd->bshd', p, v)
    return o.reshape(B, S, H * d)


def setup_inputs(seed: int = 0) -> dict:
    key = jax.random.key(seed)
    ks = jax.random.split(key, 20)
    f32 = jnp.float32

    def w(k, shape, fan_in):
        return jax.random.normal(k, shape, f32) * (fan_in ** -0.5)

    def gain(k, n):
        return 1.0 + 0.02 * jax.random.normal(k, (n,), f32)

    return {
        "x": jax.random.normal(ks[0], (BATCH, SEQ, D_MODEL), f32),
        "mem": jax.random.normal(ks[1], (BATCH, MEM_TOKENS, D_MODEL), f32),
        "norm1_g": gain(ks[2], D_MODEL),
        "w_in": w(ks[3], (D_MODEL, IN_WIDTH), D_MODEL),
        "b_gate": 0.01 * jax.random.normal(ks[4], (N_BRANCHES * D_MODEL,), f32),
        "na_q_g": gain(ks[5], NA_HEAD_DIM),
        "na_k_g": gain(ks[6], NA_HEAD_DIM),
        "na_rpb": 0.1 * jax.random.normal(ks[7], (NA_HEADS, 2 * NA_WIN_ROWS - 1, 2 * NA_WIN_COLS - 1), f32),
        "w_na_o": w(ks[8], (NA_WIDTH, D_MODEL), NA_WIDTH),
        "w_f": w(ks[9], (F_WIDTH, D_MODEL), F_WIDTH),
        "mem_norm_g": gain(ks[10], D_MODEL),
        "w_mem_kv": w(ks[11], (D_MODEL, 2 * MEM_WIDTH), D_MODEL),
        "mem_q_g": gain(ks[12], MEM_HEAD_DIM),
        "mem_k_g": gain(ks[13], MEM_HEAD_DIM),
        "w_mem_o": w(ks[14], (MEM_WIDTH, D_MODEL), MEM_WIDTH),
        "w_out": w(ks[15], (D_MODEL, D_MODEL), D_MODEL),
        "norm2_g": gain(ks[16], D_MODEL),
        "w_ff1": w(ks[17], (D_MODEL, D_FF), D_MODEL),
        "w_ff2": w(ks[18], (D_FF, D_MODEL), D_FF),
    }


def reference(x, mem, norm1_g, w_in, b_gate, na_q_g, na_k_g, na_rpb, w_na_o, w_f,
              mem_norm_g, w_mem_kv, mem_q_g, mem_k_g, w_mem_o, w_out,
              norm2_g, w_ff1, w_ff2):
    B, S, _ = x.shape
    M = mem.shape[1]
    mem_n = rmsnorm(mem, mem_norm_g)
    mem_kv = mem_n @ w_mem_kv
    for _ in range(DEPTH):
        h = rmsnorm(x, norm1_g)
        z = h @ w_in
        o0 = 0
        na_q = z[..., o0:o0 + NA_WIDTH]; o0 += NA_WIDTH
        na_k = z[..., o0:o0 + NA_WIDTH]; o0 += NA_WIDTH
        na_v = z[..., o0:o0 + NA_WIDTH]; o0 += NA_WIDTH
        f_u = z[..., o0:o0 + F_WIDTH]; o0 += F_WIDTH
        m_q = z[..., o0:o0 + MEM_WIDTH]; o0 += MEM_WIDTH
        gates = jax.nn.sigmoid(z[..., o0:] + b_gate).reshape(B, S, N_BRANCHES, D_MODEL)

        qa = rmsnorm(na_q.reshape(B, S, NA_HEADS, NA_HEAD_DIM), na_q_g)
        ka = rmsnorm(na_k.reshape(B, S, NA_HEADS, NA_HEAD_DIM), na_k_g)
        va = na_v.reshape(B, S, NA_HEADS, NA_HEAD_DIM)
        y_na = neighbourhood_attention_2d(qa, ka, va, na_rpb) @ w_na_o

        y_f = fourier_mix(f_u) @ w_f

        qm = rmsnorm(m_q.reshape(B, S, MEM_HEADS, MEM_HEAD_DIM), mem_q_g)
        km = rmsnorm(mem_kv[..., :MEM_WIDTH].reshape(B, M, MEM_HEADS, MEM_HEAD_DIM), mem_k_g)
        vm = mem_kv[..., MEM_WIDTH:].reshape(B, M, MEM_HEADS, MEM_HEAD_DIM)
        y_mem = memory_cross_attention(qm, km, vm) @ w_mem_o

        merged = gates[:, :, 0] * y_na + gates[:, :, 1] * y_f + gates[:, :, 2] * y_mem
        x = x + merged @ w_out

        h2 = rmsnorm(x, norm2_g)
        x = x + jnp.square(jax.nn.relu(h2 @ w_ff1)) @ w_ff2
    return x
```

```python
from contextlib import ExitStack

import numpy as np
import ml_dtypes

import concourse.bass as bass
import concourse.mybir as mybir
from concourse.bass_utils import run_bass_kernel_spmd

F32 = mybir.dt.float32
BF16 = mybir.dt.bfloat16
AF = mybir.ActivationFunctionType
ALU = mybir.AluOpType

SAME_ENG_SYNC = True
EPS = 1e-6
NEG = -1e30


class Buf:
    __slots__ = ("name", "lw", "rds", "dsem", "dcnt")

    def __init__(self, name):
        self.name = name
        self.lw = None
        self.rds = {}
        self.dsem = None
        self.dcnt = 0


class EngQ:
    def __init__(self, name, sem, is_pe=False):
        self.name = name
        self.sem = sem
        self.n = 0
        self.waited = {}
        self.ops = []
        self.is_pe = is_pe


class Sched:
    def __init__(self, nc, stack):
        self.nc = nc
        self.stack = stack
        self.q = {}
        for name in ("pe", "act", "dve", "pool", "sp"):
            sem = stack.enter_context(nc.semaphore("q_" + name))
            self.q[name] = EngQ(name, sem, is_pe=(name == "pe"))
        self.nsem = 5
        self.free_sems = []

    def newsem(self, name):
        self.nsem += 1
        return self.stack.enter_context(self.nc.semaphore(name))

    def _collect(self, q, reads, writes):
        deps = {}

        def add(t):
            if t is None:
                return
            sem, val = t
            if sem is q.sem and (q.is_pe or not SAME_ENG_SYNC):
                return
            k = id(sem)
            if q.waited.get(k, 0) >= val:
                return
            if k not in deps or deps[k][1] < val:
                deps[k] = (sem, val)

        for b in reads:
            add(b.lw)
        for b in writes:
            add(b.lw)
            for t in b.rds.values():
                add(t)
        out = list(deps.values())
        for sem, val in out:
            q.waited[id(sem)] = val
        return out

    def _update(self, tok, reads, writes):
        for b in writes:
            b.lw = tok
            b.rds = {}
        k = id(tok[0])
        for b in reads:
            if k not in b.rds or b.rds[k][1] < tok[1]:
                b.rds[k] = tok

    def op(self, qn, fn, reads=(), writes=(), inc=True):
        q = self.q[qn]
        waits = self._collect(q, reads, writes)
        tok = (q.sem, q.n + 1)
        if inc:
            q.n += 1
        else:
            assert q.is_pe
        q.ops.append((waits, fn, (q.sem, 1) if inc else None))
        self._update(tok, reads, writes)

    def dma(self, qn, out, in_, reads=(), writes=(), key=None, **kw):
        q = self.q[qn]
        waits = self._collect(q, reads, writes)
        kb = key if key is not None else (writes[0] if writes else reads[0])
        if kb.dsem is None:
            kb.dsem = self.newsem("d_" + kb.name)
        kb.dcnt += 16
        tok = (kb.dsem, kb.dcnt)

        def fn(e, out=out, in_=in_, kw=kw):
            return e.dma_start(out=out, in_=in_, **kw)

        q.ops.append((waits, fn, (kb.dsem, 16)))
        self._update(tok, reads, writes)
        return tok

    def claim(self, new_bufs, old_bufs):
        toks = {}
        for b in old_bufs:
            for t in ([b.lw] if b.lw else []) + list(b.rds.values()):
                k = id(t[0])
                if k not in toks or toks[k][1] < t[1]:
                    toks[k] = t
        for nb in new_bufs:
            for k, t in toks.items():
                if k not in nb.rds or nb.rds[k][1] < t[1]:
                    nb.rds[k] = t

    def wait_all(self, qn, bufs):
        q = self.q[qn]
        waits = self._collect(q, (), bufs)
        q.ops.append((waits, None, None))

    def emit(self):
        nc = self.nc
        qs = self.q

        def run(q, e):
            for waits, fn, inc in q.ops:
                for sem, val in waits:
                    e.wait_ge(sem, val)
                if fn is None:
                    continue
                ins = fn(e)
                if inc is not None:
                    ins.then_inc(inc[0], inc[1])

        with nc.Block() as block:
            @block.tensor
            def _(e):
                run(qs["pe"], e)

            @block.scalar
            def _(e):
                run(qs["act"], e)

            @block.vector
            def _(e):
                run(qs["dve"], e)

            @block.gpsimd
            def _(e):
                run(qs["pool"], e)

            @block.sync
            def _(e):
                run(qs["sp"], e)


D = 1024
SEQ = 4096
NOWN = 2048
NFR = 20
NA_VARIANTS = [(0, 6), (1, 5), (2, 5), (14, 5), (14, 6)]
NA_VOFF = [0, 6, 11, 16, 21]
NA_NCH = 27


def na_band(tl):
    if tl == 0:
        return 0, 6, NA_VOFF[0]
    if tl == 1:
        return 1, 5, NA_VOFF[1]
    if tl == 14:
        return 14, 5, NA_VOFF[3]
    if tl == 15:
        return 14, 6, NA_VOFF[4]
    return tl, 5, NA_VOFF[2]


V_G1, V_G2, V_GM, V_BG, V_QG, V_KG, V_MQG, V_MKG = 0, 8, 16, 24, 48, 49, 50, 51
NVEC = 64

A_OFF, A_SZ = 0, 20480
BZ_OFF, BZ_SZ = 20480, 16384
D_OFF, D_SZ = 36864, 16384
C_OFF, C_SZ = 53248, 20480
G_OFF, G_SZ = 73728, 4096
E_OFF, E_SZ = 77824, 8192
ARENA = 86016


def build(stop=99, dbg=False):
    nc = bass.Bass("TRN2", target_bir_lowering=False)

    def dram(n, s, d, kind="ExternalInput"):
        return nc.dram_tensor(n, s, d, kind=kind).ap()

    xr = dram("xr", [SEQ, D], F32)
    mem = dram("mem", [256, D], F32)
    w_in = dram("w_in", [D, 5632], F32)
    w_na_o = dram("w_na_o", [512, D], F32)
    w_f = dram("w_f", [512, D], F32)
    w_mem_o = dram("w_mem_o", [512, D], F32)
    w_mem_kv = dram("w_mem_kv", [D, D], F32)
    w_out = dram("w_out", [D, D], F32)
    w_ff1 = dram("w_ff1", [D, 4096], F32)
    w_ff2 = dram("w_ff2", [4096, D], F32)
    vecs_d = dram("vecs", [128, NVEC], F32)
    nab_d = dram("nabias", [4, 128, 2 * NA_NCH, 128], F32)
    fft_d = dram("fftab", [4, 128, 2, 8, 512], BF16)
    cst_d = dram("consts", [128, 5, 128], BF16)
    gbc_d = dram("gbc", [3, 128, D], F32)
    y = dram("y", [NOWN, D], F32, kind="ExternalOutput")
    if dbg:
        dbg_d = dram("dbg", [128, 8192], F32, kind="ExternalOutput")

    def wview(w):
        return w.rearrange("(k p) c -> p k c", p=128)

    st = ExitStack()
    with st:
        S = Sched(nc, st)

        def sb(n, s, d):
            return st.enter_context(nc.sbuf_tensor(n, s, d))

        ar = sb("arena", [128, ARENA], BF16)

        def reg(off, n):
            return ar[:, off:off + n]

        cst = sb("cst", [128, 5, 128], BF16)
        vec = sb("vec", [128, NVEC], F32)
        vec2 = sb("vec2", [128, 4], F32)
        stat = sb("stat", [128, 64], F32)
        rstd = sb("rstd", [128, 64], F32)
        gbc = sb("gbcsb", [128, D], F32)
        Bgbc = Buf("gbc")
        NXA = 6
        xt = [reg(D_OFF + i * 2048, 2048).bitcast(F32) for i in range(NXA)]
        xs = [sb("xs%d" % i, [128, D], BF16) for i in range(3)]
        tmpf = [sb("tmpf%d" % i, [128, 512], F32) for i in range(5)]
        tmpb = [sb("tmpb%d" % i, [128, 512], BF16) for i in range(4)]
        PT = [sb("PT%d" % i, [128, 1536], BF16) for i in range(2)]
        kmT = sb("kmT", [128, 4, 256], BF16)
        vm = sb("vm", [128, 2, 512], BF16)
        onb = [sb("onb%d" % i, [128, 128], BF16) for i in range(2)]
        rc = sb("rc", [128, 4], F32)
        ps = [st.enter_context(nc.psum_tensor("ps%d" % i, [128, 512], F32)) for i in range(8)]

        Bcst, Bvec, Bvec2 = Buf("cst"), Buf("vec"), Buf("vec2")
        Bxt = [Buf("xt%d" % i) for i in range(NXA)]
        Bxs = [Buf("xs%d" % i) for i in range(3)]
        Btf = [Buf("tmpf%d" % i) for i in range(5)]
        Btb = [Buf("tmpb%d" % i) for i in range(4)]
        BPT = [Buf("PT%d" % i) for i in range(2)]
        Bps = [Buf("ps%d" % i) for i in range(8)]
        BkmT, Bvm = Buf("kmT"), Buf("vm")
        Bonb = [Buf("onb%d" % i) for i in range(2)]
        Brc = [Buf("rc%d" % i) for i in range(2)]
        Bstat = [Buf("stat%d" % i) for i in range(64)]

        ident = cst[:, 0, :]
        blk64 = cst[:, 1, :]
        ones = cst[:, 2, :]
        CcM = cst[:, 3, :]
        ScM = cst[:, 4, :]

        ctr = {"ps": 0, "tf": 0, "tb": 0, "xt": 0, "xs": 0, "st": 0, "ev": 0}

        def nxt(key, n):
            i = ctr[key] % n
            ctr[key] += 1
            return i

        def next_ps():
            i = nxt("ps", 8)
            return ps[i], Bps[i]

        def next_tf():
            i = nxt("tf", 5)
            return tmpf[i], Btf[i]

        def next_tb():
            i = nxt("tb", 4)
            return tmpb[i], Btb[i]

        def next_stat():
            i = nxt("st", 64)
            return i, Bstat[i]

        S.dma("sp", cst[:], cst_d, writes=[Bcst])
        S.dma("sp", vec[:], vecs_d, writes=[Bvec])
        S.op("dve", lambda e: e.tensor_scalar(out=vec2[:, 0:1], in0=vec[:, V_QG:V_QG + 1], scalar1=0.125, scalar2=None, op0=ALU.mult),
             reads=[Bvec], writes=[Bvec2])
        S.op("dve", lambda e: e.tensor_scalar(out=vec2[:, 1:2], in0=vec[:, V_MQG:V_MQG + 1], scalar1=float(128 ** -0.5), scalar2=None, op0=ALU.mult),
             reads=[Bvec], writes=[Bvec2])

        junk = PT[0][:, 0:D]

        def tok_norm_a(src_ap, srcBs):
            si, sB = next_stat()
            xi = nxt("xs", 3)
            S.op("act", lambda e: e.activation(out=junk, in_=src_ap, func=AF.Square, accum_out=stat[:, si:si + 1]),
                 reads=srcBs, writes=[sB, BPT[0]])
            S.op("act", lambda e: e.activation(out=stat[:, si:si + 1], in_=stat[:, si:si + 1], func=AF.Sqrt, scale=1.0 / D, bias=EPS),
                 reads=[sB], writes=[sB])
            S.op("dve", lambda e: e.reciprocal(out=rstd[:, si:si + 1], in_=stat[:, si:si + 1]), reads=[sB], writes=[sB])
            S.op("dve", lambda e: e.scalar_tensor_tensor(out=xs[xi][:], in0=src_ap, scalar=rstd[:, si:si + 1], in1=gbc[:],
                                                          op0=ALU.mult, op1=ALU.mult),
                 reads=list(srcBs) + [sB, Bgbc], writes=[Bxs[xi]])
            return xi

        def tok_norm_b(xi, gcol, dst_fn, dstB, bank=None):
            p, pB = next_ps() if bank is None else (ps[bank], Bps[bank])
            pb = p[:].bitcast(BF16)
            for k in range(8):
                S.op("pe", lambda e, k=k: e.transpose(out=pb[:, k * 128:(k + 1) * 128], in_=xs[xi][:, k * 128:(k + 1) * 128], identity=ident),
                     reads=[Bxs[xi], Bcst], writes=[pB], inc=(k == 7))
            ev = nxt("ev", 2)
            if ev == 0:
                S.op("act", lambda e: e.activation(out=dst_fn, in_=pb.rearrange("p (k t) -> p k t", k=8), func=AF.Copy),
                     reads=[pB], writes=[dstB])
            else:
                S.op("dve", lambda e: e.tensor_copy(out=dst_fn, in_=pb.rearrange("p (k t) -> p k t", k=8)), reads=[pB], writes=[dstB])

        def tok_norm_transpose(src_ap, srcB, gcol, dst_fn, dstB):
            xi = tok_norm_a(src_ap, [srcB])
            tok_norm_b(xi, gcol, dst_fn, dstB)

        def fm_norm(p, pB, ncols, mat, inv_d, gain_ap, gainB, out_ap, outB, split=None, split3=False):
            sq, sqB = next_tb()
            S.op("act", lambda e: e.activation(out=sq[:, :ncols], in_=p[:, :ncols], func=AF.Square), reads=[pB], writes=[sqB])
            p2, p2B = next_ps()
            S.op("pe", lambda e: e.matmul(p2[:, :ncols], lhsT=mat, rhs=sq[:, :ncols], start=True, stop=True),
                 reads=[sqB, Bcst], writes=[p2B])
            t, tB = next_tf()
            S.op("act", lambda e: e.activation(out=t[:, :ncols], in_=p2[:, :ncols], func=AF.Ln, scale=inv_d, bias=EPS),
                 reads=[p2B], writes=[tB])
            S.op("act", lambda e: e.activation(out=t[:, :ncols], in_=t[:, :ncols], func=AF.Exp, scale=-0.5), reads=[tB], writes=[tB])
            if split is None:
                S.op("dve", lambda e: e.scalar_tensor_tensor(out=out_ap, in0=p[:, :ncols], scalar=gain_ap, in1=t[:, :ncols],
                                                              op0=ALU.mult, op1=ALU.mult),
                     reads=[pB, tB, gainB], writes=[outB])
            else:
                for (pr, oap) in split:
                    i0 = p[pr, :ncols]
                    i1 = t[pr, :ncols]
                    if split3:
                        i0 = i0.rearrange("p (t q) -> p t q", q=128)
                        i1 = i1.rearrange("p (t q) -> p t q", q=128)
                    S.op("dve", lambda e, pr=pr, oap=oap, i0=i0, i1=i1: e.scalar_tensor_tensor(out=oap, in0=i0, scalar=gain_ap[pr], in1=i1,
                                                                                             op0=ALU.mult, op1=ALU.mult),
                         reads=[pB, tB, gainB], writes=[outB])

        def proj_fm(wt, wB, ncolchunk, rhs_fn, rhsBs, p, pB, ncols):
            for k in range(8):
                S.op("pe", lambda e, k=k: e.matmul(p[:, :ncols], lhsT=wt[:, k, ncolchunk], rhs=rhs_fn(k), start=(k == 0), stop=(k == 7)),
                     reads=[wB] + list(rhsBs), writes=[pB], inc=(k == 7))

        def proj_norm_batch(items):
            n = len(items)
            P = []
            for it in items:
                p, pB = next_ps()
                proj_fm(it["w"], it["wB"], slice(0, 128), it["rhs_fn"], it["rhsBs"], p, pB, 512)
                P.append((p, pB))
            SQ = []
            for it, (p, pB) in zip(items, P):
                sq, sqB = next_tb()
                S.op("act", lambda e, sq=sq, p=p: e.activation(out=sq[:], in_=p[:], func=AF.Square), reads=[pB], writes=[sqB])
                SQ.append((sq, sqB))
            P2 = []
            for it, (sq, sqB) in zip(items, SQ):
                p2, p2B = next_ps()
                S.op("pe", lambda e, p2=p2, sq=sq, it=it: e.matmul(p2[:], lhsT=it["mat"], rhs=sq[:], start=True, stop=True),
                     reads=[sqB, Bcst], writes=[p2B])
                P2.append((p2, p2B))
            T = []
            for it, (p2, p2B) in zip(items, P2):
                t, tB = next_tf()
                S.op("act", lambda e, t=t, p2=p2, it=it: e.activation(out=t[:], in_=p2[:], func=AF.Ln, scale=it["inv_d"], bias=EPS),
                     reads=[p2B], writes=[tB])
                S.op("act", lambda e, t=t: e.activation(out=t[:], in_=t[:], func=AF.Exp, scale=-0.5), reads=[tB], writes=[tB])
                T.append((t, tB))
            for it, (p, pB), (t, tB) in zip(items, P, T):
                if it.get("split") is None:
                    S.op("dve", lambda e, it=it, p=p, t=t: e.scalar_tensor_tensor(out=it["out"], in0=p[:], scalar=it["gain"], in1=t[:],
                                                                                  op0=ALU.mult, op1=ALU.mult),
                         reads=[pB, tB, it["gainB"]], writes=[it["outB"]])
                else:
                    for (pr, oap) in it["split"]:
                        i0 = p[pr, :].rearrange("p (t q) -> p t q", q=128)
                        i1 = t[pr, :].rearrange("p (t q) -> p t q", q=128)
                        S.op("dve", lambda e, it=it, pr=pr, oap=oap, i0=i0, i1=i1: e.scalar_tensor_tensor(
                            out=oap, in0=i0, scalar=it["gain"][pr], in1=i1, op0=ALU.mult, op1=ALU.mult),
                            reads=[pB, tB, it["gainB"]], writes=[it["outB"]])

        dbg_off = [0]

        def dump(ap, B, n):
            for c0 in range(0, n, 512):
                w = min(512, n - c0)
                t, tB = next_tf()
                S.op("act", lambda e, c0=c0, w=w, t=t: e.activation(out=t[:, :w], in_=ap[:, c0:c0 + w], func=AF.Copy), reads=B, writes=[tB])
                o = dbg_off[0]
                S.dma("sp", dbg_d[:, o:o + w], t[:, :w], reads=[tB], writes=[Bdbg])
                dbg_off[0] += w

        Bdbg = Buf("dbg")
        By = [Buf("y%d" % i) for i in range(16)]

        def finish():
            S.wait_all("sp", By + [Bdbg])
            S.emit()

        wkv = reg(C_OFF, 8192).rearrange("p (k c) -> p k c", k=8)
        Bwkv = Buf("wkv")
        S.dma("pool", wkv, wview(w_mem_kv), writes=[Bwkv])
        S.dma("sp", gbc[:], gbc_d[2], writes=[Bgbc])
        memT = reg(BZ_OFF + 12288, 2048).rearrange("p (k t) -> p k t", k=8)
        BmemT = [Buf("memT%d" % i) for i in range(2)]
        for mt in range(2):
            xi = nxt("xt", NXA)
            S.dma("sp", xt[xi], mem[mt * 128:(mt + 1) * 128, :], writes=[Bxt[xi]])
            tok_norm_transpose(xt[xi], Bxt[xi], V_GM, memT[:, :, mt * 128:(mt + 1) * 128], BmemT[mt])
        for h in range(4):
            p, pB = next_ps()
            proj_fm(wkv, Bwkv, slice(h * 128, (h + 1) * 128), lambda k: memT[:, k, :], BmemT, p, pB, 256)
            fm_norm(p, pB, 256, ones, 1.0 / 128, vec[:, V_MKG:V_MKG + 1], Bvec, kmT[:, h, :], BkmT)
        for c in range(2):
            p, pB = next_ps()
            for k in range(8):
                S.op("pe", lambda e, k=k, c=c, p=p: e.matmul(p[:], lhsT=memT[:, k, c * 128:(c + 1) * 128], rhs=wkv[:, k, 512:1024],
                                                          start=(k == 0), stop=(k == 7)),
                     reads=[BmemT[c], Bwkv], writes=[pB], inc=(k == 7))
            S.op("act", lambda e, c=c, p=p: e.activation(out=vm[:, c, :], in_=p[:], func=AF.Copy), reads=[pB], writes=[Bvm])

        hTf = reg(A_OFF, A_SZ).rearrange("p (k t) -> p k t", k=8)
        hTo = reg(BZ_OFF, 12288).rearrange("p (k t) -> p k t", k=8)
        BhT = [Buf("hT%d" % i) for i in range(32)]

        def hT_tile(lt):
            f = (lt + 2) % 32
            if f < NFR:
                return hTf[:, :, f * 128:(f + 1) * 128]
            return hTo[:, :, (lt - 18) * 128:(lt - 17) * 128]

        BhTf = [BhT[(f - 2) % 32] for f in range(NFR)]

        S.dma("sp", gbc[:], gbc_d[0], writes=[Bgbc])
        wu = reg(G_OFF, 4096).rearrange("p (k c) -> p k c", k=8)
        Bwu = Buf("wu")
        S.dma("pool", wu, wview(w_in)[:, :, 1536:2048], writes=[Bwu])

        var = reg(C_OFF, C_SZ).rearrange("p (v i c) -> p v i c", v=5, i=8)
        Bvar = [[Buf("var%d_%d" % (v, i)) for i in range(8)] for v in range(5)]
        S.claim([b for r in Bvar for b in r], [Bwkv])

        order = [8 * e4 + i for i in range(8) for e4 in range(4)]
        pend = {}

        def emitN(n):
            lt = order[n]
            xi = nxt("xt", NXA)
            S.dma("sp", xt[xi], xr[lt * 128:(lt + 1) * 128, :], writes=[Bxt[xi]])
            pend[n] = tok_norm_a(xt[xi], [Bxt[xi]])

        def emitT(n, bank=None):
            lt = order[n]
            tok_norm_b(pend.pop(n), V_G1, hT_tile(lt), BhT[lt], bank=bank)

        def emit_combos(i, U):
            t0, t0B = next_tf()
            t1, t1B = next_tf()
            S.op("act", lambda e: e.activation(out=t0[:], in_=U[0][0][:], func=AF.Copy), reads=[U[0][1]], writes=[t0B])
            S.op("act", lambda e: e.activation(out=t1[:], in_=U[1][0][:], func=AF.Copy), reads=[U[1][1]], writes=[t1B])
            fa, faB = next_tf()
            fb, fbB = next_tf()
            S.op("dve", lambda e: e.tensor_tensor(out=fa[:], in0=t0[:], in1=U[2][0][:], op=ALU.add), reads=[t0B, U[2][1]], writes=[faB])
            S.op("dve", lambda e: e.tensor_tensor(out=var[:, 2, i, :], in0=t0[:], in1=U[2][0][:], op=ALU.subtract),
                 reads=[t0B, U[2][1]], writes=[Bvar[2][i]])
            S.op("dve", lambda e: e.tensor_tensor(out=fb[:], in0=t1[:], in1=U[3][0][:], op=ALU.add), reads=[t1B, U[3][1]], writes=[fbB])
            S.op("dve", lambda e: e.tensor_tensor(out=var[:, 3, i, :], in0=t1[:], in1=U[3][0][:], op=ALU.subtract),
                 reads=[t1B, U[3][1]], writes=[Bvar[3][i]])
            S.op("dve", lambda e: e.tensor_tensor(out=var[:, 0, i, :], in0=fa[:], in1=fb[:], op=ALU.add), reads=[faB, fbB], writes=[Bvar[0][i]])
            S.op("dve", lambda e: e.tensor_tensor(out=var[:, 1, i, :], in0=fa[:], in1=fb[:], op=ALU.subtract), reads=[faB, fbB], writes=[Bvar[1][i]])
            S.op("dve", lambda e: e.tensor_scalar(out=var[:, 4, i, :], in0=var[:, 3, i, :], scalar1=-1.0, scalar2=None, op0=ALU.mult),
                 reads=[Bvar[3][i]], writes=[Bvar[4][i]])

        emitN(0)
        Upend = []
        pend_up = []

        def emit_uproj(n):
            i, e4 = n // 4, n % 4
            lt = order[n]
            bk = 4 * (i % 2) + e4
            p, pB = ps[bk], Bps[bk]
            hv = hT_tile(lt)
            for k in range(8):
                S.op("pe", lambda e, k=k, p=p, hv=hv: e.matmul(p[:], lhsT=hv[:, k, :], rhs=wu[:, k, :], start=(k == 0), stop=(k == 7)),
                     reads=[BhT[lt], Bwu], writes=[pB], inc=(k == 7))
            if e4 == 3:
                Upend.append((i, [(ps[4 * (i % 2) + j], Bps[4 * (i % 2) + j]) for j in range(4)]))

        for n in range(32):
            if n + 1 < 32:
                emitN(n + 1)
            emitT(n)
        if stop >= 2:
            for n in range(32):
                emit_uproj(n)
                if n % 4 == 3 and len(Upend) > 1:
                    emit_combos(*Upend.pop(0))
            while Upend:
                emit_combos(*Upend.pop(0))

        if stop <= 2:
            if dbg:
                dump(hTf[:, 0, :], BhTf, 2560)
                if stop == 2:
                    for v in range(5):
                        dump(var[:, v, 0, :], [Bvar[v][0]], 512)
            finish()
            return nc

        Zt = reg(BZ_OFF, BZ_SZ).rearrange("p (r g t) -> p r g t", r=2, g=4)
        BZ = [[[Buf("Z%d_%d_%d" % (r, g, v)) for v in range(4)] for g in range(4)] for r in range(2)]
        S.claim([b for r in BZ for gg in r for b in gg], BhT[18:30] + BmemT)
        tabs = [reg(D_OFF + s * 8192, 8192).rearrange("p (a i w) -> p a i w", a=2, i=8) for s in range(2)]
        Btab = [Buf("ftab%d" % s) for s in range(2)]
        S.claim(Btab, Bxt)
        CLS = {
            0: ([(0, 0)], [(0, 1)]),
            2: ([(1, 0)], [(1, 1)]),
            1: ([(2, 0), (3, 1)], [(4, 0), (2, 1)]),
            3: ([(2, 0), (4, 1)], [(3, 0), (2, 1)]),
        }
        for ci, v in enumerate([0, 2, 1, 3]):
            s = ci % 2
            S.dma("sp", tabs[s], fft_d[v], writes=[Btab[s]])
            for g in range(4):
                for ri in range(2):
                    terms = CLS[v][ri]
                    p, pB = next_ps()
                    n = len(terms) * 8
                    j = 0
                    for (vi, ab) in terms:
                        for i in range(8):
                            S.op("pe", lambda e, vi=vi, ab=ab, i=i, g=g, p=p, j=j, n=n, s=s: e.matmul(
                                p[:], lhsT=var[:, vi, i, g * 128:(g + 1) * 128], rhs=tabs[s][:, ab, i, :], start=(j == 0), stop=(j == n - 1)),
                                reads=[Bvar[vi][i], Btab[s]], writes=[pB], inc=(j == n - 1))
                            j += 1
                    zo = Zt[:, ri, g, :].rearrange("p (w v) -> p v w", v=4)[:, v, :]
                    S.op("act", lambda e, zo=zo, p=p: e.activation(out=zo, in_=p[:], func=AF.Copy), reads=[pB], writes=[BZ[ri][g][v]])
        BZg = [[BZ[r][g] for g in range(4)] for r in range(2)]

        if stop <= 3:
            if dbg:
                dump(Zt[:, 0, 0, :], BZ[0][0], 2048)
                dump(Zt[:, 1, 1, :], BZ[1][1], 2048)
            finish()
            return nc

        Va = reg(C_OFF, 10400).rearrange("p (f h e) -> p f h e", f=NFR, h=8)
        BVa = [Buf("Va%d" % f) for f in range(NFR)]
        qzz = reg(C_OFF + 10400, 4096).rearrange("p (t h q) -> p t h q", t=16, h=2)
        kT = reg(C_OFF + 10400 + 4096, 2560)
        BqT = [Buf("qT%d" % t) for t in range(4)]
        BkT = [Buf("kT%d" % t) for t in range(5)]
        allvar = [b for r in Bvar for b in r]
        S.claim(BVa + BqT + BkT, allvar)
        wv = reg(G_OFF, 4096).rearrange("p (k c) -> p k c", k=8)
        Bwv = Buf("wv")
        S.claim([Bwv], [Bwu])
        S.dma("pool", wv, wview(w_in)[:, :, 1024:1536], writes=[Bwv])
        nab = reg(E_OFF, 2 * NA_NCH * 128).rearrange("p (c h q) -> p c h q", h=2, c=NA_NCH)
        Bnab = Buf("nab")
        oT = reg(D_OFF, 8192).rearrange("p (k t) -> p k t", k=4)
        BoT = [[Buf("oT%d_%d" % (k, t)) for t in range(16)] for k in range(4)]
        omT = reg(D_OFF + 8192, 8192).rearrange("p (k t) -> p k t", k=4)
        BomT = [[Buf("omT%d_%d" % (k, t)) for t in range(4)] for k in range(4)]
        S.claim([b for r in BoT for b in r], [Btab[0]])
        S.claim([b for r in BomT for b in r], [Btab[1]])
        wf = reg(D_OFF + 8192, 4096).rearrange("p (g c) -> p g c", g=4)
        Bwf = Buf("wf")
        S.claim([Bwf], [Btab[1]])
        S.dma("pool", wf, w_f.rearrange("(g p) c -> p g c", p=128), writes=[Bwf])

        S.op("dve", lambda e: e.memset(Va[:, :, :, 64:65], 1.0), reads=[], writes=BVa)
        S.op("dve", lambda e: e.memset(reg(C_OFF + 10400, 4096), 0.0), reads=[], writes=BqT)
        for f in range(NFR):
            p, pB = next_ps()
            for k in range(8):
                S.op("pe", lambda e, k=k, f=f, p=p: e.matmul(p[:], lhsT=hTf[:, k, f * 128:(f + 1) * 128], rhs=wv[:, k, :], start=(k == 0), stop=(k == 7)),
                     reads=[BhTf[f], Bwv], writes=[pB], inc=(k == 7))
            S.op("act", lambda e, f=f, p=p: e.activation(out=Va[:, f, :, 0:64], in_=p[:].rearrange("p (h d) -> p h d", h=8), func=AF.Copy),
                 reads=[pB], writes=[BVa[f]])

        wsl = [reg(G_OFF + i * 1024, 1024).rearrange("p (k c) -> p k c", k=8) for i in range(4)]
        Bwsl = [Buf("wsl%d" % i) for i in range(4)]
        S.claim(Bwsl, [Bwv])

        H0, H1 = slice(0, 64), slice(64, 128)
        for hp in range(4):
            s = hp % 2
            wq, wk = wsl[2 * s], wsl[2 * s + 1]
            BwqB, BwkB = Bwsl[2 * s], Bwsl[2 * s + 1]
            S.dma("pool", wq, wview(w_in)[:, :, hp * 128:(hp + 1) * 128], writes=[BwqB])
            S.dma("pool", wk, wview(w_in)[:, :, 512 + hp * 128:512 + (hp + 1) * 128], writes=[BwkB])
            S.dma("pool", nab.rearrange("p c h q -> p (c h) q"), nab_d[hp], writes=[Bnab])
            qitems = []
            for tb in range(4):
                qitems.append(dict(w=wq, wB=BwqB, rhs_fn=(lambda k, tb=tb: hTf[:, k, 256 + tb * 512:256 + (tb + 1) * 512]),
                                   rhsBs=BhTf[2 + 4 * tb:6 + 4 * tb], mat=blk64, inv_d=1.0 / 64, gain=vec2[:, 0:1], gainB=Bvec2, outB=BqT[tb],
                                   split=[(H0, qzz[H0, 4 * tb:4 * tb + 4, 0, :]), (H1, qzz[H1, 4 * tb:4 * tb + 4, 1, :])]))
            kitems = []
            for fb in range(5):
                kitems.append(dict(w=wk, wB=BwkB, rhs_fn=(lambda k, fb=fb: hTf[:, k, fb * 512:(fb + 1) * 512]),
                                   rhsBs=BhTf[4 * fb:4 * fb + 4], mat=blk64, inv_d=1.0 / 64, gain=vec[:, V_KG:V_KG + 1], gainB=Bvec,
                                   out=kT[:, fb * 512:(fb + 1) * 512], outB=BkT[fb]))
            proj_norm_batch(kitems[0:3])
            proj_norm_batch(kitems[3:5] + qitems[0:1])
            proj_norm_batch(qitems[1:4])

            def emitS(tl):
                b0, nb_, voff = na_band(tl)
                par = tl % 2
                for j in range(nb_):
                    bk = 3 * par + j // 2
                    pp, ppB = ps[bk], Bps[bk]
                    col = (j % 2) * 256
                    f = b0 + j
                    last = (j % 2 == 1) or (j == nb_ - 1)
                    S.op("pe", lambda e, pp=pp, col=col, f=f, tl=tl: e.matmul(
                        pp[:, col:col + 256], lhsT=kT[:, f * 128:(f + 1) * 128], rhs=qzz[:, tl].rearrange("p h q -> p (h q)"), start=True, stop=False),
                        reads=[BkT[f // 4], BqT[tl // 4]], writes=[ppB], inc=False)
                    S.op("pe", lambda e, pp=pp, col=col, j=j, voff=voff: e.matmul(
                        pp[:, col:col + 256], lhsT=ident, rhs=nab[:, voff + j].rearrange("p h q -> p (h q)"), start=False, stop=True),
                        reads=[Bcst, Bnab], writes=[ppB], inc=last)
                Pt, PtB = PT[par], BPT[par]
                for bi in range((nb_ + 1) // 2):
                    bk = 3 * par + bi
                    ncol = min(512, (nb_ - 2 * bi) * 256)
                    S.op("act", lambda e, Pt=Pt, bk=bk, bi=bi, ncol=ncol: e.activation(out=Pt[:, bi * 512:bi * 512 + ncol], in_=ps[bk][:, 0:ncol], func=AF.Exp),
                         reads=[Bps[bk]], writes=[PtB])

            def emitPV(tl):
                b0, nb_, voff = na_band(tl)
                par = tl % 2
                po, poB = ps[6 + par], Bps[6 + par]
                Pt, PtB = PT[par], BPT[par]
                for hh in range(2):
                    h = 2 * hp + hh
                    for j in range(nb_):
                        f = b0 + j
                        S.op("pe", lambda e, po=po, hh=hh, j=j, f=f, h=h, Pt=Pt, nb_=nb_: e.matmul(
                            po[:, hh * 65:(hh + 1) * 65], lhsT=Pt[:, j * 256 + hh * 128:j * 256 + (hh + 1) * 128], rhs=Va[:, f, h, :],
                            start=(j == 0), stop=(j == nb_ - 1)),
                            reads=[PtB, BVa[f]], writes=[poB], inc=(j == nb_ - 1))

            def emitFin(tl, hp=hp):
                par = tl % 2
                po, poB = ps[6 + par], Bps[6 + par]
                pov = po[:, 0:130].rearrange("p (h e) -> p h e", h=2)
                S.op("dve", lambda e: e.reciprocal(out=rc[:, 2 * par:2 * par + 2], in_=pov[:, :, 64]), reads=[poB], writes=[Brc[par]])
                S.op("dve", lambda e: e.tensor_tensor(
                    out=onb[par][:].rearrange("p (h d) -> p h d", h=2), in0=pov[:, :, 0:64],
                    in1=rc[:, 2 * par:2 * par + 2].unsqueeze(2).to_broadcast([128, 2, 64]), op=ALU.mult),
                    reads=[poB, Brc[par]], writes=[Bonb[par]])
                ptb = po[:].bitcast(BF16)[:, 512:640]
                S.op("pe", lambda e: e.transpose(out=ptb, in_=onb[par][:], identity=ident), reads=[Bonb[par], Bcst], writes=[poB])
                S.op("dve", lambda e: e.tensor_copy(out=oT[:, hp, tl * 128:(tl + 1) * 128], in_=ptb), reads=[poB], writes=[BoT[hp][tl]])

            for i in range(16 + 2):
                if i < 16:
                    emitS(i)
                if 1 <= i <= 16:
                    emitPV(i - 1)
                if i >= 2:
                    emitFin(i - 2)

        if stop <= 4:
            if dbg:
                dump(oT[:, 0, :], BoT[0], 2048)
                dump(oT[:, 3, :], BoT[3], 2048)
            finish()
            return nc

        Wf2 = reg(E_OFF, 8192).rearrange("p (a g c) -> p a g c", a=2, g=4)
        BWf2 = Buf("Wf2")
        S.claim([BWf2], [Bnab])
        for a, M in enumerate([CcM, ScM]):
            for g in range(4):
                for half in range(2):
                    p, pB = next_ps()
                    S.op("pe", lambda e, p=p, M=M, g=g, half=half: e.matmul(p[:], lhsT=M, rhs=wf[:, g, half * 512:(half + 1) * 512], start=True, stop=True),
                         reads=[Bcst, Bwf], writes=[pB])
                    S.op("dve", lambda e, p=p, a=a, g=g, half=half: e.tensor_copy(out=Wf2[:, a, g, half * 512:(half + 1) * 512], in_=p[:]),
                         reads=[pB], writes=[BWf2])
        S.claim([b for r in BomT for b in r], [Bwf])
        f1s = [reg(G_OFF, 4096), reg(C_OFF + 16384, 4096)]
        Bf1 = [Buf("f1s0"), Buf("f1s1")]
        S.claim([Bf1[1]], BVa + BqT + BkT)

        def load_f1(dc):
            s_ = (dc + 1) % 2
            base = f1s[s_]
            gw = base[:, 0:3072].rearrange("p (b k c) -> p b k c", b=3, k=8)
            wo = base[:, 3072:4096].rearrange("p (b k c) -> p b k c", b=2, k=4)
            for br in range(3):
                S.dma("pool", gw[:, br], wview(w_in)[:, :, 2560 + br * 1024 + dc * 128:2560 + br * 1024 + (dc + 1) * 128],
                      writes=[Bf1[s_]] if br == 0 else [], reads=[], key=Bf1[s_])
            S.dma("pool", wo[:, 0], w_na_o.rearrange("(k p) c -> p k c", p=128)[:, :, dc * 128:(dc + 1) * 128], key=Bf1[s_])
            tokw = S.dma("pool", wo[:, 1], w_mem_o.rearrange("(k p) c -> p k c", p=128)[:, :, dc * 128:(dc + 1) * 128], key=Bf1[s_])
            Bf1[s_].lw = tokw

        load_f1(0)

        mqs = [[reg(C_OFF + (hs * 4 + tb) * 512, 512) for tb in range(4)] for hs in range(2)]
        Bmqs = [[Buf("mq%d_%d" % (hs, tb)) for tb in range(4)] for hs in range(2)]
        S.claim([b for r in Bmqs for b in r], BVa + BqT + BkT)

        def emitDPN(h):
            wmq, BwmqB = wsl[h % 4], Bwsl[h % 4]
            S.dma("pool", wmq, wview(w_in)[:, :, 2048 + h * 128:2048 + (h + 1) * 128], writes=[BwmqB])
            items = []
            for tb in range(4):
                items.append(dict(w=wmq, wB=BwmqB, rhs_fn=(lambda k, tb=tb: hTf[:, k, 256 + tb * 512:256 + (tb + 1) * 512]),
                                  rhsBs=BhTf[2 + 4 * tb:6 + 4 * tb], mat=ones, inv_d=1.0 / 128, gain=vec2[:, 1:2], gainB=Bvec2,
                                  out=mqs[h % 2][tb], outB=Bmqs[h % 2][tb]))
            proj_norm_batch(items)

        def emitDattn(h, tb):
            mq, mqB = mqs[h % 2][tb], Bmqs[h % 2][tb]
            pts = []
            for c in range(2):
                pS, pSB = next_ps()
                S.op("pe", lambda e, pS=pS, c=c: e.matmul(pS[:], lhsT=kmT[:, h, c * 128:(c + 1) * 128], rhs=mq, start=True, stop=True),
                     reads=[BkmT, mqB], writes=[pSB])
                pt_, ptB_ = next_tb()
                S.op("act", lambda e, pS=pS, pt_=pt_: e.activation(out=pt_[:], in_=pS[:], func=AF.Exp), reads=[pSB], writes=[ptB_])
                pts.append((pt_, ptB_))
            po, poB = next_ps()
            pq, pqB = next_ps()
            for c in range(2):
                S.op("pe", lambda e, c=c: e.matmul(po[:], lhsT=vm[:, c, h * 128:(h + 1) * 128], rhs=pts[c][0][:], start=(c == 0), stop=(c == 1)),
                     reads=[Bvm, pts[c][1]], writes=[poB], inc=(c == 1))
            for c in range(2):
                S.op("pe", lambda e, c=c: e.matmul(pq[:], lhsT=ones, rhs=pts[c][0][:], start=(c == 0), stop=(c == 1)),
                     reads=[Bcst, pts[c][1]], writes=[pqB], inc=(c == 1))
            t, tB = next_tf()
            if tb % 2 == 0:
                S.op("act", lambda e: e.activation(out=t[:], in_=pq[:], func=AF.Ln), reads=[pqB], writes=[tB])
                S.op("act", lambda e: e.activation(out=t[:], in_=t[:], func=AF.Exp, scale=-1.0), reads=[tB], writes=[tB])
            else:
                S.op("dve", lambda e: e.reciprocal(out=t[:], in_=pq[:]), reads=[pqB], writes=[tB])
            S.op("dve", lambda e: e.tensor_tensor(out=omT[:, h, tb * 512:(tb + 1) * 512], in0=po[:], in1=t[:], op=ALU.mult),
                 reads=[poB, tB], writes=[BomT[h][tb]])

        emitDPN(0)
        for h in range(4):
            if h + 1 < 4:
                emitDPN(h + 1)
            for tb in range(4):
                emitDattn(h, tb)

        if stop <= 5:
            if dbg:
                dump(omT[:, 0, :], BomT[0], 2048)
                dump(omT[:, 3, :], BomT[3], 2048)
            finish()
            return nc

        mT = reg(C_OFF, 16384).rearrange("p (k t) -> p k t", k=8)
        BmT = [[Buf("mT%d_%d" % (k, t)) for t in range(4)] for k in range(8)]
        ncbufs = BVa + BqT + BkT + [b for r in Bmqs for b in r]
        S.claim([b for r in BmT for b in r], ncbufs)
        S.claim([Bf1[0]], [Bwv] + Bwsl)
        for dc in range(8):
            s = (dc + 1) % 2
            base = f1s[s]
            gw = base[:, 0:3072].rearrange("p (b k c) -> p b k c", b=3, k=8)
            wo = base[:, 3072:4096].rearrange("p (b k c) -> p b k c", b=2, k=4)
            if dc + 1 < 8:
                load_f1(dc + 1)
            for tb in range(4):
                tsl = slice(tb * 512, (tb + 1) * 512)
                acc = None
                for br in range(3):
                    pg, pgB = next_ps()
                    for k in range(8):
                        S.op("pe", lambda e, pg=pg, gw=gw, br=br, k=k, tb=tb: e.matmul(
                            pg[:], lhsT=gw[:, br, k, :], rhs=hTf[:, k, 256 + tb * 512:256 + (tb + 1) * 512], start=(k == 0), stop=(k == 7)),
                            reads=[Bf1[s]] + BhTf[2 + 4 * tb:6 + 4 * tb], writes=[pgB], inc=(k == 7))
                    py, pyB = next_ps()
                    if br == 0:
                        for k in range(4):
                            S.op("pe", lambda e, py=py, wo=wo, k=k, tsl=tsl: e.matmul(py[:], lhsT=wo[:, 0, k, :], rhs=oT[:, k, tsl], start=(k == 0), stop=(k == 3)),
                                 reads=[Bf1[s]] + BoT[k][4 * tb:4 * tb + 4], writes=[pyB], inc=(k == 3))
                    elif br == 1:
                        j = 0
                        for a in range(2):
                            for g in range(4):
                                S.op("pe", lambda e, py=py, a=a, g=g, dc=dc, tsl=tsl, j=j: e.matmul(
                                    py[:], lhsT=Wf2[:, a, g, dc * 128:(dc + 1) * 128], rhs=Zt[:, a, g, tsl], start=(j == 0), stop=(j == 7)),
                                    reads=[BWf2] + BZ[a][g], writes=[pyB], inc=(j == 7))
                                j += 1
                    else:
                        for k in range(4):
                            S.op("pe", lambda e, py=py, wo=wo, k=k, tsl=tsl: e.matmul(py[:], lhsT=wo[:, 1, k, :], rhs=omT[:, k, tsl], start=(k == 0), stop=(k == 3)),
                                 reads=[Bf1[s], BomT[k][tb]], writes=[pyB], inc=(k == 3))
                    sg, sgB = next_tf()
                    bcol = V_BG + br * 8 + dc
                    S.op("act", lambda e, sg=sg, pg=pg, bcol=bcol: e.activation(out=sg[:], in_=pg[:], func=AF.Sigmoid, bias=vec[:, bcol:bcol + 1]),
                         reads=[pgB, Bvec], writes=[sgB])
                    if br == 0:
                        S.op("dve", lambda e, sg=sg, py=py: e.tensor_tensor(out=sg[:], in0=sg[:], in1=py[:], op=ALU.mult), reads=[sgB, pyB], writes=[sgB])
                        acc, accB = sg, sgB
                    elif br == 1:
                        S.op("dve", lambda e, sg=sg, py=py: e.tensor_tensor(out=sg[:], in0=sg[:], in1=py[:], op=ALU.mult), reads=[sgB, pyB], writes=[sgB])
                        S.op("dve", lambda e, sg=sg, acc=acc: e.tensor_tensor(out=acc[:], in0=acc[:], in1=sg[:], op=ALU.add), reads=[sgB, accB], writes=[accB])
                    else:
                        S.op("dve", lambda e, sg=sg, py=py: e.tensor_tensor(out=sg[:], in0=sg[:], in1=py[:], op=ALU.mult), reads=[sgB, pyB], writes=[sgB])
                        S.op("dve", lambda e, sg=sg, acc=acc, dc=dc, tsl=tsl: e.tensor_tensor(out=mT[:, dc, tsl], in0=acc[:], in1=sg[:], op=ALU.add),
                             reads=[sgB, accB], writes=[BmT[dc][tb]])

        if stop <= 6:
            if dbg:
                dump(mT[:, 0, :], BmT[0], 2048)
                dump(mT[:, 7, :], BmT[7], 2048)
            finish()
            return nc

        wout = reg(C_OFF + 16384, 8192).rearrange("p (k c) -> p k c", k=8)
        Bwout = Buf("wout")
        S.claim([Bwout], [Bf1[1], Bf1[0]])
        S.dma("pool", wout, wview(w_out), writes=[Bwout])
        x1 = reg(BZ_OFF, BZ_SZ + D_SZ).bitcast(F32).rearrange("p (t c) -> p t c", t=16)
        Bx1 = [[Buf("x1_%d_%d" % (t, hf)) for hf in range(2)] for t in range(16)]
        oldz = [b for r in BZ for gg in r for b in gg] + [b for r in BoT for b in r] + [b for r in BomT for b in r]
        S.claim([b for r in Bx1 for b in r], oldz)
        h2T = reg(A_OFF, 16384).rearrange("p (k t) -> p k t", k=8)
        Bh2 = [Buf("h2T%d" % t) for t in range(16)]
        S.claim(Bh2, BhTf)
        S.dma("sp", gbc[:], gbc_d[1], writes=[Bgbc])
        pend2 = {}
        NXF = 4
        xtf = [reg(E_OFF + i * 2048, 2048).bitcast(F32) for i in range(NXF)]
        Bxtf = [Buf("xtf%d" % i) for i in range(NXF)]
        S.claim(Bxtf, [BWf2])
        ctr["xtf"] = 0

        def emitX1(t):
            xi = nxt("xtf", NXF)
            S.dma("sp", xtf[xi], xr[t * 128:(t + 1) * 128, :], writes=[Bxtf[xi]])
            for hf in range(2):
                p, pB = next_ps()
                for k in range(8):
                    S.op("pe", lambda e, p=p, k=k, hf=hf: e.matmul(p[:], lhsT=mT[:, k, t * 128:(t + 1) * 128], rhs=wout[:, k, hf * 512:(hf + 1) * 512],
                                                                start=(k == 0), stop=(k == 7)),
                         reads=[BmT[k][t // 4], Bwout], writes=[pB], inc=(k == 7))
                S.op("dve", lambda e, p=p, hf=hf, xi=xi: e.tensor_tensor(out=x1[:, t, hf * 512:(hf + 1) * 512], in0=p[:], in1=xtf[xi][:, hf * 512:(hf + 1) * 512], op=ALU.add),
                     reads=[pB, Bxtf[xi]], writes=[Bx1[t][hf]])
            pend2[t] = tok_norm_a(x1[:, t, :], Bx1[t])

        def emitT2(t):
            tok_norm_b(pend2.pop(t), V_G2, h2T[:, :, t * 128:(t + 1) * 128], Bh2[t])

        emitX1(0)
        for t in range(16):
            if t + 1 < 16:
                emitX1(t + 1)
            emitT2(t)

        if stop <= 7:
            if dbg:
                dump(x1[:, 0, :], Bx1[0], 1024)
                dump(h2T[:, 0, :], Bh2, 2048)
            finish()
            return nc

        w1s = [reg(C_OFF + s * 8192, 4096).rearrange("p (k c) -> p k c", k=8) for s in range(2)]
        w2s = [reg(C_OFF + s * 8192 + 4096, 4096).rearrange("p (k c) -> p k c", k=4) for s in range(2)]
        Bw1 = [Buf("w1_%d" % s) for s in range(2)]
        Bw2 = [Buf("w2_%d" % s) for s in range(2)]
        S.claim(Bw1 + Bw2, [b for r in BmT for b in r])
        aTs = [reg(C_OFF + 16384, 8192).rearrange("p (k t) -> p k t", k=4), reg(E_OFF, 8192).rearrange("p (k t) -> p k t", k=4)]
        BaT = [[[Buf("aT%d_%d_%d" % (s, k, t)) for t in range(4)] for k in range(4)] for s in range(2)]
        S.claim([b for r in BaT[0] for b in r], [Bwout])
        S.claim([b for r in BaT[1] for b in r], [BWf2] + Bxtf)
        NG = 8
        for grp in range(NG):
            s = grp % 2
            S.dma("pool", w1s[s], wview(w_ff1)[:, :, grp * 512:(grp + 1) * 512], writes=[Bw1[s]])
            S.dma("pool", w2s[s], w_ff2[grp * 512:(grp + 1) * 512, :].rearrange("(k p) c -> p k c", p=128), writes=[Bw2[s]])
            aT = aTs[s]
            for fc in range(4):
                for tb in range(4):
                    p, pB = next_ps()
                    for k in range(8):
                        S.op("pe", lambda e, p=p, k=k, fc=fc, tb=tb, s=s: e.matmul(p[:], lhsT=w1s[s][:, k, fc * 128:(fc + 1) * 128],
                                                                                 rhs=h2T[:, k, tb * 512:(tb + 1) * 512], start=(k == 0), stop=(k == 7)),
                             reads=[Bw1[s]] + Bh2[4 * tb:4 * tb + 4], writes=[pB], inc=(k == 7))
                    t, tB = next_tf()
                    S.op("act", lambda e, t=t, p=p: e.activation(out=t[:], in_=p[:], func=AF.Relu), reads=[pB], writes=[tB])
                    S.op("act", lambda e, t=t, aT=aT, fc=fc, tb=tb: e.activation(out=aT[:, fc, tb * 512:(tb + 1) * 512], in_=t[:], func=AF.Square),
                         reads=[tB], writes=[BaT[s][fc][tb]])
            for t in range(16):
                for hf in range(2):
                    p, pB = next_ps()
                    for fc in range(4):
                        S.op("pe", lambda e, p=p, fc=fc, t=t, hf=hf, s=s, aT=aT: e.matmul(p[:], lhsT=aT[:, fc, t * 128:(t + 1) * 128],
                                                                                      rhs=w2s[s][:, fc, hf * 512:(hf + 1) * 512], start=(fc == 0), stop=(fc == 3)),
                             reads=[BaT[s][fc][t // 4], Bw2[s]], writes=[pB], inc=(fc == 3))
                    S.op("dve", lambda e, p=p, t=t, hf=hf: e.tensor_tensor(out=x1[:, t, hf * 512:(hf + 1) * 512], in0=p[:], in1=x1[:, t, hf * 512:(hf + 1) * 512], op=ALU.add),
                         reads=[pB, Bx1[t][hf]], writes=[Bx1[t][hf]])
                if grp == NG - 1:
                    S.dma("sp", y[t * 128:(t + 1) * 128, :], x1[:, t, :], reads=Bx1[t], writes=[By[t]])
        finish()
    return nc


def _na_table(rpb, hf):
    rpb = np.asarray(rpb, np.float32)
    tab = np.full((8, NA_NCH, 128, 128), NEG, np.float32)
    kr2 = np.arange(128) // 64
    kc = np.arange(128) % 64
    qr2 = np.arange(128) // 64
    qc = np.arange(128) % 64
    cs = np.clip(qc - 8, 0, 48)
    tls = [0, 1, 2, 14, 15]
    for vi, tl in enumerate(tls):
        b0, nb_, voff = na_band(tl)
        t = 16 * hf + tl
        qrow = 2 * t + qr2
        rs = np.clip(qrow - 4, 0, 56)
        for j in range(nb_):
            f = b0 + j
            lt = (f - 2) % 32
            real = (2 <= f < 18) or (f < 2 and hf == 1) or (f >= 18 and hf == 0)
            if not real:
                continue
            gt = (lt + 16 * hf) % 32
            krow = 2 * gt + kr2
            dr = krow[:, None] - qrow[None, :]
            rowok = (krow[:, None] >= rs[None, :]) & (krow[:, None] < rs[None, :] + 8)
            colok = (kc[:, None] >= cs[None, :]) & (kc[:, None] < cs[None, :] + 16)
            ok = rowok & colok
            dri = np.clip(dr + 7, 0, 14)
            dci = np.clip(kc[:, None] - qc[None, :], -15, 15) + 15
            g = rpb[:, dri, dci]
            tab[:, voff + j] = np.where(ok[None], g, NEG)
    tab = tab.reshape(4, 2, NA_NCH, 128, 128).transpose(0, 3, 2, 1, 4).reshape(4, 128, 2 * NA_NCH, 128)
    return np.ascontiguousarray(tab)


def _fft_tables(hf):
    m = np.arange(1024, dtype=np.int64)
    out = np.zeros((4, 2, 1024, 512), np.float64)
    w = np.arange(512, dtype=np.int64)
    for v in range(4):
        sp = 2048 * hf + 4 * w + v
        ph = (m[:, None] * sp[None, :]) % 4096
        ang = 2.0 * np.pi * ph / 4096.0
        sign = -1.0 if (hf == 1 and v % 2 == 1) else 1.0
        out[v, 0] = sign * np.cos(ang)
        out[v, 1] = -sign * np.sin(ang)
    out = out.reshape(4, 2, 8, 128, 512).transpose(0, 3, 1, 2, 4)
    return np.ascontiguousarray(out.astype(np.float32)).astype(ml_dtypes.bfloat16)


def _consts():
    c = np.zeros((128, 5, 128), np.float64)
    c[:, 0] = np.eye(128)
    blk = np.zeros((128, 128))
    blk[:64, :64] = 1.0
    blk[64:, 64:] = 1.0
    c[:, 1] = blk
    c[:, 2] = 1.0
    i = np.arange(128)
    ang = 2.0 * np.pi * ((i[:, None] * i[None, :]) % 128) / 128.0
    nrm = 1.0 / np.sqrt(4096.0 * 128.0)
    c[:, 3] = np.cos(ang) * nrm
    c[:, 4] = np.sin(ang) * nrm
    return c.astype(np.float32).astype(ml_dtypes.bfloat16)


def _vecs(norm1_g, norm2_g, mem_norm_g, b_gate, na_q_g, na_k_g, mem_q_g, mem_k_g):
    v = np.zeros((128, NVEC), np.float32)
    v[:, V_G1:V_G1 + 8] = np.asarray(norm1_g, np.float32).reshape(8, 128).T
    v[:, V_G2:V_G2 + 8] = np.asarray(norm2_g, np.float32).reshape(8, 128).T
    v[:, V_GM:V_GM + 8] = np.asarray(mem_norm_g, np.float32).reshape(8, 128).T
    v[:, V_BG:V_BG + 24] = np.asarray(b_gate, np.float32).reshape(24, 128).T
    v[:, V_QG] = np.tile(np.asarray(na_q_g, np.float32), 2)
    v[:, V_KG] = np.tile(np.asarray(na_k_g, np.float32), 2)
    v[:, V_MQG] = np.asarray(mem_q_g, np.float32)
    v[:, V_MKG] = np.asarray(mem_k_g, np.float32)
    return v


def make_in_maps(x, mem, norm1_g, w_in, b_gate, na_q_g, na_k_g, na_rpb, w_na_o, w_f,
                 mem_norm_g, w_mem_kv, mem_q_g, mem_k_g, w_mem_o, w_out, norm2_g, w_ff1, w_ff2):
    f = lambda a: np.ascontiguousarray(np.asarray(a, np.float32))
    x = f(x)
    mem = f(mem)
    shared = {
        "w_in": f(w_in), "w_na_o": f(w_na_o), "w_f": f(w_f), "w_mem_o": f(w_mem_o), "w_mem_kv": f(w_mem_kv),
        "w_out": f(w_out), "w_ff1": f(w_ff1), "w_ff2": f(w_ff2),
        "vecs": _vecs(norm1_g, norm2_g, mem_norm_g, b_gate, na_q_g, na_k_g, mem_q_g, mem_k_g),
        "consts": _consts(),
        "gbc": np.ascontiguousarray(np.stack([np.broadcast_to(np.asarray(g, np.float32)[None, :], (128, D))
                                              for g in (norm1_g, norm2_g, mem_norm_g)])),
    }
    nat = [_na_table(na_rpb, hf) for hf in range(2)]
    fft = [_fft_tables(hf) for hf in range(2)]
    maps = []
    for c in range(8):
        b, hf = c // 2, c % 2
        m = dict(shared)
        m["xr"] = np.ascontiguousarray(np.roll(x[b], -NOWN * hf, axis=0))
        m["mem"] = mem[b]
        m["nabias"] = nat[hf]
        m["fftab"] = fft[hf]
        maps.append(m)
    return maps


_NC_CACHE = {}


def kernel(**inputs):
    maps = make_in_maps(**inputs)
    if "nc" not in _NC_CACHE:
        _NC_CACHE["nc"] = build()
    nc = _NC_CACHE["nc"]
    res = run_bass_kernel_spmd(nc, maps, core_ids=list(range(8)))
    out = np.zeros((4, SEQ, D), np.float32)
    for c in range(8):
        b, hf = c // 2, c % 2
        out[b, NOWN * hf:NOWN * (hf + 1)] = res.results[c]["y"]
    return out
```

```python
from contextlib import ExitStack

import numpy as np
import ml_dtypes

import concourse.bass as bass
import concourse.mybir as mybir
from concourse.bass_utils import run_bass_kernel_spmd

F32 = mybir.dt.float32
BF16 = mybir.dt.bfloat16
AF = mybir.ActivationFunctionType
ALU = mybir.AluOpType

SAME_ENG_SYNC = True
EPS = 1e-6
NEG = -1e30


class Buf:
    __slots__ = ("name", "lw", "rds", "dsem", "dcnt")

    def __init__(self, name):
        self.name = name
        self.lw = None
        self.rds = {}
        self.dsem = None
        self.dcnt = 0


class EngQ:
    def __init__(self, name, sem, is_pe=False):
        self.name = name
        self.sem = sem
        self.n = 0
        self.waited = {}
        self.ops = []
        self.is_pe = is_pe


class Sched:
    def __init__(self, nc, stack):
        self.nc = nc
        self.stack = stack
        self.q = {}
        for name in ("pe", "act", "dve", "pool", "sp"):
            sem = stack.enter_context(nc.semaphore("q_" + name))
            self.q[name] = EngQ(name, sem, is_pe=(name == "pe"))
        self.nsem = 5
        self.free_sems = []

    def newsem(self, name):
        self.nsem += 1
        return self.stack.enter_context(self.nc.semaphore(name))

    def _collect(self, q, reads, writes):
        deps = {}

        def add(t):
            if t is None:
                return
            sem, val = t
            if sem is q.sem and (q.is_pe or not SAME_ENG_SYNC):
                return
            k = id(sem)
            if q.waited.get(k, 0) >= val:
                return
            if k not in deps or deps[k][1] < val:
                deps[k] = (sem, val)

        for b in reads:
            add(b.lw)
        for b in writes:
            add(b.lw)
            for t in b.rds.values():
                add(t)
        out = list(deps.values())
        for sem, val in out:
            q.waited[id(sem)] = val
        return out

    def _update(self, tok, reads, writes):
        for b in writes:
            b.lw = tok
            b.rds = {}
        k = id(tok[0])
        for b in reads:
            if k not in b.rds or b.rds[k][1] < tok[1]:
                b.rds[k] = tok

    def op(self, qn, fn, reads=(), writes=(), inc=True):
        q = self.q[qn]
        waits = self._collect(q, reads, writes)
        tok = (q.sem, q.n + 1)
        if inc:
            q.n += 1
        else:
            assert q.is_pe
        q.ops.append((waits, fn, (q.sem, 1) if inc else None))
        self._update(tok, reads, writes)

    def dma(self, qn, out, in_, reads=(), writes=(), key=None, **kw):
        q = self.q[qn]
        waits = self._collect(q, reads, writes)
        kb = key if key is not None else (writes[0] if writes else reads[0])
        if kb.dsem is None:
            kb.dsem = self.newsem("d_" + kb.name)
        kb.dcnt += 16
        tok = (kb.dsem, kb.dcnt)

        def fn(e, out=out, in_=in_, kw=kw):
            return e.dma_start(out=out, in_=in_, **kw)

        q.ops.append((waits, fn, (kb.dsem, 16)))
        self._update(tok, reads, writes)
        return tok

    def claim(self, new_bufs, old_bufs):
        toks = {}
        for b in old_bufs:
            for t in ([b.lw] if b.lw else []) + list(b.rds.values()):
                k = id(t[0])
                if k not in toks or toks[k][1] < t[1]:
                    toks[k] = t
        for nb in new_bufs:
            for k, t in toks.items():
                if k not in nb.rds or nb.rds[k][1] < t[1]:
                    nb.rds[k] = t

    def wait_all(self, qn, bufs):
        q = self.q[qn]
        waits = self._collect(q, (), bufs)
        q.ops.append((waits, None, None))

    def emit(self):
        nc = self.nc
        qs = self.q

        def run(q, e):
            for waits, fn, inc in q.ops:
                for sem, val in waits:
                    e.wait_ge(sem, val)
                if fn is None:
                    continue
                ins = fn(e)
                if inc is not None:
                    ins.then_inc(inc[0], inc[1])

        with nc.Block() as block:
            @block.tensor
            def _(e):
                run(qs["pe"], e)

            @block.scalar
            def _(e):
                run(qs["act"], e)

            @block.vector
            def _(e):
                run(qs["dve"], e)

            @block.gpsimd
            def _(e):
                run(qs["pool"], e)

            @block.sync
            def _(e):
                run(qs["sp"], e)


D = 1024
SEQ = 4096
NOWN = 2048
NFR = 20
NA_VARIANTS = [(0, 6), (1, 5), (2, 5), (14, 5), (14, 6)]
NA_VOFF = [0, 6, 11, 16, 21]
NA_NCH = 27


def na_band(tl):
    if tl == 0:
        return 0, 6, NA_VOFF[0]
    if tl == 1:
        return 1, 5, NA_VOFF[1]
    if tl == 14:
        return 14, 5, NA_VOFF[3]
    if tl == 15:
        return 14, 6, NA_VOFF[4]
    return tl, 5, NA_VOFF[2]


V_G1, V_G2, V_GM, V_BG, V_QG, V_KG, V_MQG, V_MKG = 0, 8, 16, 24, 48, 49, 50, 51
NVEC = 64

A_OFF, A_SZ = 0, 20480
BZ_OFF, BZ_SZ = 20480, 16384
D_OFF, D_SZ = 36864, 16384
C_OFF, C_SZ = 53248, 20480
G_OFF, G_SZ = 73728, 4096
E_OFF, E_SZ = 77824, 8192
ARENA = 86016


def build(stop=99, dbg=False):
    nc = bass.Bass("TRN2", target_bir_lowering=False)

    def dram(n, s, d, kind="ExternalInput"):
        return nc.dram_tensor(n, s, d, kind=kind).ap()

    xr = dram("xr", [SEQ, D], F32)
    mem = dram("mem", [256, D], F32)
    w_in = dram("w_in", [D, 5632], F32)
    w_na_o = dram("w_na_o", [512, D], F32)
    w_f = dram("w_f", [512, D], F32)
    w_mem_o = dram("w_mem_o", [512, D], F32)
    w_mem_kv = dram("w_mem_kv", [D, D], F32)
    w_out = dram("w_out", [D, D], F32)
    w_ff1 = dram("w_ff1", [D, 4096], F32)
    w_ff2 = dram("w_ff2", [4096, D], F32)
    vecs_d = dram("vecs", [128, NVEC], F32)
    nab_d = dram("nabias", [4, 128, 2 * NA_NCH, 128], F32)
    fft_d = dram("fftab", [4, 128, 2, 8, 512], BF16)
    cst_d = dram("consts", [128, 5, 128], BF16)
    gbc_d = dram("gbc", [3, 128, D], F32)
    y = dram("y", [NOWN, D], F32, kind="ExternalOutput")
    if dbg:
        dbg_d = dram("dbg", [128, 8192], F32, kind="ExternalOutput")

    def wview(w):
        return w.rearrange("(k p) c -> p k c", p=128)

    st = ExitStack()
    with st:
        S = Sched(nc, st)

        def sb(n, s, d):
            return st.enter_context(nc.sbuf_tensor(n, s, d))

        ar = sb("arena", [128, ARENA], BF16)

        def reg(off, n):
            return ar[:, off:off + n]

        cst = sb("cst", [128, 5, 128], BF16)
        vec = sb("vec", [128, NVEC], F32)
        vec2 = sb("vec2", [128, 4], F32)
        stat = sb("stat", [128, 64], F32)
        rstd = sb("rstd", [128, 64], F32)
        gbc = sb("gbcsb", [128, D], F32)
        Bgbc = Buf("gbc")
        NXA = 6
        xt = [reg(D_OFF + i * 2048, 2048).bitcast(F32) for i in range(NXA)]
        xs = [sb("xs%d" % i, [128, D], BF16) for i in range(3)]
        tmpf = [sb("tmpf%d" % i, [128, 512], F32) for i in range(5)]
        tmpb = [sb("tmpb%d" % i, [128, 512], BF16) for i in range(4)]
        PT = [sb("PT%d" % i, [128, 1536], BF16) for i in range(2)]
        kmT = sb("kmT", [128, 4, 256], BF16)
        vm = sb("vm", [128, 2, 512], BF16)
        onb = [sb("onb%d" % i, [128, 128], BF16) for i in range(2)]
        rc = sb("rc", [128, 4], F32)
        ps = [st.enter_context(nc.psum_tensor("ps%d" % i, [128, 512], F32)) for i in range(8)]

        Bcst, Bvec, Bvec2 = Buf("cst"), Buf("vec"), Buf("vec2")
        Bxt = [Buf("xt%d" % i) for i in range(NXA)]
        Bxs = [Buf("xs%d" % i) for i in range(3)]
        Btf = [Buf("tmpf%d" % i) for i in range(5)]
        Btb = [Buf("tmpb%d" % i) for i in range(4)]
        BPT = [Buf("PT%d" % i) for i in range(2)]
        Bps = [Buf("ps%d" % i) for i in range(8)]
        BkmT, Bvm = Buf("kmT"), Buf("vm")
        Bonb = [Buf("onb%d" % i) for i in range(2)]
        Brc = [Buf("rc%d" % i) for i in range(2)]
        Bstat = [Buf("stat%d" % i) for i in range(64)]

        ident = cst[:, 0, :]
        blk64 = cst[:, 1, :]
        ones = cst[:, 2, :]
        CcM = cst[:, 3, :]
        ScM = cst[:, 4, :]

        ctr = {"ps": 0, "tf": 0, "tb": 0, "xt": 0, "xs": 0, "st": 0, "ev": 0}

        def nxt(key, n):
            i = ctr[key] % n
            ctr[key] += 1
            return i

        def next_ps():
            i = nxt("ps", 8)
            return ps[i], Bps[i]

        def next_tf():
            i = nxt("tf", 5)
            return tmpf[i], Btf[i]

        def next_tb():
            i = nxt("tb", 4)
            return tmpb[i], Btb[i]

        def next_stat():
            i = nxt("st", 64)
            return i, Bstat[i]

        S.dma("sp", cst[:], cst_d, writes=[Bcst])
        S.dma("sp", vec[:], vecs_d, writes=[Bvec])
        S.op("dve", lambda e: e.tensor_scalar(out=vec2[:, 0:1], in0=vec[:, V_QG:V_QG + 1], scalar1=0.125, scalar2=None, op0=ALU.mult),
             reads=[Bvec], writes=[Bvec2])
        S.op("dve", lambda e: e.tensor_scalar(out=vec2[:, 1:2], in0=vec[:, V_MQG:V_MQG + 1], scalar1=float(128 ** -0.5), scalar2=None, op0=ALU.mult),
             reads=[Bvec], writes=[Bvec2])

        junk = PT[0][:, 0:D]

        def tok_norm_a(src_ap, srcBs):
            si, sB = next_stat()
            xi = nxt("xs", 3)
            S.op("act", lambda e: e.activation(out=junk, in_=src_ap, func=AF.Square, accum_out=stat[:, si:si + 1]),
                 reads=srcBs, writes=[sB, BPT[0]])
            S.op("act", lambda e: e.activation(out=stat[:, si:si + 1], in_=stat[:, si:si + 1], func=AF.Sqrt, scale=1.0 / D, bias=EPS),
                 reads=[sB], writes=[sB])
            S.op("dve", lambda e: e.reciprocal(out=rstd[:, si:si + 1], in_=stat[:, si:si + 1]), reads=[sB], writes=[sB])
            S.op("dve", lambda e: e.scalar_tensor_tensor(out=xs[xi][:], in0=src_ap, scalar=rstd[:, si:si + 1], in1=gbc[:],
                                                          op0=ALU.mult, op1=ALU.mult),
                 reads=list(srcBs) + [sB, Bgbc], writes=[Bxs[xi]])
            return xi

        def tok_norm_b(xi, gcol, dst_fn, dstB, bank=None):
            p, pB = next_ps() if bank is None else (ps[bank], Bps[bank])
            pb = p[:].bitcast(BF16)
            for k in range(8):
                S.op("pe", lambda e, k=k: e.transpose(out=pb[:, k * 128:(k + 1) * 128], in_=xs[xi][:, k * 128:(k + 1) * 128], identity=ident),
                     reads=[Bxs[xi], Bcst], writes=[pB], inc=(k == 7))
            ev = nxt("ev", 2)
            if ev == 0:
                S.op("act", lambda e: e.activation(out=dst_fn, in_=pb.rearrange("p (k t) -> p k t", k=8), func=AF.Copy),
                     reads=[pB], writes=[dstB])
            else:
                S.op("dve", lambda e: e.tensor_copy(out=dst_fn, in_=pb.rearrange("p (k t) -> p k t", k=8)), reads=[pB], writes=[dstB])

        def tok_norm_transpose(src_ap, srcB, gcol, dst_fn, dstB):
            xi = tok_norm_a(src_ap, [srcB])
            tok_norm_b(xi, gcol, dst_fn, dstB)

        def fm_norm(p, pB, ncols, mat, inv_d, gain_ap, gainB, out_ap, outB, split=None, split3=False):
            sq, sqB = next_tb()
            S.op("act", lambda e: e.activation(out=sq[:, :ncols], in_=p[:, :ncols], func=AF.Square), reads=[pB], writes=[sqB])
            p2, p2B = next_ps()
            S.op("pe", lambda e: e.matmul(p2[:, :ncols], lhsT=mat, rhs=sq[:, :ncols], start=True, stop=True),
                 reads=[sqB, Bcst], writes=[p2B])
            t, tB = next_tf()
            S.op("act", lambda e: e.activation(out=t[:, :ncols], in_=p2[:, :ncols], func=AF.Ln, scale=inv_d, bias=EPS),
                 reads=[p2B], writes=[tB])
            S.op("act", lambda e: e.activation(out=t[:, :ncols], in_=t[:, :ncols], func=AF.Exp, scale=-0.5), reads=[tB], writes=[tB])
            if split is None:
                S.op("dve", lambda e: e.scalar_tensor_tensor(out=out_ap, in0=p[:, :ncols], scalar=gain_ap, in1=t[:, :ncols],
                                                              op0=ALU.mult, op1=ALU.mult),
                     reads=[pB, tB, gainB], writes=[outB])
            else:
                for (pr, oap) in split:
                    i0 = p[pr, :ncols]
                    i1 = t[pr, :ncols]
                    if split3:
                        i0 = i0.rearrange("p (t q) -> p t q", q=128)
                        i1 = i1.rearrange("p (t q) -> p t q", q=128)
                    S.op("dve", lambda e, pr=pr, oap=oap, i0=i0, i1=i1: e.scalar_tensor_tensor(out=oap, in0=i0, scalar=gain_ap[pr], in1=i1,
                                                                                             op0=ALU.mult, op1=ALU.mult),
                         reads=[pB, tB, gainB], writes=[outB])

        def proj_fm(wt, wB, ncolchunk, rhs_fn, rhsBs, p, pB, ncols):
            for k in range(8):
                S.op("pe", lambda e, k=k: e.matmul(p[:, :ncols], lhsT=wt[:, k, ncolchunk], rhs=rhs_fn(k), start=(k == 0), stop=(k == 7)),
                     reads=[wB] + list(rhsBs), writes=[pB], inc=(k == 7))

        def proj_norm_batch(items):
            n = len(items)
            P = []
            for it in items:
                p, pB = next_ps()
                proj_fm(it["w"], it["wB"], slice(0, 128), it["rhs_fn"], it["rhsBs"], p, pB, 512)
                P.append((p, pB))
            SQ = []
            for it, (p, pB) in zip(items, P):
                sq, sqB = next_tb()
                S.op("act", lambda e, sq=sq, p=p: e.activation(out=sq[:], in_=p[:], func=AF.Square), reads=[pB], writes=[sqB])
                SQ.append((sq, sqB))
            P2 = []
            for it, (sq, sqB) in zip(items, SQ):
                p2, p2B = next_ps()
                S.op("pe", lambda e, p2=p2, sq=sq, it=it: e.matmul(p2[:], lhsT=it["mat"], rhs=sq[:], start=True, stop=True),
                     reads=[sqB, Bcst], writes=[p2B])
                P2.append((p2, p2B))
            T = []
            for it, (p2, p2B) in zip(items, P2):
                t, tB = next_tf()
                S.op("act", lambda e, t=t, p2=p2, it=it: e.activation(out=t[:], in_=p2[:], func=AF.Ln, scale=it["inv_d"], bias=EPS),
                     reads=[p2B], writes=[tB])
                S.op("act", lambda e, t=t: e.activation(out=t[:], in_=t[:], func=AF.Exp, scale=-0.5), reads=[tB], writes=[tB])
                T.append((t, tB))
            for it, (p, pB), (t, tB) in zip(items, P, T):
                if it.get("split") is None:
                    S.op("dve", lambda e, it=it, p=p, t=t: e.scalar_tensor_tensor(out=it["out"], in0=p[:], scalar=it["gain"], in1=t[:],
                                                                                  op0=ALU.mult, op1=ALU.mult),
                         reads=[pB, tB, it["gainB"]], writes=[it["outB"]])
                else:
                    for (pr, oap) in it["split"]:
                        i0 = p[pr, :].rearrange("p (t q) -> p t q", q=128)
                        i1 = t[pr, :].rearrange("p (t q) -> p t q", q=128)
                        S.op("dve", lambda e, it=it, pr=pr, oap=oap, i0=i0, i1=i1: e.scalar_tensor_tensor(
                            out=oap, in0=i0, scalar=it["gain"][pr], in1=i1, op0=ALU.mult, op1=ALU.mult),
                            reads=[pB, tB, it["gainB"]], writes=[it["outB"]])

        dbg_off = [0]

        def dump(ap, B, n):
            for c0 in range(0, n, 512):
                w = min(512, n - c0)
                t, tB = next_tf()
                S.op("act", lambda e, c0=c0, w=w, t=t: e.activation(out=t[:, :w], in_=ap[:, c0:c0 + w], func=AF.Copy), reads=B, writes=[tB])
                o = dbg_off[0]
                S.dma("sp", dbg_d[:, o:o + w], t[:, :w], reads=[tB], writes=[Bdbg])
                dbg_off[0] += w

        Bdbg = Buf("dbg")
        By = [Buf("y%d" % i) for i in range(16)]

        def finish():
            S.wait_all("sp", By + [Bdbg])
            S.emit()

        wkv = reg(C_OFF, 8192).rearrange("p (k c) -> p k c", k=8)
        Bwkv = Buf("wkv")
        S.dma("pool", wkv, wview(w_mem_kv), writes=[Bwkv])
        S.dma("sp", gbc[:], gbc_d[2], writes=[Bgbc])
        memT = reg(BZ_OFF + 12288, 2048).rearrange("p (k t) -> p k t", k=8)
        BmemT = [Buf("memT%d" % i) for i in range(2)]
        for mt in range(2):
            xi = nxt("xt", NXA)
            S.dma("sp", xt[xi], mem[mt * 128:(mt + 1) * 128, :], writes=[Bxt[xi]])
            tok_norm_transpose(xt[xi], Bxt[xi], V_GM, memT[:, :, mt * 128:(mt + 1) * 128], BmemT[mt])
        for h in range(4):
            p, pB = next_ps()
            proj_fm(wkv, Bwkv, slice(h * 128, (h + 1) * 128), lambda k: memT[:, k, :], BmemT, p, pB, 256)
            fm_norm(p, pB, 256, ones, 1.0 / 128, vec[:, V_MKG:V_MKG + 1], Bvec, kmT[:, h, :], BkmT)
        for c in range(2):
            p, pB = next_ps()
            for k in range(8):
                S.op("pe", lambda e, k=k, c=c, p=p: e.matmul(p[:], lhsT=memT[:, k, c * 128:(c + 1) * 128], rhs=wkv[:, k, 512:1024],
                                                          start=(k == 0), stop=(k == 7)),
                     reads=[BmemT[c], Bwkv], writes=[pB], inc=(k == 7))
            S.op("act", lambda e, c=c, p=p: e.activation(out=vm[:, c, :], in_=p[:], func=AF.Copy), reads=[pB], writes=[Bvm])

        hTf = reg(A_OFF, A_SZ).rearrange("p (k t) -> p k t", k=8)
        hTo = reg(BZ_OFF, 12288).rearrange("p (k t) -> p k t", k=8)
        BhT = [Buf("hT%d" % i) for i in range(32)]

        def hT_tile(lt):
            f = (lt + 2) % 32
            if f < NFR:
                return hTf[:, :, f * 128:(f + 1) * 128]
            return hTo[:, :, (lt - 18) * 128:(lt - 17) * 128]

        BhTf = [BhT[(f - 2) % 32] for f in range(NFR)]

        S.dma("sp", gbc[:], gbc_d[0], writes=[Bgbc])
        wu = reg(G_OFF, 4096).rearrange("p (k c) -> p k c", k=8)
        Bwu = Buf("wu")
        S.dma("pool", wu, wview(w_in)[:, :, 1536:2048], writes=[Bwu])

        var = reg(C_OFF, C_SZ).rearrange("p (v i c) -> p v i c", v=5, i=8)
        Bvar = [[Buf("var%d_%d" % (v, i)) for i in range(8)] for v in range(5)]
        S.claim([b for r in Bvar for b in r], [Bwkv])

        order = [8 * e4 + i for i in range(8) for e4 in range(4)]
        pend = {}

        def emitN(n):
            lt = order[n]
            xi = nxt("xt", NXA)
            S.dma("sp", xt[xi], xr[lt * 128:(lt + 1) * 128, :], writes=[Bxt[xi]])
            pend[n] = tok_norm_a(xt[xi], [Bxt[xi]])

        def emitT(n, bank=None):
            lt = order[n]
            tok_norm_b(pend.pop(n), V_G1, hT_tile(lt), BhT[lt], bank=bank)

        def emit_combos(i, U):
            t0, t0B = next_tf()
            t1, t1B = next_tf()
            S.op("act", lambda e: e.activation(out=t0[:], in_=U[0][0][:], func=AF.Copy), reads=[U[0][1]], writes=[t0B])
            S.op("act", lambda e: e.activation(out=t1[:], in_=U[1][0][:], func=AF.Copy), reads=[U[1][1]], writes=[t1B])
            fa, faB = next_tf()
            fb, fbB = next_tf()
            S.op("dve", lambda e: e.tensor_tensor(out=fa[:], in0=t0[:], in1=U[2][0][:], op=ALU.add), reads=[t0B, U[2][1]], writes=[faB])
            S.op("dve", lambda e: e.tensor_tensor(out=var[:, 2, i, :], in0=t0[:], in1=U[2][0][:], op=ALU.subtract),
                 reads=[t0B, U[2][1]], writes=[Bvar[2][i]])
            S.op("dve", lambda e: e.tensor_tensor(out=fb[:], in0=t1[:], in1=U[3][0][:], op=ALU.add), reads=[t1B, U[3][1]], writes=[fbB])
            S.op("dve", lambda e: e.tensor_tensor(out=var[:, 3, i, :], in0=t1[:], in1=U[3][0][:], op=ALU.subtract),
                 reads=[t1B, U[3][1]], writes=[Bvar[3][i]])
            S.op("dve", lambda e: e.tensor_tensor(out=var[:, 0, i, :], in0=fa[:], in1=fb[:], op=ALU.add), reads=[faB, fbB], writes=[Bvar[0][i]])
            S.op("dve", lambda e: e.tensor_tensor(out=var[:, 1, i, :], in0=fa[:], in1=fb[:], op=ALU.subtract), reads=[faB, fbB], writes=[Bvar[1][i]])
            S.op("dve", lambda e: e.tensor_scalar(out=var[:, 4, i, :], in0=var[:, 3, i, :], scalar1=-1.0, scalar2=None, op0=ALU.mult),
                 reads=[Bvar[3][i]], writes=[Bvar[4][i]])

        emitN(0)
        Upend = []
        pend_up = []

        def emit_uproj(n):
            i, e4 = n // 4, n % 4
            lt = order[n]
            bk = 4 * (i % 2) + e4
            p, pB = ps[bk], Bps[bk]
            hv = hT_tile(lt)
            for k in range(8):
                S.op("pe", lambda e, k=k, p=p, hv=hv: e.matmul(p[:], lhsT=hv[:, k, :], rhs=wu[:, k, :], start=(k == 0), stop=(k == 7)),
                     reads=[BhT[lt], Bwu], writes=[pB], inc=(k == 7))
            if e4 == 3:
                Upend.append((i, [(ps[4 * (i % 2) + j], Bps[4 * (i % 2) + j]) for j in range(4)]))

        for n in range(32):
            i, e4 = n // 4, n % 4
            if n + 1 < 32:
                emitN(n + 1)
            emitT(n, bank=4 * (i % 2) + e4)
            if stop >= 2:
                if pend_up:
                    emit_uproj(pend_up.pop(0))
                pend_up.append(n)
                if e4 == 2 and Upend:
                    emit_combos(*Upend.pop(0))
        while pend_up:
            emit_uproj(pend_up.pop(0))
        while Upend:
            emit_combos(*Upend.pop(0))

        if stop <= 2:
            if dbg:
                dump(hTf[:, 0, :], BhTf, 2560)
                if stop == 2:
                    for v in range(5):
                        dump(var[:, v, 0, :], [Bvar[v][0]], 512)
            finish()
            return nc

        Zt = reg(BZ_OFF, BZ_SZ).rearrange("p (r g t) -> p r g t", r=2, g=4)
        BZ = [[[Buf("Z%d_%d_%d" % (r, g, v)) for v in range(4)] for g in range(4)] for r in range(2)]
        S.claim([b for r in BZ for gg in r for b in gg], BhT[18:30] + BmemT)
        tabs = [reg(D_OFF + s * 8192, 8192).rearrange("p (a i w) -> p a i w", a=2, i=8) for s in range(2)]
        Btab = [Buf("ftab%d" % s) for s in range(2)]
        S.claim(Btab, Bxt)
        CLS = {
            0: ([(0, 0)], [(0, 1)]),
            2: ([(1, 0)], [(1, 1)]),
            1: ([(2, 0), (3, 1)], [(4, 0), (2, 1)]),
            3: ([(2, 0), (4, 1)], [(3, 0), (2, 1)]),
        }
        for ci, v in enumerate([0, 2, 1, 3]):
            s = ci % 2
            S.dma("sp", tabs[s], fft_d[v], writes=[Btab[s]])
            for g in range(4):
                for ri in range(2):
                    terms = CLS[v][ri]
                    p, pB = next_ps()
                    n = len(terms) * 8
                    j = 0
                    for (vi, ab) in terms:
                        for i in range(8):
                            S.op("pe", lambda e, vi=vi, ab=ab, i=i, g=g, p=p, j=j, n=n, s=s: e.matmul(
                                p[:], lhsT=var[:, vi, i, g * 128:(g + 1) * 128], rhs=tabs[s][:, ab, i, :], start=(j == 0), stop=(j == n - 1)),
                                reads=[Bvar[vi][i], Btab[s]], writes=[pB], inc=(j == n - 1))
                            j += 1
                    zo = Zt[:, ri, g, :].rearrange("p (w v) -> p v w", v=4)[:, v, :]
                    S.op("act", lambda e, zo=zo, p=p: e.activation(out=zo, in_=p[:], func=AF.Copy), reads=[pB], writes=[BZ[ri][g][v]])
        BZg = [[BZ[r][g] for g in range(4)] for r in range(2)]

        if stop <= 3:
            if dbg:
                dump(Zt[:, 0, 0, :], BZ[0][0], 2048)
                dump(Zt[:, 1, 1, :], BZ[1][1], 2048)
            finish()
            return nc

        Va = reg(C_OFF, 10400).rearrange("p (f h e) -> p f h e", f=NFR, h=8)
        BVa = [Buf("Va%d" % f) for f in range(NFR)]
        qzz = reg(C_OFF + 10400, 4096).rearrange("p (t h q) -> p t h q", t=16, h=2)
        kT = reg(C_OFF + 10400 + 4096, 2560)
        BqT = [Buf("qT%d" % t) for t in range(4)]
        BkT = [Buf("kT%d" % t) for t in range(5)]
        allvar = [b for r in Bvar for b in r]
        S.claim(BVa + BqT + BkT, allvar)
        wv = reg(G_OFF, 4096).rearrange("p (k c) -> p k c", k=8)
        Bwv = Buf("wv")
        S.claim([Bwv], [Bwu])
        S.dma("pool", wv, wview(w_in)[:, :, 1024:1536], writes=[Bwv])
        nab = reg(E_OFF, 2 * NA_NCH * 128).rearrange("p (c h q) -> p c h q", h=2, c=NA_NCH)
        Bnab = Buf("nab")
        oT = reg(D_OFF, 8192).rearrange("p (k t) -> p k t", k=4)
        BoT = [[Buf("oT%d_%d" % (k, t)) for t in range(16)] for k in range(4)]
        omT = reg(D_OFF + 8192, 8192).rearrange("p (k t) -> p k t", k=4)
        BomT = [[Buf("omT%d_%d" % (k, t)) for t in range(4)] for k in range(4)]
        S.claim([b for r in BoT for b in r], [Btab[0]])
        S.claim([b for r in BomT for b in r], [Btab[1]])
        wf = reg(D_OFF + 8192, 4096).rearrange("p (g c) -> p g c", g=4)
        Bwf = Buf("wf")
        S.claim([Bwf], [Btab[1]])
        S.dma("pool", wf, w_f.rearrange("(g p) c -> p g c", p=128), writes=[Bwf])

        S.op("dve", lambda e: e.memset(Va[:, :, :, 64:65], 1.0), reads=[], writes=BVa)
        S.op("dve", lambda e: e.memset(reg(C_OFF + 10400, 4096), 0.0), reads=[], writes=BqT)
        for f in range(NFR):
            p, pB = next_ps()
            for k in range(8):
                S.op("pe", lambda e, k=k, f=f, p=p: e.matmul(p[:], lhsT=hTf[:, k, f * 128:(f + 1) * 128], rhs=wv[:, k, :], start=(k == 0), stop=(k == 7)),
                     reads=[BhTf[f], Bwv], writes=[pB], inc=(k == 7))
            S.op("act", lambda e, f=f, p=p: e.activation(out=Va[:, f, :, 0:64], in_=p[:].rearrange("p (h d) -> p h d", h=8), func=AF.Copy),
                 reads=[pB], writes=[BVa[f]])

        wsl = [reg(G_OFF + i * 1024, 1024).rearrange("p (k c) -> p k c", k=8) for i in range(4)]
        Bwsl = [Buf("wsl%d" % i) for i in range(4)]
        S.claim(Bwsl, [Bwv])

        H0, H1 = slice(0, 64), slice(64, 128)
        for hp in range(4):
            s = hp % 2
            wq, wk = wsl[2 * s], wsl[2 * s + 1]
            BwqB, BwkB = Bwsl[2 * s], Bwsl[2 * s + 1]
            S.dma("pool", wq, wview(w_in)[:, :, hp * 128:(hp + 1) * 128], writes=[BwqB])
            S.dma("pool", wk, wview(w_in)[:, :, 512 + hp * 128:512 + (hp + 1) * 128], writes=[BwkB])
            S.dma("pool", nab.rearrange("p c h q -> p (c h) q"), nab_d[hp], writes=[Bnab])
            qitems = []
            for tb in range(4):
                qitems.append(dict(w=wq, wB=BwqB, rhs_fn=(lambda k, tb=tb: hTf[:, k, 256 + tb * 512:256 + (tb + 1) * 512]),
                                   rhsBs=BhTf[2 + 4 * tb:6 + 4 * tb], mat=blk64, inv_d=1.0 / 64, gain=vec2[:, 0:1], gainB=Bvec2, outB=BqT[tb],
                                   split=[(H0, qzz[H0, 4 * tb:4 * tb + 4, 0, :]), (H1, qzz[H1, 4 * tb:4 * tb + 4, 1, :])]))
            kitems = []
            for fb in range(5):
                kitems.append(dict(w=wk, wB=BwkB, rhs_fn=(lambda k, fb=fb: hTf[:, k, fb * 512:(fb + 1) * 512]),
                                   rhsBs=BhTf[4 * fb:4 * fb + 4], mat=blk64, inv_d=1.0 / 64, gain=vec[:, V_KG:V_KG + 1], gainB=Bvec,
                                   out=kT[:, fb * 512:(fb + 1) * 512], outB=BkT[fb]))
            proj_norm_batch(kitems[0:3])
            proj_norm_batch(kitems[3:5] + qitems[0:1])
            proj_norm_batch(qitems[1:4])

            def emitS(tl):
                b0, nb_, voff = na_band(tl)
                par = tl % 2
                for j in range(nb_):
                    bk = 3 * par + j // 2
                    pp, ppB = ps[bk], Bps[bk]
                    col = (j % 2) * 256
                    f = b0 + j
                    last = (j % 2 == 1) or (j == nb_ - 1)
                    S.op("pe", lambda e, pp=pp, col=col, f=f, tl=tl: e.matmul(
                        pp[:, col:col + 256], lhsT=kT[:, f * 128:(f + 1) * 128], rhs=qzz[:, tl].rearrange("p h q -> p (h q)"), start=True, stop=False),
                        reads=[BkT[f // 4], BqT[tl // 4]], writes=[ppB], inc=False)
                    S.op("pe", lambda e, pp=pp, col=col, j=j, voff=voff: e.matmul(
                        pp[:, col:col + 256], lhsT=ident, rhs=nab[:, voff + j].rearrange("p h q -> p (h q)"), start=False, stop=True),
                        reads=[Bcst, Bnab], writes=[ppB], inc=last)
                Pt, PtB = PT[par], BPT[par]
                for bi in range((nb_ + 1) // 2):
                    bk = 3 * par + bi
                    ncol = min(512, (nb_ - 2 * bi) * 256)
                    S.op("act", lambda e, Pt=Pt, bk=bk, bi=bi, ncol=ncol: e.activation(out=Pt[:, bi * 512:bi * 512 + ncol], in_=ps[bk][:, 0:ncol], func=AF.Exp),
                         reads=[Bps[bk]], writes=[PtB])

            def emitPV(tl):
                b0, nb_, voff = na_band(tl)
                par = tl % 2
                po, poB = ps[6 + par], Bps[6 + par]
                Pt, PtB = PT[par], BPT[par]
                for hh in range(2):
                    h = 2 * hp + hh
                    for j in range(nb_):
                        f = b0 + j
                        S.op("pe", lambda e, po=po, hh=hh, j=j, f=f, h=h, Pt=Pt, nb_=nb_: e.matmul(
                            po[:, hh * 65:(hh + 1) * 65], lhsT=Pt[:, j * 256 + hh * 128:j * 256 + (hh + 1) * 128], rhs=Va[:, f, h, :],
                            start=(j == 0), stop=(j == nb_ - 1)),
                            reads=[PtB, BVa[f]], writes=[poB], inc=(j == nb_ - 1))

            def emitFin(tl, hp=hp):
                par = tl % 2
                po, poB = ps[6 + par], Bps[6 + par]
                pov = po[:, 0:130].rearrange("p (h e) -> p h e", h=2)
                S.op("dve", lambda e: e.reciprocal(out=rc[:, 2 * par:2 * par + 2], in_=pov[:, :, 64]), reads=[poB], writes=[Brc[par]])
                S.op("dve", lambda e: e.tensor_tensor(
                    out=onb[par][:].rearrange("p (h d) -> p h d", h=2), in0=pov[:, :, 0:64],
                    in1=rc[:, 2 * par:2 * par + 2].unsqueeze(2).to_broadcast([128, 2, 64]), op=ALU.mult),
                    reads=[poB, Brc[par]], writes=[Bonb[par]])
                ptb = po[:].bitcast(BF16)[:, 512:640]
                S.op("pe", lambda e: e.transpose(out=ptb, in_=onb[par][:], identity=ident), reads=[Bonb[par], Bcst], writes=[poB])
                S.op("dve", lambda e: e.tensor_copy(out=oT[:, hp, tl * 128:(tl + 1) * 128], in_=ptb), reads=[poB], writes=[BoT[hp][tl]])

            for i in range(16 + 2):
                if i < 16:
                    emitS(i)
                if 1 <= i <= 16:
                    emitPV(i - 1)
                if i >= 2:
                    emitFin(i - 2)

        if stop <= 4:
            if dbg:
                dump(oT[:, 0, :], BoT[0], 2048)
                dump(oT[:, 3, :], BoT[3], 2048)
            finish()
            return nc

        Wf2 = reg(E_OFF, 8192).rearrange("p (a g c) -> p a g c", a=2, g=4)
        BWf2 = Buf("Wf2")
        S.claim([BWf2], [Bnab])
        for a, M in enumerate([CcM, ScM]):
            for g in range(4):
                for half in range(2):
                    p, pB = next_ps()
                    S.op("pe", lambda e, p=p, M=M, g=g, half=half: e.matmul(p[:], lhsT=M, rhs=wf[:, g, half * 512:(half + 1) * 512], start=True, stop=True),
                         reads=[Bcst, Bwf], writes=[pB])
                    S.op("dve", lambda e, p=p, a=a, g=g, half=half: e.tensor_copy(out=Wf2[:, a, g, half * 512:(half + 1) * 512], in_=p[:]),
                         reads=[pB], writes=[BWf2])
        S.claim([b for r in BomT for b in r], [Bwf])
        f1s = [reg(G_OFF, 4096), reg(C_OFF + 16384, 4096)]
        Bf1 = [Buf("f1s0"), Buf("f1s1")]
        S.claim([Bf1[1]], BVa + BqT + BkT)

        def load_f1(dc):
            s_ = (dc + 1) % 2
            base = f1s[s_]
            gw = base[:, 0:3072].rearrange("p (b k c) -> p b k c", b=3, k=8)
            wo = base[:, 3072:4096].rearrange("p (b k c) -> p b k c", b=2, k=4)
            for br in range(3):
                S.dma("pool", gw[:, br], wview(w_in)[:, :, 2560 + br * 1024 + dc * 128:2560 + br * 1024 + (dc + 1) * 128],
                      writes=[Bf1[s_]] if br == 0 else [], reads=[], key=Bf1[s_])
            S.dma("pool", wo[:, 0], w_na_o.rearrange("(k p) c -> p k c", p=128)[:, :, dc * 128:(dc + 1) * 128], key=Bf1[s_])
            tokw = S.dma("pool", wo[:, 1], w_mem_o.rearrange("(k p) c -> p k c", p=128)[:, :, dc * 128:(dc + 1) * 128], key=Bf1[s_])
            Bf1[s_].lw = tokw

        load_f1(0)

        mqs = [[reg(C_OFF + (hs * 4 + tb) * 512, 512) for tb in range(4)] for hs in range(2)]
        Bmqs = [[Buf("mq%d_%d" % (hs, tb)) for tb in range(4)] for hs in range(2)]
        S.claim([b for r in Bmqs for b in r], BVa + BqT + BkT)

        def emitDPN(h):
            wmq, BwmqB = wsl[h % 4], Bwsl[h % 4]
            S.dma("pool", wmq, wview(w_in)[:, :, 2048 + h * 128:2048 + (h + 1) * 128], writes=[BwmqB])
            items = []
            for tb in range(4):
                items.append(dict(w=wmq, wB=BwmqB, rhs_fn=(lambda k, tb=tb: hTf[:, k, 256 + tb * 512:256 + (tb + 1) * 512]),
                                  rhsBs=BhTf[2 + 4 * tb:6 + 4 * tb], mat=ones, inv_d=1.0 / 128, gain=vec2[:, 1:2], gainB=Bvec2,
                                  out=mqs[h % 2][tb], outB=Bmqs[h % 2][tb]))
            proj_norm_batch(items)

        def emitDattn(h, tb):
            mq, mqB = mqs[h % 2][tb], Bmqs[h % 2][tb]
            pts = []
            for c in range(2):
                pS, pSB = next_ps()
                S.op("pe", lambda e, pS=pS, c=c: e.matmul(pS[:], lhsT=kmT[:, h, c * 128:(c + 1) * 128], rhs=mq, start=True, stop=True),
                     reads=[BkmT, mqB], writes=[pSB])
                pt_, ptB_ = next_tb()
                S.op("act", lambda e, pS=pS, pt_=pt_: e.activation(out=pt_[:], in_=pS[:], func=AF.Exp), reads=[pSB], writes=[ptB_])
                pts.append((pt_, ptB_))
            po, poB = next_ps()
            pq, pqB = next_ps()
            for c in range(2):
                S.op("pe", lambda e, c=c: e.matmul(po[:], lhsT=vm[:, c, h * 128:(h + 1) * 128], rhs=pts[c][0][:], start=(c == 0), stop=(c == 1)),
                     reads=[Bvm, pts[c][1]], writes=[poB], inc=(c == 1))
            for c in range(2):
                S.op("pe", lambda e, c=c: e.matmul(pq[:], lhsT=ones, rhs=pts[c][0][:], start=(c == 0), stop=(c == 1)),
                     reads=[Bcst, pts[c][1]], writes=[pqB], inc=(c == 1))
            t, tB = next_tf()
            if True:
                S.op("act", lambda e: e.activation(out=t[:], in_=pq[:], func=AF.Ln), reads=[pqB], writes=[tB])
                S.op("act", lambda e: e.activation(out=t[:], in_=t[:], func=AF.Exp, scale=-1.0), reads=[tB], writes=[tB])
            else:
                S.op("dve", lambda e: e.reciprocal(out=t[:], in_=pq[:]), reads=[pqB], writes=[tB])
            S.op("dve", lambda e: e.tensor_tensor(out=omT[:, h, tb * 512:(tb + 1) * 512], in0=po[:], in1=t[:], op=ALU.mult),
                 reads=[poB, tB], writes=[BomT[h][tb]])

        emitDPN(0)
        for h in range(4):
            if h + 1 < 4:
                emitDPN(h + 1)
            for tb in range(4):
                emitDattn(h, tb)

        if stop <= 5:
            if dbg:
                dump(omT[:, 0, :], BomT[0], 2048)
                dump(omT[:, 3, :], BomT[3], 2048)
            finish()
            return nc

        mT = reg(C_OFF, 16384).rearrange("p (k t) -> p k t", k=8)
        BmT = [[Buf("mT%d_%d" % (k, t)) for t in range(4)] for k in range(8)]
        ncbufs = BVa + BqT + BkT + [b for r in Bmqs for b in r]
        S.claim([b for r in BmT for b in r], ncbufs)
        S.claim([Bf1[0]], [Bwv] + Bwsl)
        for dc in range(8):
            s = (dc + 1) % 2
            base = f1s[s]
            gw = base[:, 0:3072].rearrange("p (b k c) -> p b k c", b=3, k=8)
            wo = base[:, 3072:4096].rearrange("p (b k c) -> p b k c", b=2, k=4)
            if dc + 1 < 8:
                load_f1(dc + 1)
            for tb in range(4):
                tsl = slice(tb * 512, (tb + 1) * 512)
                acc = None
                for br in range(3):
                    pg, pgB = next_ps()
                    for k in range(8):
                        S.op("pe", lambda e, pg=pg, gw=gw, br=br, k=k, tb=tb: e.matmul(
                            pg[:], lhsT=gw[:, br, k, :], rhs=hTf[:, k, 256 + tb * 512:256 + (tb + 1) * 512], start=(k == 0), stop=(k == 7)),
                            reads=[Bf1[s]] + BhTf[2 + 4 * tb:6 + 4 * tb], writes=[pgB], inc=(k == 7))
                    py, pyB = next_ps()
                    if br == 0:
                        for k in range(4):
                            S.op("pe", lambda e, py=py, wo=wo, k=k, tsl=tsl: e.matmul(py[:], lhsT=wo[:, 0, k, :], rhs=oT[:, k, tsl], start=(k == 0), stop=(k == 3)),
                                 reads=[Bf1[s]] + BoT[k][4 * tb:4 * tb + 4], writes=[pyB], inc=(k == 3))
                    elif br == 1:
                        j = 0
                        for a in range(2):
                            for g in range(4):
                                S.op("pe", lambda e, py=py, a=a, g=g, dc=dc, tsl=tsl, j=j: e.matmul(
                                    py[:], lhsT=Wf2[:, a, g, dc * 128:(dc + 1) * 128], rhs=Zt[:, a, g, tsl], start=(j == 0), stop=(j == 7)),
                                    reads=[BWf2] + BZ[a][g], writes=[pyB], inc=(j == 7))
                                j += 1
                    else:
                        for k in range(4):
                            S.op("pe", lambda e, py=py, wo=wo, k=k, tsl=tsl: e.matmul(py[:], lhsT=wo[:, 1, k, :], rhs=omT[:, k, tsl], start=(k == 0), stop=(k == 3)),
                                 reads=[Bf1[s], BomT[k][tb]], writes=[pyB], inc=(k == 3))
                    sg, sgB = next_tf()
                    bcol = V_BG + br * 8 + dc
                    S.op("act", lambda e, sg=sg, pg=pg, bcol=bcol: e.activation(out=sg[:], in_=pg[:], func=AF.Sigmoid, bias=vec[:, bcol:bcol + 1]),
                         reads=[pgB, Bvec], writes=[sgB])
                    if br == 0:
                        S.op("dve", lambda e, sg=sg, py=py: e.tensor_tensor(out=sg[:], in0=sg[:], in1=py[:], op=ALU.mult), reads=[sgB, pyB], writes=[sgB])
                        acc, accB = sg, sgB
                    elif br == 1:
                        S.op("dve", lambda e, sg=sg, py=py: e.tensor_tensor(out=sg[:], in0=sg[:], in1=py[:], op=ALU.mult), reads=[sgB, pyB], writes=[sgB])
                        S.op("dve", lambda e, sg=sg, acc=acc: e.tensor_tensor(out=acc[:], in0=acc[:], in1=sg[:], op=ALU.add), reads=[sgB, accB], writes=[accB])
                    else:
                        S.op("dve", lambda e, sg=sg, py=py: e.tensor_tensor(out=sg[:], in0=sg[:], in1=py[:], op=ALU.mult), reads=[sgB, pyB], writes=[sgB])
                        S.op("dve", lambda e, sg=sg, acc=acc, dc=dc, tsl=tsl: e.tensor_tensor(out=mT[:, dc, tsl], in0=acc[:], in1=sg[:], op=ALU.add),
                             reads=[sgB, accB], writes=[BmT[dc][tb]])

        if stop <= 6:
            if dbg:
                dump(mT[:, 0, :], BmT[0], 2048)
                dump(mT[:, 7, :], BmT[7], 2048)
            finish()
            return nc

        wout = reg(C_OFF + 16384, 8192).rearrange("p (k c) -> p k c", k=8)
        Bwout = Buf("wout")
        S.claim([Bwout], [Bf1[1], Bf1[0]])
        S.dma("pool", wout, wview(w_out), writes=[Bwout])
        x1 = reg(BZ_OFF, BZ_SZ + D_SZ).bitcast(F32).rearrange("p (t c) -> p t c", t=16)
        Bx1 = [[Buf("x1_%d_%d" % (t, hf)) for hf in range(2)] for t in range(16)]
        oldz = [b for r in BZ for gg in r for b in gg] + [b for r in BoT for b in r] + [b for r in BomT for b in r]
        S.claim([b for r in Bx1 for b in r], oldz)
        h2T = reg(A_OFF, 16384).rearrange("p (k t) -> p k t", k=8)
        Bh2 = [Buf("h2T%d" % t) for t in range(16)]
        S.claim(Bh2, BhTf)
        S.dma("sp", gbc[:], gbc_d[1], writes=[Bgbc])
        pend2 = {}
        NXF = 4
        xtf = [reg(E_OFF + i * 2048, 2048).bitcast(F32) for i in range(NXF)]
        Bxtf = [Buf("xtf%d" % i) for i in range(NXF)]
        S.claim(Bxtf, [BWf2])
        ctr["xtf"] = 0

        def emitX1(t):
            xi = nxt("xtf", NXF)
            S.dma("sp", xtf[xi], xr[t * 128:(t + 1) * 128, :], writes=[Bxtf[xi]])
            for hf in range(2):
                p, pB = next_ps()
                for k in range(8):
                    S.op("pe", lambda e, p=p, k=k, hf=hf: e.matmul(p[:], lhsT=mT[:, k, t * 128:(t + 1) * 128], rhs=wout[:, k, hf * 512:(hf + 1) * 512],
                                                                start=(k == 0), stop=(k == 7)),
                         reads=[BmT[k][t // 4], Bwout], writes=[pB], inc=(k == 7))
                S.op("dve", lambda e, p=p, hf=hf, xi=xi: e.tensor_tensor(out=x1[:, t, hf * 512:(hf + 1) * 512], in0=p[:], in1=xtf[xi][:, hf * 512:(hf + 1) * 512], op=ALU.add),
                     reads=[pB, Bxtf[xi]], writes=[Bx1[t][hf]])
            pend2[t] = tok_norm_a(x1[:, t, :], Bx1[t])

        def emitT2(t):
            tok_norm_b(pend2.pop(t), V_G2, h2T[:, :, t * 128:(t + 1) * 128], Bh2[t])

        emitX1(0)
        for t in range(16):
            if t + 1 < 16:
                emitX1(t + 1)
            emitT2(t)

        if stop <= 7:
            if dbg:
                dump(x1[:, 0, :], Bx1[0], 1024)
                dump(h2T[:, 0, :], Bh2, 2048)
            finish()
            return nc

        w1s = [reg(C_OFF + s * 8192, 4096).rearrange("p (k c) -> p k c", k=8) for s in range(2)]
        w2s = [reg(C_OFF + s * 8192 + 4096, 4096).rearrange("p (k c) -> p k c", k=4) for s in range(2)]
        Bw1 = [Buf("w1_%d" % s) for s in range(2)]
        Bw2 = [Buf("w2_%d" % s) for s in range(2)]
        S.claim(Bw1 + Bw2, [b for r in BmT for b in r])
        aTs = [reg(C_OFF + 16384, 8192).rearrange("p (k t) -> p k t", k=4), reg(E_OFF, 8192).rearrange("p (k t) -> p k t", k=4)]
        BaT = [[[Buf("aT%d_%d_%d" % (s, k, t)) for t in range(4)] for k in range(4)] for s in range(2)]
        S.claim([b for r in BaT[0] for b in r], [Bwout])
        S.claim([b for r in BaT[1] for b in r], [BWf2] + Bxtf)
        NG = 8
        for grp in range(NG):
            s = grp % 2
            S.dma("pool", w1s[s], wview(w_ff1)[:, :, grp * 512:(grp + 1) * 512], writes=[Bw1[s]])
            S.dma("pool", w2s[s], w_ff2[grp * 512:(grp + 1) * 512, :].rearrange("(k p) c -> p k c", p=128), writes=[Bw2[s]])
            aT = aTs[s]
            for fc in range(4):
                for tb in range(4):
                    p, pB = next_ps()
                    for k in range(8):
                        S.op("pe", lambda e, p=p, k=k, fc=fc, tb=tb, s=s: e.matmul(p[:], lhsT=w1s[s][:, k, fc * 128:(fc + 1) * 128],
                                                                                 rhs=h2T[:, k, tb * 512:(tb + 1) * 512], start=(k == 0), stop=(k == 7)),
                             reads=[Bw1[s]] + Bh2[4 * tb:4 * tb + 4], writes=[pB], inc=(k == 7))
                    t, tB = next_tf()
                    S.op("act", lambda e, t=t, p=p: e.activation(out=t[:], in_=p[:], func=AF.Relu), reads=[pB], writes=[tB])
                    S.op("act", lambda e, t=t, aT=aT, fc=fc, tb=tb: e.activation(out=aT[:, fc, tb * 512:(tb + 1) * 512], in_=t[:], func=AF.Square),
                         reads=[tB], writes=[BaT[s][fc][tb]])
            for t in range(16):
                for hf in range(2):
                    p, pB = next_ps()
                    for fc in range(4):
                        S.op("pe", lambda e, p=p, fc=fc, t=t, hf=hf, s=s, aT=aT: e.matmul(p[:], lhsT=aT[:, fc, t * 128:(t + 1) * 128],
                                                                                      rhs=w2s[s][:, fc, hf * 512:(hf + 1) * 512], start=(fc == 0), stop=(fc == 3)),
                             reads=[BaT[s][fc][t // 4], Bw2[s]], writes=[pB], inc=(fc == 3))
                    S.op("dve", lambda e, p=p, t=t, hf=hf: e.tensor_tensor(out=x1[:, t, hf * 512:(hf + 1) * 512], in0=p[:], in1=x1[:, t, hf * 512:(hf + 1) * 512], op=ALU.add),
                         reads=[pB, Bx1[t][hf]], writes=[Bx1[t][hf]])
                if grp == NG - 1:
                    S.dma("sp", y[t * 128:(t + 1) * 128, :], x1[:, t, :], reads=Bx1[t], writes=[By[t]])
        finish()
    return nc


def _na_table(rpb, hf):
    rpb = np.asarray(rpb, np.float32)
    tab = np.full((8, NA_NCH, 128, 128), NEG, np.float32)
    kr2 = np.arange(128) // 64
    kc = np.arange(128) % 64
    qr2 = np.arange(128) // 64
    qc = np.arange(128) % 64
    cs = np.clip(qc - 8, 0, 48)
    tls = [0, 1, 2, 14, 15]
    for vi, tl in enumerate(tls):
        b0, nb_, voff = na_band(tl)
        t = 16 * hf + tl
        qrow = 2 * t + qr2
        rs = np.clip(qrow - 4, 0, 56)
        for j in range(nb_):
            f = b0 + j
            lt = (f - 2) % 32
            real = (2 <= f < 18) or (f < 2 and hf == 1) or (f >= 18 and hf == 0)
            if not real:
                continue
            gt = (lt + 16 * hf) % 32
            krow = 2 * gt + kr2
            dr = krow[:, None] - qrow[None, :]
            rowok = (krow[:, None] >= rs[None, :]) & (krow[:, None] < rs[None, :] + 8)
            colok = (kc[:, None] >= cs[None, :]) & (kc[:, None] < cs[None, :] + 16)
            ok = rowok & colok
            dri = np.clip(dr + 7, 0, 14)
            dci = np.clip(kc[:, None] - qc[None, :], -15, 15) + 15
            g = rpb[:, dri, dci]
            tab[:, voff + j] = np.where(ok[None], g, NEG)
    tab = tab.reshape(4, 2, NA_NCH, 128, 128).transpose(0, 3, 2, 1, 4).reshape(4, 128, 2 * NA_NCH, 128)
    return np.ascontiguousarray(tab)


def _fft_tables(hf):
    m = np.arange(1024, dtype=np.int64)
    out = np.zeros((4, 2, 1024, 512), np.float64)
    w = np.arange(512, dtype=np.int64)
    for v in range(4):
        sp = 2048 * hf + 4 * w + v
        ph = (m[:, None] * sp[None, :]) % 4096
        ang = 2.0 * np.pi * ph / 4096.0
        sign = -1.0 if (hf == 1 and v % 2 == 1) else 1.0
        out[v, 0] = sign * np.cos(ang)
        out[v, 1] = -sign * np.sin(ang)
    out = out.reshape(4, 2, 8, 128, 512).transpose(0, 3, 1, 2, 4)
    return np.ascontiguousarray(out.astype(np.float32)).astype(ml_dtypes.bfloat16)


def _consts():
    c = np.zeros((128, 5, 128), np.float64)
    c[:, 0] = np.eye(128)
    blk = np.zeros((128, 128))
    blk[:64, :64] = 1.0
    blk[64:, 64:] = 1.0
    c[:, 1] = blk
    c[:, 2] = 1.0
    i = np.arange(128)
    ang = 2.0 * np.pi * ((i[:, None] * i[None, :]) % 128) / 128.0
    nrm = 1.0 / np.sqrt(4096.0 * 128.0)
    c[:, 3] = np.cos(ang) * nrm
    c[:, 4] = np.sin(ang) * nrm
    return c.astype(np.float32).astype(ml_dtypes.bfloat16)


def _vecs(norm1_g, norm2_g, mem_norm_g, b_gate, na_q_g, na_k_g, mem_q_g, mem_k_g):
    v = np.zeros((128, NVEC), np.float32)
    v[:, V_G1:V_G1 + 8] = np.asarray(norm1_g, np.float32).reshape(8, 128).T
    v[:, V_G2:V_G2 + 8] = np.asarray(norm2_g, np.float32).reshape(8, 128).T
    v[:, V_GM:V_GM + 8] = np.asarray(mem_norm_g, np.float32).reshape(8, 128).T
    v[:, V_BG:V_BG + 24] = np.asarray(b_gate, np.float32).reshape(24, 128).T
    v[:, V_QG] = np.tile(np.asarray(na_q_g, np.float32), 2)
    v[:, V_KG] = np.tile(np.asarray(na_k_g, np.float32), 2)
    v[:, V_MQG] = np.asarray(mem_q_g, np.float32)
    v[:, V_MKG] = np.asarray(mem_k_g, np.float32)
    return v


def make_in_maps(x, mem, norm1_g, w_in, b_gate, na_q_g, na_k_g, na_rpb, w_na_o, w_f,
                 mem_norm_g, w_mem_kv, mem_q_g, mem_k_g, w_mem_o, w_out, norm2_g, w_ff1, w_ff2):
    f = lambda a: np.ascontiguousarray(np.asarray(a, np.float32))
    x = f(x)
    mem = f(mem)
    shared = {
        "w_in": f(w_in), "w_na_o": f(w_na_o), "w_f": f(w_f), "w_mem_o": f(w_mem_o), "w_mem_kv": f(w_mem_kv),
        "w_out": f(w_out), "w_ff1": f(w_ff1), "w_ff2": f(w_ff2),
        "vecs": _vecs(norm1_g, norm2_g, mem_norm_g, b_gate, na_q_g, na_k_g, mem_q_g, mem_k_g),
        "consts": _consts(),
        "gbc": np.ascontiguousarray(np.stack([np.broadcast_to(np.asarray(g, np.float32)[None, :], (128, D))
                                              for g in (norm1_g, norm2_g, mem_norm_g)])),
    }
    nat = [_na_table(na_rpb, hf) for hf in range(2)]
    fft = [_fft_tables(hf) for hf in range(2)]
    maps = []
    for c in range(8):
        b, hf = c // 2, c % 2
        m = dict(shared)
        m["xr"] = np.ascontiguousarray(np.roll(x[b], -NOWN * hf, axis=0))
        m["mem"] = mem[b]
        m["nabias"] = nat[hf]
        m["fftab"] = fft[hf]
        maps.append(m)
    return maps


_NC_CACHE = {}


def kernel(**inputs):
    maps = make_in_maps(**inputs)
    if "nc" not in _NC_CACHE:
        _NC_CACHE["nc"] = build()
    nc = _NC_CACHE["nc"]
    res = run_bass_kernel_spmd(nc, maps, core_ids=list(range(8)))
    out = np.zeros((4, SEQ, D), np.float32)
    for c in range(8):
        b, hf = c // 2, c % 2
        out[b, NOWN * hf:NOWN * (hf + 1)] = res.results[c]["y"]
    return out
```

```python
from contextlib import ExitStack

import numpy as np
import ml_dtypes

import concourse.bass as bass
import concourse.mybir as mybir
from concourse.bass_utils import run_bass_kernel_spmd

F32 = mybir.dt.float32
BF16 = mybir.dt.bfloat16
AF = mybir.ActivationFunctionType
ALU = mybir.AluOpType

SAME_ENG_SYNC = True
EPS = 1e-6
NEG = -1e30


class Buf:
    __slots__ = ("name", "lw", "rds", "dsem", "dcnt")

    def __init__(self, name):
        self.name = name
        self.lw = None
        self.rds = {}
        self.dsem = None
        self.dcnt = 0


class EngQ:
    def __init__(self, name, sem, is_pe=False):
        self.name = name
        self.sem = sem
        self.n = 0
        self.waited = {}
        self.ops = []
        self.is_pe = is_pe


class Sched:
    def __init__(self, nc, stack):
        self.nc = nc
        self.stack = stack
        self.q = {}
        for name in ("pe", "act", "dve", "pool", "sp"):
            sem = stack.enter_context(nc.semaphore("q_" + name))
            self.q[name] = EngQ(name, sem, is_pe=(name == "pe"))
        self.nsem = 5
        self.free_sems = []

    def newsem(self, name):
        self.nsem += 1
        return self.stack.enter_context(self.nc.semaphore(name))

    def _collect(self, q, reads, writes):
        deps = {}

        def add(t):
            if t is None:
                return
            sem, val = t
            if sem is q.sem and (q.is_pe or not SAME_ENG_SYNC):
                return
            k = id(sem)
            if q.waited.get(k, 0) >= val:
                return
            if k not in deps or deps[k][1] < val:
                deps[k] = (sem, val)

        for b in reads:
            add(b.lw)
        for b in writes:
            add(b.lw)
            for t in b.rds.values():
                add(t)
        out = list(deps.values())
        for sem, val in out:
            q.waited[id(sem)] = val
        return out

    def _update(self, tok, reads, writes):
        for b in writes:
            b.lw = tok
            b.rds = {}
        k = id(tok[0])
        for b in reads:
            if k not in b.rds or b.rds[k][1] < tok[1]:
                b.rds[k] = tok

    def op(self, qn, fn, reads=(), writes=(), inc=True):
        q = self.q[qn]
        waits = self._collect(q, reads, writes)
        tok = (q.sem, q.n + 1)
        if inc:
            q.n += 1
        else:
            assert q.is_pe
        q.ops.append((waits, fn, (q.sem, 1) if inc else None))
        self._update(tok, reads, writes)

    def dma(self, qn, out, in_, reads=(), writes=(), key=None, **kw):
        q = self.q[qn]
        waits = self._collect(q, reads, writes)
        kb = key if key is not None else (writes[0] if writes else reads[0])
        if kb.dsem is None:
            kb.dsem = self.newsem("d_" + kb.name)
        kb.dcnt += 16
        tok = (kb.dsem, kb.dcnt)

        def fn(e, out=out, in_=in_, kw=kw):
            return e.dma_start(out=out, in_=in_, **kw)

        q.ops.append((waits, fn, (kb.dsem, 16)))
        self._update(tok, reads, writes)
        return tok

    def claim(self, new_bufs, old_bufs):
        toks = {}
        for b in old_bufs:
            for t in ([b.lw] if b.lw else []) + list(b.rds.values()):
                k = id(t[0])
                if k not in toks or toks[k][1] < t[1]:
                    toks[k] = t
        for nb in new_bufs:
            for k, t in toks.items():
                if k not in nb.rds or nb.rds[k][1] < t[1]:
                    nb.rds[k] = t

    def wait_all(self, qn, bufs):
        q = self.q[qn]
        waits = self._collect(q, (), bufs)
        q.ops.append((waits, None, None))

    def emit(self):
        nc = self.nc
        qs = self.q

        def run(q, e):
            for waits, fn, inc in q.ops:
                for sem, val in waits:
                    e.wait_ge(sem, val)
                if fn is None:
                    continue
                ins = fn(e)
                if inc is not None:
                    ins.then_inc(inc[0], inc[1])

        with nc.Block() as block:
            @block.tensor
            def _(e):
                run(qs["pe"], e)

            @block.scalar
            def _(e):
                run(qs["act"], e)

            @block.vector
            def _(e):
                run(qs["dve"], e)

            @block.gpsimd
            def _(e):
                run(qs["pool"], e)

            @block.sync
            def _(e):
                run(qs["sp"], e)


D = 1024
SEQ = 4096
NOWN = 2048
NFR = 20
NA_VARIANTS = [(0, 6), (1, 5), (2, 5), (14, 5), (14, 6)]
NA_VOFF = [0, 6, 11, 16, 21]
NA_NCH = 27


def na_band(tl):
    if tl == 0:
        return 0, 6, NA_VOFF[0]
    if tl == 1:
        return 1, 5, NA_VOFF[1]
    if tl == 14:
        return 14, 5, NA_VOFF[3]
    if tl == 15:
        return 14, 6, NA_VOFF[4]
    return tl, 5, NA_VOFF[2]


V_G1, V_G2, V_GM, V_BG, V_QG, V_KG, V_MQG, V_MKG = 0, 8, 16, 24, 48, 49, 50, 51
NVEC = 64

A_OFF, A_SZ = 0, 20480
BZ_OFF, BZ_SZ = 20480, 16384
D_OFF, D_SZ = 36864, 16384
C_OFF, C_SZ = 53248, 20480
G_OFF, G_SZ = 73728, 4096
E_OFF, E_SZ = 77824, 8192
ARENA = 86016


def build(stop=99, dbg=False):
    nc = bass.Bass("TRN2", target_bir_lowering=False)

    def dram(n, s, d, kind="ExternalInput"):
        return nc.dram_tensor(n, s, d, kind=kind).ap()

    xr = dram("xr", [SEQ, D], F32)
    mem = dram("mem", [256, D], F32)
    w_in = dram("w_in", [D, 5632], F32)
    w_na_o = dram("w_na_o", [512, D], F32)
    w_f = dram("w_f", [512, D], F32)
    w_mem_o = dram("w_mem_o", [512, D], F32)
    w_mem_kv = dram("w_mem_kv", [D, D], F32)
    w_out = dram("w_out", [D, D], F32)
    w_ff1 = dram("w_ff1", [D, 4096], F32)
    w_ff2 = dram("w_ff2", [4096, D], F32)
    vecs_d = dram("vecs", [128, NVEC], F32)
    nab_d = dram("nabias", [4, 128, 2 * NA_NCH, 128], F32)
    fft_d = dram("fftab", [4, 128, 2, 8, 512], BF16)
    cst_d = dram("consts", [128, 5, 128], BF16)
    gbc_d = dram("gbc", [3, 128, D], F32)
    y = dram("y", [NOWN, D], F32, kind="ExternalOutput")
    if dbg:
        dbg_d = dram("dbg", [128, 8192], F32, kind="ExternalOutput")

    def wview(w):
        return w.rearrange("(k p) c -> p k c", p=128)

    st = ExitStack()
    with st:
        S = Sched(nc, st)

        def sb(n, s, d):
            return st.enter_context(nc.sbuf_tensor(n, s, d))

        ar = sb("arena", [128, ARENA], BF16)

        def reg(off, n):
            return ar[:, off:off + n]

        cst = sb("cst", [128, 5, 128], BF16)
        vec = sb("vec", [128, NVEC], F32)
        vec2 = sb("vec2", [128, 4], F32)
        stat = sb("stat", [128, 64], F32)
        rstd = sb("rstd", [128, 64], F32)
        gbc = sb("gbcsb", [128, D], F32)
        Bgbc = Buf("gbc")
        NXA = 6
        xt = [reg(D_OFF + i * 2048, 2048).bitcast(F32) for i in range(NXA)]
        xs = [sb("xs%d" % i, [128, D], BF16) for i in range(4)]
        tmpf = [sb("tmpf%d" % i, [128, 512], F32) for i in range(5)]
        tmpb = [sb("tmpb%d" % i, [128, 512], BF16) for i in range(4)]
        PT = [sb("PT%d" % i, [128, 1536], BF16) for i in range(2)]
        kmT = sb("kmT", [128, 4, 256], BF16)
        vm = sb("vm", [128, 2, 512], BF16)
        onb = [sb("onb%d" % i, [128, 128], BF16) for i in range(2)]
        rc = sb("rc", [128, 4], F32)
        ps = [st.enter_context(nc.psum_tensor("ps%d" % i, [128, 512], F32)) for i in range(8)]

        Bcst, Bvec, Bvec2 = Buf("cst"), Buf("vec"), Buf("vec2")
        Bxt = [Buf("xt%d" % i) for i in range(NXA)]
        Bxs = [Buf("xs%d" % i) for i in range(4)]
        Btf = [Buf("tmpf%d" % i) for i in range(5)]
        Btb = [Buf("tmpb%d" % i) for i in range(4)]
        BPT = [Buf("PT%d" % i) for i in range(2)]
        Bps = [Buf("ps%d" % i) for i in range(8)]
        BkmT, Bvm = Buf("kmT"), Buf("vm")
        Bonb = [Buf("onb%d" % i) for i in range(2)]
        Brc = [Buf("rc%d" % i) for i in range(2)]
        Bstat = [Buf("stat%d" % i) for i in range(64)]

        ident = cst[:, 0, :]
        blk64 = cst[:, 1, :]
        ones = cst[:, 2, :]
        CcM = cst[:, 3, :]
        ScM = cst[:, 4, :]

        ctr = {"ps": 0, "tf": 0, "tb": 0, "xt": 0, "xs": 0, "st": 0, "ev": 0, "st2": 0}

        def nxt(key, n):
            i = ctr[key] % n
            ctr[key] += 1
            return i

        def next_ps():
            i = nxt("ps", 8)
            return ps[i], Bps[i]

        def next_tf():
            i = nxt("tf", 5)
            return tmpf[i], Btf[i]

        def next_tb():
            i = nxt("tb", 4)
            return tmpb[i], Btb[i]

        def next_stat():
            i = nxt("st", 32)
            return i, Bstat[i]

        S.dma("sp", cst[:], cst_d, writes=[Bcst])
        S.dma("sp", vec[:], vecs_d, writes=[Bvec])
        S.op("dve", lambda e: e.tensor_scalar(out=vec2[:, 0:1], in0=vec[:, V_QG:V_QG + 1], scalar1=0.125, scalar2=None, op0=ALU.mult),
             reads=[Bvec], writes=[Bvec2])
        S.op("dve", lambda e: e.tensor_scalar(out=vec2[:, 1:2], in0=vec[:, V_MQG:V_MQG + 1], scalar1=float(128 ** -0.5), scalar2=None, op0=ALU.mult),
             reads=[Bvec], writes=[Bvec2])

        junk = PT[0][:, 0:D]

        def tok_norm_a(src_ap, srcBs):
            si, sB = next_stat()
            xi = nxt("xs", 4)
            S.op("act", lambda e: e.activation(out=junk, in_=src_ap, func=AF.Square, accum_out=stat[:, si:si + 1]),
                 reads=srcBs, writes=[sB, BPT[0]])
            S.op("act", lambda e: e.activation(out=stat[:, si:si + 1], in_=stat[:, si:si + 1], func=AF.Sqrt, scale=1.0 / D, bias=EPS),
                 reads=[sB], writes=[sB])
            S.op("dve", lambda e: e.reciprocal(out=rstd[:, si:si + 1], in_=stat[:, si:si + 1]), reads=[sB], writes=[sB])
            S.op("dve", lambda e: e.scalar_tensor_tensor(out=xs[xi][:], in0=src_ap, scalar=rstd[:, si:si + 1], in1=gbc[:],
                                                          op0=ALU.mult, op1=ALU.mult),
                 reads=list(srcBs) + [sB, Bgbc], writes=[Bxs[xi]])
            return xi

        def tok_norm_b(xi, gcol, dst_fn, dstB, bank=None, ev=None):
            p, pB = next_ps() if bank is None else (ps[bank], Bps[bank])
            pb = p[:].bitcast(BF16)
            for k in range(8):
                S.op("pe", lambda e, k=k: e.transpose(out=pb[:, k * 128:(k + 1) * 128], in_=xs[xi][:, k * 128:(k + 1) * 128], identity=ident),
                     reads=[Bxs[xi], Bcst], writes=[pB], inc=(k == 7))
            if ev is None:
                ev = nxt("ev", 2)
            if ev == 0:
                S.op("act", lambda e: e.activation(out=dst_fn, in_=pb.rearrange("p (k t) -> p k t", k=8), func=AF.Copy),
                     reads=[pB], writes=[dstB])
            else:
                S.op("dve", lambda e: e.tensor_copy(out=dst_fn, in_=pb.rearrange("p (k t) -> p k t", k=8)), reads=[pB], writes=[dstB])

        def tok_norm_a2(srcs):
            c = 32 + 2 * nxt("st2", 16)
            sB = Bstat[c]
            for j, (src_ap, srcBs) in enumerate(srcs):
                S.op("act", lambda e, src_ap=src_ap, j=j: e.activation(out=junk, in_=src_ap, func=AF.Square, accum_out=stat[:, c + j:c + j + 1]),
                     reads=srcBs, writes=[sB, BPT[0]])
            S.op("act", lambda e: e.activation(out=stat[:, c:c + 2], in_=stat[:, c:c + 2], func=AF.Sqrt, scale=1.0 / D, bias=EPS),
                 reads=[sB], writes=[sB])
            S.op("dve", lambda e: e.reciprocal(out=rstd[:, c:c + 2], in_=stat[:, c:c + 2]), reads=[sB], writes=[sB])
            xis = []
            for j, (src_ap, srcBs) in enumerate(srcs):
                xi = nxt("xs", 4)
                S.op("dve", lambda e, src_ap=src_ap, j=j, xi=xi: e.scalar_tensor_tensor(out=xs[xi][:], in0=src_ap, scalar=rstd[:, c + j:c + j + 1], in1=gbc[:],
                                                                                     op0=ALU.mult, op1=ALU.mult),
                     reads=list(srcBs) + [sB, Bgbc], writes=[Bxs[xi]])
                xis.append(xi)
            return xis

        def tok_norm_transpose(src_ap, srcB, gcol, dst_fn, dstB):
            xi = tok_norm_a(src_ap, [srcB])
            tok_norm_b(xi, gcol, dst_fn, dstB)

        def fm_norm(p, pB, ncols, mat, inv_d, gain_ap, gainB, out_ap, outB, split=None, split3=False):
            sq, sqB = next_tb()
            S.op("act", lambda e: e.activation(out=sq[:, :ncols], in_=p[:, :ncols], func=AF.Square), reads=[pB], writes=[sqB])
            p2, p2B = next_ps()
            S.op("pe", lambda e: e.matmul(p2[:, :ncols], lhsT=mat, rhs=sq[:, :ncols], start=True, stop=True),
                 reads=[sqB, Bcst], writes=[p2B])
            t, tB = next_tf()
            S.op("act", lambda e: e.activation(out=t[:, :ncols], in_=p2[:, :ncols], func=AF.Ln, scale=inv_d, bias=EPS),
                 reads=[p2B], writes=[tB])
            S.op("act", lambda e: e.activation(out=t[:, :ncols], in_=t[:, :ncols], func=AF.Exp, scale=-0.5), reads=[tB], writes=[tB])
            if split is None:
                S.op("dve", lambda e: e.scalar_tensor_tensor(out=out_ap, in0=p[:, :ncols], scalar=gain_ap, in1=t[:, :ncols],
                                                              op0=ALU.mult, op1=ALU.mult),
                     reads=[pB, tB, gainB], writes=[outB])
            else:
                for (pr, oap) in split:
                    i0 = p[pr, :ncols]
                    i1 = t[pr, :ncols]
                    if split3:
                        i0 = i0.rearrange("p (t q) -> p t q", q=128)
                        i1 = i1.rearrange("p (t q) -> p t q", q=128)
                    S.op("dve", lambda e, pr=pr, oap=oap, i0=i0, i1=i1: e.scalar_tensor_tensor(out=oap, in0=i0, scalar=gain_ap[pr], in1=i1,
                                                                                             op0=ALU.mult, op1=ALU.mult),
                         reads=[pB, tB, gainB], writes=[outB])

        def proj_fm(wt, wB, ncolchunk, rhs_fn, rhsBs, p, pB, ncols):
            for k in range(8):
                S.op("pe", lambda e, k=k: e.matmul(p[:, :ncols], lhsT=wt[:, k, ncolchunk], rhs=rhs_fn(k), start=(k == 0), stop=(k == 7)),
                     reads=[wB] + list(rhsBs), writes=[pB], inc=(k == 7))

        def proj_norm_batch(items):
            n = len(items)
            P = []
            for it in items:
                p, pB = next_ps()
                proj_fm(it["w"], it["wB"], slice(0, 128), it["rhs_fn"], it["rhsBs"], p, pB, 512)
                P.append((p, pB))
            SQ = []
            for it, (p, pB) in zip(items, P):
                sq, sqB = next_tb()
                S.op("act", lambda e, sq=sq, p=p: e.activation(out=sq[:], in_=p[:], func=AF.Square), reads=[pB], writes=[sqB])
                SQ.append((sq, sqB))
            P2 = []
            for it, (sq, sqB) in zip(items, SQ):
                p2, p2B = next_ps()
                S.op("pe", lambda e, p2=p2, sq=sq, it=it: e.matmul(p2[:], lhsT=it["mat"], rhs=sq[:], start=True, stop=True),
                     reads=[sqB, Bcst], writes=[p2B])
                P2.append((p2, p2B))
            T = []
            for it, (p2, p2B) in zip(items, P2):
                t, tB = next_tf()
                S.op("act", lambda e, t=t, p2=p2, it=it: e.activation(out=t[:], in_=p2[:], func=AF.Ln, scale=it["inv_d"], bias=EPS),
                     reads=[p2B], writes=[tB])
                S.op("act", lambda e, t=t: e.activation(out=t[:], in_=t[:], func=AF.Exp, scale=-0.5), reads=[tB], writes=[tB])
                T.append((t, tB))
            for it, (p, pB), (t, tB) in zip(items, P, T):
                if it.get("split") is None:
                    S.op("dve", lambda e, it=it, p=p, t=t: e.scalar_tensor_tensor(out=it["out"], in0=p[:], scalar=it["gain"], in1=t[:],
                                                                                  op0=ALU.mult, op1=ALU.mult),
                         reads=[pB, tB, it["gainB"]], writes=[it["outB"]])
                else:
                    for (pr, oap) in it["split"]:
                        i0 = p[pr, :].rearrange("p (t q) -> p t q", q=128)
                        i1 = t[pr, :].rearrange("p (t q) -> p t q", q=128)
                        S.op("dve", lambda e, it=it, pr=pr, oap=oap, i0=i0, i1=i1: e.scalar_tensor_tensor(
                            out=oap, in0=i0, scalar=it["gain"][pr], in1=i1, op0=ALU.mult, op1=ALU.mult),
                            reads=[pB, tB, it["gainB"]], writes=[it["outB"]])

        dbg_off = [0]

        def dump(ap, B, n):
            for c0 in range(0, n, 512):
                w = min(512, n - c0)
                t, tB = next_tf()
                S.op("act", lambda e, c0=c0, w=w, t=t: e.activation(out=t[:, :w], in_=ap[:, c0:c0 + w], func=AF.Copy), reads=B, writes=[tB])
                o = dbg_off[0]
                S.dma("sp", dbg_d[:, o:o + w], t[:, :w], reads=[tB], writes=[Bdbg])
                dbg_off[0] += w

        Bdbg = Buf("dbg")
        By = [Buf("y%d" % i) for i in range(16)]

        def finish():
            S.wait_all("sp", By + [Bdbg])
            S.emit()

        wkv = reg(C_OFF, 8192).rearrange("p (k c) -> p k c", k=8)
        Bwkv = Buf("wkv")
        S.dma("pool", wkv, wview(w_mem_kv), writes=[Bwkv])
        S.dma("sp", gbc[:], gbc_d[2], writes=[Bgbc])
        memT = reg(BZ_OFF + 12288, 2048).rearrange("p (k t) -> p k t", k=8)
        BmemT = [Buf("memT%d" % i) for i in range(2)]
        for mt in range(2):
            xi = nxt("xt", NXA)
            S.dma("sp", xt[xi], mem[mt * 128:(mt + 1) * 128, :], writes=[Bxt[xi]])
            tok_norm_transpose(xt[xi], Bxt[xi], V_GM, memT[:, :, mt * 128:(mt + 1) * 128], BmemT[mt])
        for h in range(4):
            p, pB = next_ps()
            proj_fm(wkv, Bwkv, slice(h * 128, (h + 1) * 128), lambda k: memT[:, k, :], BmemT, p, pB, 256)
            fm_norm(p, pB, 256, ones, 1.0 / 128, vec[:, V_MKG:V_MKG + 1], Bvec, kmT[:, h, :], BkmT)
        for c in range(2):
            p, pB = next_ps()
            for k in range(8):
                S.op("pe", lambda e, k=k, c=c, p=p: e.matmul(p[:], lhsT=memT[:, k, c * 128:(c + 1) * 128], rhs=wkv[:, k, 512:1024],
                                                          start=(k == 0), stop=(k == 7)),
                     reads=[BmemT[c], Bwkv], writes=[pB], inc=(k == 7))
            S.op("act", lambda e, c=c, p=p: e.activation(out=vm[:, c, :], in_=p[:], func=AF.Copy), reads=[pB], writes=[Bvm])

        hTf = reg(A_OFF, A_SZ).rearrange("p (k t) -> p k t", k=8)
        hTo = reg(BZ_OFF, 12288).rearrange("p (k t) -> p k t", k=8)
        BhT = [Buf("hT%d" % i) for i in range(32)]

        def hT_tile(lt):
            f = (lt + 2) % 32
            if f < NFR:
                return hTf[:, :, f * 128:(f + 1) * 128]
            return hTo[:, :, (lt - 18) * 128:(lt - 17) * 128]

        BhTf = [BhT[(f - 2) % 32] for f in range(NFR)]

        S.dma("sp", gbc[:], gbc_d[0], writes=[Bgbc])
        wu = reg(G_OFF, 4096).rearrange("p (k c) -> p k c", k=8)
        Bwu = Buf("wu")
        S.dma("pool", wu, wview(w_in)[:, :, 1536:2048], writes=[Bwu])

        var = reg(C_OFF, C_SZ).rearrange("p (v i c) -> p v i c", v=5, i=8)
        Bvar = [[Buf("var%d_%d" % (v, i)) for i in range(8)] for v in range(5)]
        S.claim([b for r in Bvar for b in r], [Bwkv])

        order = [8 * e4 + i for i in range(8) for e4 in range(4)]
        pend = {}

        def emitN(n):
            lt = order[n]
            xi = nxt("xt", NXA)
            S.dma("sp", xt[xi], xr[lt * 128:(lt + 1) * 128, :], writes=[Bxt[xi]])
            pend[n] = tok_norm_a(xt[xi], [Bxt[xi]])

        def emitT(n, bank=None):
            lt = order[n]
            tok_norm_b(pend.pop(n), V_G1, hT_tile(lt), BhT[lt], bank=bank)

        def emit_combos(i, U):
            t0, t0B = next_tf()
            t1, t1B = next_tf()
            S.op("act", lambda e: e.activation(out=t0[:], in_=U[0][0][:], func=AF.Copy), reads=[U[0][1]], writes=[t0B])
            S.op("act", lambda e: e.activation(out=t1[:], in_=U[1][0][:], func=AF.Copy), reads=[U[1][1]], writes=[t1B])
            fa, faB = next_tf()
            fb, fbB = next_tf()
            S.op("dve", lambda e: e.tensor_tensor(out=fa[:], in0=t0[:], in1=U[2][0][:], op=ALU.add), reads=[t0B, U[2][1]], writes=[faB])
            S.op("dve", lambda e: e.tensor_tensor(out=var[:, 2, i, :], in0=t0[:], in1=U[2][0][:], op=ALU.subtract),
                 reads=[t0B, U[2][1]], writes=[Bvar[2][i]])
            S.op("dve", lambda e: e.tensor_tensor(out=fb[:], in0=t1[:], in1=U[3][0][:], op=ALU.add), reads=[t1B, U[3][1]], writes=[fbB])
            S.op("dve", lambda e: e.tensor_tensor(out=var[:, 3, i, :], in0=t1[:], in1=U[3][0][:], op=ALU.subtract),
                 reads=[t1B, U[3][1]], writes=[Bvar[3][i]])
            S.op("dve", lambda e: e.tensor_tensor(out=var[:, 0, i, :], in0=fa[:], in1=fb[:], op=ALU.add), reads=[faB, fbB], writes=[Bvar[0][i]])
            S.op("dve", lambda e: e.tensor_tensor(out=var[:, 1, i, :], in0=fa[:], in1=fb[:], op=ALU.subtract), reads=[faB, fbB], writes=[Bvar[1][i]])
            S.op("dve", lambda e: e.tensor_scalar(out=var[:, 4, i, :], in0=var[:, 3, i, :], scalar1=-1.0, scalar2=None, op0=ALU.mult),
                 reads=[Bvar[3][i]], writes=[Bvar[4][i]])

        Upend = []

        def emit_uproj(n):
            i, e4 = n // 4, n % 4
            lt = order[n]
            bk = 4 * (i % 2) + e4
            p, pB = ps[bk], Bps[bk]
            hv = hT_tile(lt)
            for k in range(8):
                S.op("pe", lambda e, k=k, p=p, hv=hv: e.matmul(p[:], lhsT=hv[:, k, :], rhs=wu[:, k, :], start=(k == 0), stop=(k == 7)),
                     reads=[BhT[lt], Bwu], writes=[pB], inc=(k == 7))
            if e4 == 3:
                Upend.append((i, [(ps[4 * (i % 2) + j], Bps[4 * (i % 2) + j]) for j in range(4)]))

        def emitN2(j):
            srcs = []
            for n in (2 * j, 2 * j + 1):
                lt = order[n]
                xi = nxt("xt", NXA)
                S.dma("sp", xt[xi], xr[lt * 128:(lt + 1) * 128, :], writes=[Bxt[xi]])
                srcs.append((xt[xi], [Bxt[xi]]))
            xis = tok_norm_a2(srcs)
            pend[2 * j], pend[2 * j + 1] = xis

        def emitT2p(j):
            for q_, n in enumerate((2 * j, 2 * j + 1)):
                i, e4 = n // 4, n % 4
                lt = order[n]
                tok_norm_b(pend.pop(n), V_G1, hT_tile(lt), BhT[lt], bank=4 * (i % 2) + e4, ev=q_)

        emitN2(0)
        for j in range(16):
            if j + 1 < 16:
                emitN2(j + 1)
            emitT2p(j)
            if stop >= 2:
                if j >= 1:
                    emit_uproj(2 * (j - 1))
                    emit_uproj(2 * (j - 1) + 1)
                if Upend and j >= 2 * Upend[0][0] + 3:
                    emit_combos(*Upend.pop(0))
        if stop >= 2:
            emit_uproj(30)
            emit_uproj(31)
        while Upend:
            emit_combos(*Upend.pop(0))

        if stop <= 2:
            if dbg:
                dump(hTf[:, 0, :], BhTf, 2560)
                if stop == 2:
                    for v in range(5):
                        dump(var[:, v, 0, :], [Bvar[v][0]], 512)
            finish()
            return nc

        Zt = reg(BZ_OFF, BZ_SZ).rearrange("p (r g t) -> p r g t", r=2, g=4)
        BZ = [[[Buf("Z%d_%d_%d" % (r, g, v)) for v in range(4)] for g in range(4)] for r in range(2)]
        S.claim([b for r in BZ for gg in r for b in gg], BhT[18:30] + BmemT)
        tabs = [reg(D_OFF + s * 8192, 8192).rearrange("p (a i w) -> p a i w", a=2, i=8) for s in range(2)]
        Btab = [Buf("ftab%d" % s) for s in range(2)]
        S.claim(Btab, Bxt)
        CLS = {
            0: ([(0, 0)], [(0, 1)]),
            2: ([(1, 0)], [(1, 1)]),
            1: ([(2, 0), (3, 1)], [(4, 0), (2, 1)]),
            3: ([(2, 0), (4, 1)], [(3, 0), (2, 1)]),
        }
        for ci, v in enumerate([0, 2, 1, 3]):
            s = ci % 2
            S.dma("sp", tabs[s], fft_d[v], writes=[Btab[s]])
            for g in range(4):
                for ri in range(2):
                    terms = CLS[v][ri]
                    p, pB = next_ps()
                    n = len(terms) * 8
                    j = 0
                    for (vi, ab) in terms:
                        for i in range(8):
                            S.op("pe", lambda e, vi=vi, ab=ab, i=i, g=g, p=p, j=j, n=n, s=s: e.matmul(
                                p[:], lhsT=var[:, vi, i, g * 128:(g + 1) * 128], rhs=tabs[s][:, ab, i, :], start=(j == 0), stop=(j == n - 1)),
                                reads=[Bvar[vi][i], Btab[s]], writes=[pB], inc=(j == n - 1))
                            j += 1
                    zo = Zt[:, ri, g, :].rearrange("p (w v) -> p v w", v=4)[:, v, :]
                    S.op("act", lambda e, zo=zo, p=p: e.activation(out=zo, in_=p[:], func=AF.Copy), reads=[pB], writes=[BZ[ri][g][v]])
        BZg = [[BZ[r][g] for g in range(4)] for r in range(2)]

        if stop <= 3:
            if dbg:
                dump(Zt[:, 0, 0, :], BZ[0][0], 2048)
                dump(Zt[:, 1, 1, :], BZ[1][1], 2048)
            finish()
            return nc

        Va = reg(C_OFF, 10400).rearrange("p (f h e) -> p f h e", f=NFR, h=8)
        BVa = [Buf("Va%d" % f) for f in range(NFR)]
        qzz = reg(C_OFF + 10400, 4096).rearrange("p (t h q) -> p t h q", t=16, h=2)
        kT = reg(C_OFF + 10400 + 4096, 2560)
        BqT = [Buf("qT%d" % t) for t in range(4)]
        BkT = [Buf("kT%d" % t) for t in range(5)]
        allvar = [b for r in Bvar for b in r]
        S.claim(BVa + BqT + BkT, allvar)
        wv = reg(G_OFF, 4096).rearrange("p (k c) -> p k c", k=8)
        Bwv = Buf("wv")
        S.claim([Bwv], [Bwu])
        S.dma("pool", wv, wview(w_in)[:, :, 1024:1536], writes=[Bwv])
        nab = reg(E_OFF, 2 * NA_NCH * 128).rearrange("p (c h q) -> p c h q", h=2, c=NA_NCH)
        Bnab = Buf("nab")
        oT = reg(D_OFF, 8192).rearrange("p (k t) -> p k t", k=4)
        BoT = [[Buf("oT%d_%d" % (k, t)) for t in range(16)] for k in range(4)]
        omT = reg(D_OFF + 8192, 8192).rearrange("p (k t) -> p k t", k=4)
        BomT = [[Buf("omT%d_%d" % (k, t)) for t in range(4)] for k in range(4)]
        S.claim([b for r in BoT for b in r], [Btab[0]])
        S.claim([b for r in BomT for b in r], [Btab[1]])
        wf = reg(D_OFF + 8192, 4096).rearrange("p (g c) -> p g c", g=4)
        Bwf = Buf("wf")
        S.claim([Bwf], [Btab[1]])
        S.dma("pool", wf, w_f.rearrange("(g p) c -> p g c", p=128), writes=[Bwf])

        S.op("dve", lambda e: e.memset(Va[:, :, :, 64:65], 1.0), reads=[], writes=BVa)
        S.op("dve", lambda e: e.memset(reg(C_OFF + 10400, 4096), 0.0), reads=[], writes=BqT)
        for f in range(NFR):
            p, pB = next_ps()
            for k in range(8):
                S.op("pe", lambda e, k=k, f=f, p=p: e.matmul(p[:], lhsT=hTf[:, k, f * 128:(f + 1) * 128], rhs=wv[:, k, :], start=(k == 0), stop=(k == 7)),
                     reads=[BhTf[f], Bwv], writes=[pB], inc=(k == 7))
            S.op("act", lambda e, f=f, p=p: e.activation(out=Va[:, f, :, 0:64], in_=p[:].rearrange("p (h d) -> p h d", h=8), func=AF.Copy),
                 reads=[pB], writes=[BVa[f]])

        wsl = [reg(G_OFF + i * 1024, 1024).rearrange("p (k c) -> p k c", k=8) for i in range(4)]
        Bwsl = [Buf("wsl%d" % i) for i in range(4)]
        S.claim(Bwsl, [Bwv])

        H0, H1 = slice(0, 64), slice(64, 128)
        for hp in range(4):
            s = hp % 2
            wq, wk = wsl[2 * s], wsl[2 * s + 1]
            BwqB, BwkB = Bwsl[2 * s], Bwsl[2 * s + 1]
            S.dma("pool", wq, wview(w_in)[:, :, hp * 128:(hp + 1) * 128], writes=[BwqB])
            S.dma("pool", wk, wview(w_in)[:, :, 512 + hp * 128:512 + (hp + 1) * 128], writes=[BwkB])
            S.dma("pool", nab.rearrange("p c h q -> p (c h) q"), nab_d[hp], writes=[Bnab])
            qitems = []
            for tb in range(4):
                qitems.append(dict(w=wq, wB=BwqB, rhs_fn=(lambda k, tb=tb: hTf[:, k, 256 + tb * 512:256 + (tb + 1) * 512]),
                                   rhsBs=BhTf[2 + 4 * tb:6 + 4 * tb], mat=blk64, inv_d=1.0 / 64, gain=vec2[:, 0:1], gainB=Bvec2, outB=BqT[tb],
                                   split=[(H0, qzz[H0, 4 * tb:4 * tb + 4, 0, :]), (H1, qzz[H1, 4 * tb:4 * tb + 4, 1, :])]))
            kitems = []
            for fb in range(5):
                kitems.append(dict(w=wk, wB=BwkB, rhs_fn=(lambda k, fb=fb: hTf[:, k, fb * 512:(fb + 1) * 512]),
                                   rhsBs=BhTf[4 * fb:4 * fb + 4], mat=blk64, inv_d=1.0 / 64, gain=vec[:, V_KG:V_KG + 1], gainB=Bvec,
                                   out=kT[:, fb * 512:(fb + 1) * 512], outB=BkT[fb]))
            proj_norm_batch(kitems[0:3])
            proj_norm_batch(kitems[3:5] + qitems[0:1])
            proj_norm_batch(qitems[1:4])

            def emitS(tl):
                b0, nb_, voff = na_band(tl)
                par = tl % 2
                for j in range(nb_):
                    bk = 3 * par + j // 2
                    pp, ppB = ps[bk], Bps[bk]
                    col = (j % 2) * 256
                    f = b0 + j
                    last = (j % 2 == 1) or (j == nb_ - 1)
                    S.op("pe", lambda e, pp=pp, col=col, f=f, tl=tl: e.matmul(
                        pp[:, col:col + 256], lhsT=kT[:, f * 128:(f + 1) * 128], rhs=qzz[:, tl].rearrange("p h q -> p (h q)"), start=True, stop=False),
                        reads=[BkT[f // 4], BqT[tl // 4]], writes=[ppB], inc=False)
                    S.op("pe", lambda e, pp=pp, col=col, j=j, voff=voff: e.matmul(
                        pp[:, col:col + 256], lhsT=ident, rhs=nab[:, voff + j].rearrange("p h q -> p (h q)"), start=False, stop=True),
                        reads=[Bcst, Bnab], writes=[ppB], inc=last)
                Pt, PtB = PT[par], BPT[par]
                for bi in range((nb_ + 1) // 2):
                    bk = 3 * par + bi
                    ncol = min(512, (nb_ - 2 * bi) * 256)
                    S.op("act", lambda e, Pt=Pt, bk=bk, bi=bi, ncol=ncol: e.activation(out=Pt[:, bi * 512:bi * 512 + ncol], in_=ps[bk][:, 0:ncol], func=AF.Exp),
                         reads=[Bps[bk]], writes=[PtB])

            def emitPV(tl):
                b0, nb_, voff = na_band(tl)
                par = tl % 2
                po, poB = ps[6 + par], Bps[6 + par]
                Pt, PtB = PT[par], BPT[par]
                for hh in range(2):
                    h = 2 * hp + hh
                    for j in range(nb_):
                        f = b0 + j
                        S.op("pe", lambda e, po=po, hh=hh, j=j, f=f, h=h, Pt=Pt, nb_=nb_: e.matmul(
                            po[:, hh * 65:(hh + 1) * 65], lhsT=Pt[:, j * 256 + hh * 128:j * 256 + (hh + 1) * 128], rhs=Va[:, f, h, :],
                            start=(j == 0), stop=(j == nb_ - 1)),
                            reads=[PtB, BVa[f]], writes=[poB], inc=(j == nb_ - 1))

            def emitFin(tl, hp=hp):
                par = tl % 2
                po, poB = ps[6 + par], Bps[6 + par]
                pov = po[:, 0:130].rearrange("p (h e) -> p h e", h=2)
                S.op("dve", lambda e: e.reciprocal(out=rc[:, 2 * par:2 * par + 2], in_=pov[:, :, 64]), reads=[poB], writes=[Brc[par]])
                S.op("dve", lambda e: e.tensor_tensor(
                    out=onb[par][:].rearrange("p (h d) -> p h d", h=2), in0=pov[:, :, 0:64],
                    in1=rc[:, 2 * par:2 * par + 2].unsqueeze(2).to_broadcast([128, 2, 64]), op=ALU.mult),
                    reads=[poB, Brc[par]], writes=[Bonb[par]])
                ptb = po[:].bitcast(BF16)[:, 512:640]
                S.op("pe", lambda e: e.transpose(out=ptb, in_=onb[par][:], identity=ident), reads=[Bonb[par], Bcst], writes=[poB])
                S.op("dve", lambda e: e.tensor_copy(out=oT[:, hp, tl * 128:(tl + 1) * 128], in_=ptb), reads=[poB], writes=[BoT[hp][tl]])

            for i in range(16 + 2):
                if i < 16:
                    emitS(i)
                if 1 <= i <= 16:
                    emitPV(i - 1)
                if i >= 2:
                    emitFin(i - 2)

        if stop <= 4:
            if dbg:
                dump(oT[:, 0, :], BoT[0], 2048)
                dump(oT[:, 3, :], BoT[3], 2048)
            finish()
            return nc

        Wf2 = reg(E_OFF, 8192).rearrange("p (a g c) -> p a g c", a=2, g=4)
        BWf2 = Buf("Wf2")
        S.claim([BWf2], [Bnab])
        for a, M in enumerate([CcM, ScM]):
            for g in range(4):
                for half in range(2):
                    p, pB = next_ps()
                    S.op("pe", lambda e, p=p, M=M, g=g, half=half: e.matmul(p[:], lhsT=M, rhs=wf[:, g, half * 512:(half + 1) * 512], start=True, stop=True),
                         reads=[Bcst, Bwf], writes=[pB])
                    S.op("dve", lambda e, p=p, a=a, g=g, half=half: e.tensor_copy(out=Wf2[:, a, g, half * 512:(half + 1) * 512], in_=p[:]),
                         reads=[pB], writes=[BWf2])
        S.claim([b for r in BomT for b in r], [Bwf])
        f1s = [reg(G_OFF, 4096), reg(C_OFF + 16384, 4096)]
        Bf1 = [Buf("f1s0"), Buf("f1s1")]
        S.claim([Bf1[1]], BVa + BqT + BkT)

        def load_f1(dc):
            s_ = (dc + 1) % 2
            base = f1s[s_]
            gw = base[:, 0:3072].rearrange("p (b k c) -> p b k c", b=3, k=8)
            wo = base[:, 3072:4096].rearrange("p (b k c) -> p b k c", b=2, k=4)
            for br in range(3):
                S.dma("pool", gw[:, br], wview(w_in)[:, :, 2560 + br * 1024 + dc * 128:2560 + br * 1024 + (dc + 1) * 128],
                      writes=[Bf1[s_]] if br == 0 else [], reads=[], key=Bf1[s_])
            S.dma("pool", wo[:, 0], w_na_o.rearrange("(k p) c -> p k c", p=128)[:, :, dc * 128:(dc + 1) * 128], key=Bf1[s_])
            tokw = S.dma("pool", wo[:, 1], w_mem_o.rearrange("(k p) c -> p k c", p=128)[:, :, dc * 128:(dc + 1) * 128], key=Bf1[s_])
            Bf1[s_].lw = tokw

        load_f1(0)

        mqs = [[reg(C_OFF + (hs * 4 + tb) * 512, 512) for tb in range(4)] for hs in range(2)]
        Bmqs = [[Buf("mq%d_%d" % (hs, tb)) for tb in range(4)] for hs in range(2)]
        S.claim([b for r in Bmqs for b in r], BVa + BqT + BkT)

        def emitDPN(h):
            wmq, BwmqB = wsl[h % 4], Bwsl[h % 4]
            S.dma("pool", wmq, wview(w_in)[:, :, 2048 + h * 128:2048 + (h + 1) * 128], writes=[BwmqB])
            items = []
            for tb in range(4):
                items.append(dict(w=wmq, wB=BwmqB, rhs_fn=(lambda k, tb=tb: hTf[:, k, 256 + tb * 512:256 + (tb + 1) * 512]),
                                  rhsBs=BhTf[2 + 4 * tb:6 + 4 * tb], mat=ones, inv_d=1.0 / 128, gain=vec2[:, 1:2], gainB=Bvec2,
                                  out=mqs[h % 2][tb], outB=Bmqs[h % 2][tb]))
            proj_norm_batch(items)

        def emitDattn(h, tb):
            mq, mqB = mqs[h % 2][tb], Bmqs[h % 2][tb]
            pts = []
            for c in range(2):
                pS, pSB = next_ps()
                S.op("pe", lambda e, pS=pS, c=c: e.matmul(pS[:], lhsT=kmT[:, h, c * 128:(c + 1) * 128], rhs=mq, start=True, stop=True),
                     reads=[BkmT, mqB], writes=[pSB])
                pt_, ptB_ = next_tb()
                S.op("act", lambda e, pS=pS, pt_=pt_: e.activation(out=pt_[:], in_=pS[:], func=AF.Exp), reads=[pSB], writes=[ptB_])
                pts.append((pt_, ptB_))
            po, poB = next_ps()
            pq, pqB = next_ps()
            for c in range(2):
                S.op("pe", lambda e, c=c: e.matmul(po[:], lhsT=vm[:, c, h * 128:(h + 1) * 128], rhs=pts[c][0][:], start=(c == 0), stop=(c == 1)),
                     reads=[Bvm, pts[c][1]], writes=[poB], inc=(c == 1))
            for c in range(2):
                S.op("pe", lambda e, c=c: e.matmul(pq[:], lhsT=ones, rhs=pts[c][0][:], start=(c == 0), stop=(c == 1)),
                     reads=[Bcst, pts[c][1]], writes=[pqB], inc=(c == 1))
            t, tB = next_tf()
            if True:
                S.op("act", lambda e: e.activation(out=t[:], in_=pq[:], func=AF.Ln), reads=[pqB], writes=[tB])
                S.op("act", lambda e: e.activation(out=t[:], in_=t[:], func=AF.Exp, scale=-1.0), reads=[tB], writes=[tB])
            else:
                S.op("dve", lambda e: e.reciprocal(out=t[:], in_=pq[:]), reads=[pqB], writes=[tB])
            S.op("dve", lambda e: e.tensor_tensor(out=omT[:, h, tb * 512:(tb + 1) * 512], in0=po[:], in1=t[:], op=ALU.mult),
                 reads=[poB, tB], writes=[BomT[h][tb]])

        emitDPN(0)
        for h in range(4):
            if h + 1 < 4:
                emitDPN(h + 1)
            for tb in range(4):
                emitDattn(h, tb)

        if stop <= 5:
            if dbg:
                dump(omT[:, 0, :], BomT[0], 2048)
                dump(omT[:, 3, :], BomT[3], 2048)
            finish()
            return nc

        mT = reg(C_OFF, 16384).rearrange("p (k t) -> p k t", k=8)
        BmT = [[Buf("mT%d_%d" % (k, t)) for t in range(4)] for k in range(8)]
        ncbufs = BVa + BqT + BkT + [b for r in Bmqs for b in r]
        S.claim([b for r in BmT for b in r], ncbufs)
        S.claim([Bf1[0]], [Bwv] + Bwsl)
        for dc in range(8):
            s = (dc + 1) % 2
            base = f1s[s]
            gw = base[:, 0:3072].rearrange("p (b k c) -> p b k c", b=3, k=8)
            wo = base[:, 3072:4096].rearrange("p (b k c) -> p b k c", b=2, k=4)
            if dc + 1 < 8:
                load_f1(dc + 1)
            for tb in range(4):
                tsl = slice(tb * 512, (tb + 1) * 512)
                acc = None
                for br in range(3):
                    pg, pgB = next_ps()
                    for k in range(8):
                        S.op("pe", lambda e, pg=pg, gw=gw, br=br, k=k, tb=tb: e.matmul(
                            pg[:], lhsT=gw[:, br, k, :], rhs=hTf[:, k, 256 + tb * 512:256 + (tb + 1) * 512], start=(k == 0), stop=(k == 7)),
                            reads=[Bf1[s]] + BhTf[2 + 4 * tb:6 + 4 * tb], writes=[pgB], inc=(k == 7))
                    py, pyB = next_ps()
                    if br == 0:
                        for k in range(4):
                            S.op("pe", lambda e, py=py, wo=wo, k=k, tsl=tsl: e.matmul(py[:], lhsT=wo[:, 0, k, :], rhs=oT[:, k, tsl], start=(k == 0), stop=(k == 3)),
                                 reads=[Bf1[s]] + BoT[k][4 * tb:4 * tb + 4], writes=[pyB], inc=(k == 3))
                    elif br == 1:
                        j = 0
                        for a in range(2):
                            for g in range(4):
                                S.op("pe", lambda e, py=py, a=a, g=g, dc=dc, tsl=tsl, j=j: e.matmul(
                                    py[:], lhsT=Wf2[:, a, g, dc * 128:(dc + 1) * 128], rhs=Zt[:, a, g, tsl], start=(j == 0), stop=(j == 7)),
                                    reads=[BWf2] + BZ[a][g], writes=[pyB], inc=(j == 7))
                                j += 1
                    else:
                        for k in range(4):
                            S.op("pe", lambda e, py=py, wo=wo, k=k, tsl=tsl: e.matmul(py[:], lhsT=wo[:, 1, k, :], rhs=omT[:, k, tsl], start=(k == 0), stop=(k == 3)),
                                 reads=[Bf1[s], BomT[k][tb]], writes=[pyB], inc=(k == 3))
                    sg, sgB = next_tf()
                    bcol = V_BG + br * 8 + dc
                    S.op("act", lambda e, sg=sg, pg=pg, bcol=bcol: e.activation(out=sg[:], in_=pg[:], func=AF.Sigmoid, bias=vec[:, bcol:bcol + 1]),
                         reads=[pgB, Bvec], writes=[sgB])
                    if br == 0:
                        S.op("dve", lambda e, sg=sg, py=py: e.tensor_tensor(out=sg[:], in0=sg[:], in1=py[:], op=ALU.mult), reads=[sgB, pyB], writes=[sgB])
                        acc, accB = sg, sgB
                    elif br == 1:
                        S.op("dve", lambda e, sg=sg, py=py: e.tensor_tensor(out=sg[:], in0=sg[:], in1=py[:], op=ALU.mult), reads=[sgB, pyB], writes=[sgB])
                        S.op("dve", lambda e, sg=sg, acc=acc: e.tensor_tensor(out=acc[:], in0=acc[:], in1=sg[:], op=ALU.add), reads=[sgB, accB], writes=[accB])
                    else:
                        S.op("dve", lambda e, sg=sg, py=py: e.tensor_tensor(out=sg[:], in0=sg[:], in1=py[:], op=ALU.mult), reads=[sgB, pyB], writes=[sgB])
                        S.op("dve", lambda e, sg=sg, acc=acc, dc=dc, tsl=tsl: e.tensor_tensor(out=mT[:, dc, tsl], in0=acc[:], in1=sg[:], op=ALU.add),
                             reads=[sgB, accB], writes=[BmT[dc][tb]])

        if stop <= 6:
            if dbg:
                dump(mT[:, 0, :], BmT[0], 2048)
                dump(mT[:, 7, :], BmT[7], 2048)
            finish()
            return nc

        wout = reg(C_OFF + 16384, 8192).rearrange("p (k c) -> p k c", k=8)
        Bwout = Buf("wout")
        S.claim([Bwout], [Bf1[1], Bf1[0]])
        S.dma("pool", wout, wview(w_out), writes=[Bwout])
        x1 = reg(BZ_OFF, BZ_SZ + D_SZ).bitcast(F32).rearrange("p (t c) -> p t c", t=16)
        Bx1 = [[Buf("x1_%d_%d" % (t, hf)) for hf in range(2)] for t in range(16)]
        oldz = [b for r in BZ for gg in r for b in gg] + [b for r in BoT for b in r] + [b for r in BomT for b in r]
        S.claim([b for r in Bx1 for b in r], oldz)
        h2T = reg(A_OFF, 16384).rearrange("p (k t) -> p k t", k=8)
        Bh2 = [Buf("h2T%d" % t) for t in range(16)]
        S.claim(Bh2, BhTf)
        S.dma("sp", gbc[:], gbc_d[1], writes=[Bgbc])
        pend2 = {}
        NXF = 4
        xtf = [reg(E_OFF + i * 2048, 2048).bitcast(F32) for i in range(NXF)]
        Bxtf = [Buf("xtf%d" % i) for i in range(NXF)]
        S.claim(Bxtf, [BWf2])
        ctr["xtf"] = 0

        def emitX1(t):
            xi = nxt("xtf", NXF)
            S.dma("sp", xtf[xi], xr[t * 128:(t + 1) * 128, :], writes=[Bxtf[xi]])
            for hf in range(2):
                p, pB = next_ps()
                for k in range(8):
                    S.op("pe", lambda e, p=p, k=k, hf=hf: e.matmul(p[:], lhsT=mT[:, k, t * 128:(t + 1) * 128], rhs=wout[:, k, hf * 512:(hf + 1) * 512],
                                                                start=(k == 0), stop=(k == 7)),
                         reads=[BmT[k][t // 4], Bwout], writes=[pB], inc=(k == 7))
                S.op("dve", lambda e, p=p, hf=hf, xi=xi: e.tensor_tensor(out=x1[:, t, hf * 512:(hf + 1) * 512], in0=p[:], in1=xtf[xi][:, hf * 512:(hf + 1) * 512], op=ALU.add),
                     reads=[pB, Bxtf[xi]], writes=[Bx1[t][hf]])
            pend2[t] = tok_norm_a(x1[:, t, :], Bx1[t])

        def emitT2(t):
            tok_norm_b(pend2.pop(t), V_G2, h2T[:, :, t * 128:(t + 1) * 128], Bh2[t])

        emitX1(0)
        for t in range(16):
            if t + 1 < 16:
                emitX1(t + 1)
            emitT2(t)

        if stop <= 7:
            if dbg:
                dump(x1[:, 0, :], Bx1[0], 1024)
                dump(h2T[:, 0, :], Bh2, 2048)
            finish()
            return nc

        w1s = [reg(C_OFF + s * 8192, 4096).rearrange("p (k c) -> p k c", k=8) for s in range(2)]
        w2s = [reg(C_OFF + s * 8192 + 4096, 4096).rearrange("p (k c) -> p k c", k=4) for s in range(2)]
        Bw1 = [Buf("w1_%d" % s) for s in range(2)]
        Bw2 = [Buf("w2_%d" % s) for s in range(2)]
        S.claim(Bw1 + Bw2, [b for r in BmT for b in r])
        aTs = [reg(C_OFF + 16384, 8192).rearrange("p (k t) -> p k t", k=4), reg(E_OFF, 8192).rearrange("p (k t) -> p k t", k=4)]
        BaT = [[[Buf("aT%d_%d_%d" % (s, k, t)) for t in range(4)] for k in range(4)] for s in range(2)]
        S.claim([b for r in BaT[0] for b in r], [Bwout])
        S.claim([b for r in BaT[1] for b in r], [BWf2] + Bxtf)
        NG = 8
        for grp in range(NG):
            s = grp % 2
            S.dma("pool", w1s[s], wview(w_ff1)[:, :, grp * 512:(grp + 1) * 512], writes=[Bw1[s]])
            S.dma("pool", w2s[s], w_ff2[grp * 512:(grp + 1) * 512, :].rearrange("(k p) c -> p k c", p=128), writes=[Bw2[s]])
            aT = aTs[s]
            for fc in range(4):
                for tb in range(4):
                    p, pB = next_ps()
                    for k in range(8):
                        S.op("pe", lambda e, p=p, k=k, fc=fc, tb=tb, s=s: e.matmul(p[:], lhsT=w1s[s][:, k, fc * 128:(fc + 1) * 128],
                                                                                 rhs=h2T[:, k, tb * 512:(tb + 1) * 512], start=(k == 0), stop=(k == 7)),
                             reads=[Bw1[s]] + Bh2[4 * tb:4 * tb + 4], writes=[pB], inc=(k == 7))
                    t, tB = next_tf()
                    S.op("act", lambda e, t=t, p=p: e.activation(out=t[:], in_=p[:], func=AF.Relu), reads=[pB], writes=[tB])
                    S.op("act", lambda e, t=t, aT=aT, fc=fc, tb=tb: e.activation(out=aT[:, fc, tb * 512:(tb + 1) * 512], in_=t[:], func=AF.Square),
                         reads=[tB], writes=[BaT[s][fc][tb]])
            for t in range(16):
                for hf in range(2):
                    p, pB = next_ps()
                    for fc in range(4):
                        S.op("pe", lambda e, p=p, fc=fc, t=t, hf=hf, s=s, aT=aT: e.matmul(p[:], lhsT=aT[:, fc, t * 128:(t + 1) * 128],
                                                                                      rhs=w2s[s][:, fc, hf * 512:(hf + 1) * 512], start=(fc == 0), stop=(fc == 3)),
                             reads=[BaT[s][fc][t // 4], Bw2[s]], writes=[pB], inc=(fc == 3))
                    S.op("dve", lambda e, p=p, t=t, hf=hf: e.tensor_tensor(out=x1[:, t, hf * 512:(hf + 1) * 512], in0=p[:], in1=x1[:, t, hf * 512:(hf + 1) * 512], op=ALU.add),
                         reads=[pB, Bx1[t][hf]], writes=[Bx1[t][hf]])
                if grp == NG - 1:
                    S.dma("sp", y[t * 128:(t + 1) * 128, :], x1[:, t, :], reads=Bx1[t], writes=[By[t]])
        finish()
    return nc


def _na_table(rpb, hf):
    rpb = np.asarray(rpb, np.float32)
    tab = np.full((8, NA_NCH, 128, 128), NEG, np.float32)
    kr2 = np.arange(128) // 64
    kc = np.arange(128) % 64
    qr2 = np.arange(128) // 64
    qc = np.arange(128) % 64
    cs = np.clip(qc - 8, 0, 48)
    tls = [0, 1, 2, 14, 15]
    for vi, tl in enumerate(tls):
        b0, nb_, voff = na_band(tl)
        t = 16 * hf + tl
        qrow = 2 * t + qr2
        rs = np.clip(qrow - 4, 0, 56)
        for j in range(nb_):
            f = b0 + j
            lt = (f - 2) % 32
            real = (2 <= f < 18) or (f < 2 and hf == 1) or (f >= 18 and hf == 0)
            if not real:
                continue
            gt = (lt + 16 * hf) % 32
            krow = 2 * gt + kr2
            dr = krow[:, None] - qrow[None, :]
            rowok = (krow[:, None] >= rs[None, :]) & (krow[:, None] < rs[None, :] + 8)
            colok = (kc[:, None] >= cs[None, :]) & (kc[:, None] < cs[None, :] + 16)
            ok = rowok & colok
            dri = np.clip(dr + 7, 0, 14)
            dci = np.clip(kc[:, None] - qc[None, :], -15, 15) + 15
            g = rpb[:, dri, dci]
            tab[:, voff + j] = np.where(ok[None], g, NEG)
    tab = tab.reshape(4, 2, NA_NCH, 128, 128).transpose(0, 3, 2, 1, 4).reshape(4, 128, 2 * NA_NCH, 128)
    return np.ascontiguousarray(tab)


def _fft_tables(hf):
    m = np.arange(1024, dtype=np.int64)
    out = np.zeros((4, 2, 1024, 512), np.float64)
    w = np.arange(512, dtype=np.int64)
    for v in range(4):
        sp = 2048 * hf + 4 * w + v
        ph = (m[:, None] * sp[None, :]) % 4096
        ang = 2.0 * np.pi * ph / 4096.0
        sign = -1.0 if (hf == 1 and v % 2 == 1) else 1.0
        out[v, 0] = sign * np.cos(ang)
        out[v, 1] = -sign * np.sin(ang)
    out = out.reshape(4, 2, 8, 128, 512).transpose(0, 3, 1, 2, 4)
    return np.ascontiguousarray(out.astype(np.float32)).astype(ml_dtypes.bfloat16)


def _consts():
    c = np.zeros((128, 5, 128), np.float64)
    c[:, 0] = np.eye(128)
    blk = np.zeros((128, 128))
    blk[:64, :64] = 1.0
    blk[64:, 64:] = 1.0
    c[:, 1] = blk
    c[:, 2] = 1.0
    i = np.arange(128)
    ang = 2.0 * np.pi * ((i[:, None] * i[None, :]) % 128) / 128.0
    nrm = 1.0 / np.sqrt(4096.0 * 128.0)
    c[:, 3] = np.cos(ang) * nrm
    c[:, 4] = np.sin(ang) * nrm
    return c.astype(np.float32).astype(ml_dtypes.bfloat16)


def _vecs(norm1_g, norm2_g, mem_norm_g, b_gate, na_q_g, na_k_g, mem_q_g, mem_k_g):
    v = np.zeros((128, NVEC), np.float32)
    v[:, V_G1:V_G1 + 8] = np.asarray(norm1_g, np.float32).reshape(8, 128).T
    v[:, V_G2:V_G2 + 8] = np.asarray(norm2_g, np.float32).reshape(8, 128).T
    v[:, V_GM:V_GM + 8] = np.asarray(mem_norm_g, np.float32).reshape(8, 128).T
    v[:, V_BG:V_BG + 24] = np.asarray(b_gate, np.float32).reshape(24, 128).T
    v[:, V_QG] = np.tile(np.asarray(na_q_g, np.float32), 2)
    v[:, V_KG] = np.tile(np.asarray(na_k_g, np.float32), 2)
    v[:, V_MQG] = np.asarray(mem_q_g, np.float32)
    v[:, V_MKG] = np.asarray(mem_k_g, np.float32)
    return v


def make_in_maps(x, mem, norm1_g, w_in, b_gate, na_q_g, na_k_g, na_rpb, w_na_o, w_f,
                 mem_norm_g, w_mem_kv, mem_q_g, mem_k_g, w_mem_o, w_out, norm2_g, w_ff1, w_ff2):
    f = lambda a: np.ascontiguousarray(np.asarray(a, np.float32))
    x = f(x)
    mem = f(mem)
    shared = {
        "w_in": f(w_in), "w_na_o": f(w_na_o), "w_f": f(w_f), "w_mem_o": f(w_mem_o), "w_mem_kv": f(w_mem_kv),
        "w_out": f(w_out), "w_ff1": f(w_ff1), "w_ff2": f(w_ff2),
        "vecs": _vecs(norm1_g, norm2_g, mem_norm_g, b_gate, na_q_g, na_k_g, mem_q_g, mem_k_g),
        "consts": _consts(),
        "gbc": np.ascontiguousarray(np.stack([np.broadcast_to(np.asarray(g, np.float32)[None, :], (128, D))
                                              for g in (norm1_g, norm2_g, mem_norm_g)])),
    }
    nat = [_na_table(na_rpb, hf) for hf in range(2)]
    fft = [_fft_tables(hf) for hf in range(2)]
    maps = []
    for c in range(8):
        b, hf = c // 2, c % 2
        m = dict(shared)
        m["xr"] = np.ascontiguousarray(np.roll(x[b], -NOWN * hf, axis=0))
        m["mem"] = mem[b]
        m["nabias"] = nat[hf]
        m["fftab"] = fft[hf]
        maps.append(m)
    return maps


_NC_CACHE = {}


def kernel(**inputs):
    maps = make_in_maps(**inputs)
    if "nc" not in _NC_CACHE:
        _NC_CACHE["nc"] = build()
    nc = _NC_CACHE["nc"]
    res = run_bass_kernel_spmd(nc, maps, core_ids=list(range(8)))
    out = np.zeros((4, SEQ, D), np.float32)
    for c in range(8):
        b, hf = c // 2, c % 2
        out[b, NOWN * hf:NOWN * (hf + 1)] = res.results[c]["y"]
    return out
```

```python
from contextlib import ExitStack

import numpy as np
import ml_dtypes

import concourse.bass as bass
import concourse.mybir as mybir
from concourse.bass_utils import run_bass_kernel_spmd

F32 = mybir.dt.float32
BF16 = mybir.dt.bfloat16
AF = mybir.ActivationFunctionType
ALU = mybir.AluOpType

SAME_ENG_SYNC = True
EPS = 1e-6
NEG = -1e30


class Buf:
    __slots__ = ("name", "lw", "rds", "dsem", "dcnt")

    def __init__(self, name):
        self.name = name
        self.lw = None
        self.rds = {}
        self.dsem = None
        self.dcnt = 0


class EngQ:
    def __init__(self, name, sem, is_pe=False):
        self.name = name
        self.sem = sem
        self.n = 0
        self.waited = {}
        self.ops = []
        self.is_pe = is_pe


class Sched:
    def __init__(self, nc, stack):
        self.nc = nc
        self.stack = stack
        self.q = {}
        for name in ("pe", "act", "dve", "pool", "sp"):
            sem = stack.enter_context(nc.semaphore("q_" + name))
            self.q[name] = EngQ(name, sem, is_pe=(name == "pe"))
        self.nsem = 5
        self.free_sems = []

    def newsem(self, name):
        self.nsem += 1
        return self.stack.enter_context(self.nc.semaphore(name))

    def _collect(self, q, reads, writes):
        deps = {}

        def add(t):
            if t is None:
                return
            sem, val = t
            if sem is q.sem and (q.is_pe or not SAME_ENG_SYNC):
                return
            k = id(sem)
            if q.waited.get(k, 0) >= val:
                return
            if k not in deps or deps[k][1] < val:
                deps[k] = (sem, val)

        for b in reads:
            add(b.lw)
        for b in writes:
            add(b.lw)
            for t in b.rds.values():
                add(t)
        out = list(deps.values())
        for sem, val in out:
            q.waited[id(sem)] = val
        return out

    def _update(self, tok, reads, writes):
        for b in writes:
            b.lw = tok
            b.rds = {}
        k = id(tok[0])
        for b in reads:
            if k not in b.rds or b.rds[k][1] < tok[1]:
                b.rds[k] = tok

    def op(self, qn, fn, reads=(), writes=(), inc=True):
        q = self.q[qn]
        waits = self._collect(q, reads, writes)
        tok = (q.sem, q.n + 1)
        if inc:
            q.n += 1
        else:
            assert q.is_pe
        q.ops.append((waits, fn, (q.sem, 1) if inc else None))
        self._update(tok, reads, writes)

    def dma(self, qn, out, in_, reads=(), writes=(), key=None, **kw):
        q = self.q[qn]
        waits = self._collect(q, reads, writes)
        kb = key if key is not None else (writes[0] if writes else reads[0])
        if kb.dsem is None:
            kb.dsem = self.newsem("d_" + kb.name)
        kb.dcnt += 16
        tok = (kb.dsem, kb.dcnt)

        def fn(e, out=out, in_=in_, kw=kw):
            return e.dma_start(out=out, in_=in_, **kw)

        q.ops.append((waits, fn, (kb.dsem, 16)))
        self._update(tok, reads, writes)
        return tok

    def claim(self, new_bufs, old_bufs):
        toks = {}
        for b in old_bufs:
            for t in ([b.lw] if b.lw else []) + list(b.rds.values()):
                k = id(t[0])
                if k not in toks or toks[k][1] < t[1]:
                    toks[k] = t
        for nb in new_bufs:
            for k, t in toks.items():
                if k not in nb.rds or nb.rds[k][1] < t[1]:
                    nb.rds[k] = t

    def wait_all(self, qn, bufs):
        q = self.q[qn]
        waits = self._collect(q, (), bufs)
        q.ops.append((waits, None, None))

    def emit(self):
        nc = self.nc
        qs = self.q

        def run(q, e):
            for waits, fn, inc in q.ops:
                for sem, val in waits:
                    e.wait_ge(sem, val)
                if fn is None:
                    continue
                ins = fn(e)
                if inc is not None:
                    ins.then_inc(inc[0], inc[1])

        with nc.Block() as block:
            @block.tensor
            def _(e):
                run(qs["pe"], e)

            @block.scalar
            def _(e):
                run(qs["act"], e)

            @block.vector
            def _(e):
                run(qs["dve"], e)

            @block.gpsimd
            def _(e):
                run(qs["pool"], e)

            @block.sync
            def _(e):
                run(qs["sp"], e)


D = 1024
SEQ = 4096
NOWN = 2048
NFR = 20
NA_VARIANTS = [(0, 6), (1, 5), (2, 5), (14, 5), (14, 6)]
NA_VOFF = [0, 6, 11, 16, 21]
NA_NCH = 27


def na_band(tl):
    if tl == 0:
        return 0, 6, NA_VOFF[0]
    if tl == 1:
        return 1, 5, NA_VOFF[1]
    if tl == 14:
        return 14, 5, NA_VOFF[3]
    if tl == 15:
        return 14, 6, NA_VOFF[4]
    return tl, 5, NA_VOFF[2]


V_G1, V_G2, V_GM, V_BG, V_QG, V_KG, V_MQG, V_MKG = 0, 8, 16, 24, 48, 49, 50, 51
NVEC = 64

A_OFF, A_SZ = 0, 20480
BZ_OFF, BZ_SZ = 20480, 16384
D_OFF, D_SZ = 36864, 16384
C_OFF, C_SZ = 53248, 20480
G_OFF, G_SZ = 73728, 4096
E_OFF, E_SZ = 77824, 8192
ARENA = 86016


def build(stop=99, dbg=False):
    nc = bass.Bass("TRN2", target_bir_lowering=False)

    def dram(n, s, d, kind="ExternalInput"):
        return nc.dram_tensor(n, s, d, kind=kind).ap()

    xr = dram("xr", [SEQ, D], F32)
    mem = dram("mem", [256, D], F32)
    w_in = dram("w_in", [D, 5632], F32)
    w_na_o = dram("w_na_o", [512, D], F32)
    w_f = dram("w_f", [512, D], F32)
    w_mem_o = dram("w_mem_o", [512, D], F32)
    w_mem_kv = dram("w_mem_kv", [D, D], F32)
    w_out = dram("w_out", [D, D], F32)
    w_ff1 = dram("w_ff1", [D, 4096], F32)
    w_ff2 = dram("w_ff2", [4096, D], F32)
    vecs_d = dram("vecs", [128, NVEC], F32)
    nab_d = dram("nabias", [4, 128, 2 * NA_NCH, 128], F32)
    fft_d = dram("fftab", [4, 128, 2, 8, 512], BF16)
    cst_d = dram("consts", [128, 5, 128], BF16)
    gbc_d = dram("gbc", [3, 128, D], F32)
    y = dram("y", [NOWN, D], F32, kind="ExternalOutput")
    if dbg:
        dbg_d = dram("dbg", [128, 8192], F32, kind="ExternalOutput")

    def wview(w):
        return w.rearrange("(k p) c -> p k c", p=128)

    st = ExitStack()
    with st:
        S = Sched(nc, st)

        def sb(n, s, d):
            return st.enter_context(nc.sbuf_tensor(n, s, d))

        ar = sb("arena", [128, ARENA], BF16)

        def reg(off, n):
            return ar[:, off:off + n]

        cst = sb("cst", [128, 5, 128], BF16)
        vec = sb("vec", [128, NVEC], F32)
        vec2 = sb("vec2", [128, 4], F32)
        stat = sb("stat", [128, 64], F32)
        rstd = sb("rstd", [128, 64], F32)
        gbc = sb("gbcsb", [128, D], F32)
        Bgbc = Buf("gbc")
        NXA = 6
        xt = [reg(D_OFF + i * 2048, 2048).bitcast(F32) for i in range(NXA)]
        xs = [sb("xs%d" % i, [128, D], BF16) for i in range(4)]
        tmpf = [sb("tmpf%d" % i, [128, 512], F32) for i in range(5)]
        tmpb = [sb("tmpb%d" % i, [128, 512], BF16) for i in range(4)]
        PT = [sb("PT%d" % i, [128, 1536], BF16) for i in range(2)]
        kmT = sb("kmT", [128, 4, 256], BF16)
        vm = sb("vm", [128, 2, 512], BF16)
        onb = [sb("onb%d" % i, [128, 128], BF16) for i in range(2)]
        rc = sb("rc", [128, 4], F32)
        ps = [st.enter_context(nc.psum_tensor("ps%d" % i, [128, 512], F32)) for i in range(8)]

        Bcst, Bvec, Bvec2 = Buf("cst"), Buf("vec"), Buf("vec2")
        Bxt = [Buf("xt%d" % i) for i in range(NXA)]
        Bxs = [Buf("xs%d" % i) for i in range(4)]
        Btf = [Buf("tmpf%d" % i) for i in range(5)]
        Btb = [Buf("tmpb%d" % i) for i in range(4)]
        BPT = [Buf("PT%d" % i) for i in range(2)]
        Bps = [Buf("ps%d" % i) for i in range(8)]
        BkmT, Bvm = Buf("kmT"), Buf("vm")
        Bonb = [Buf("onb%d" % i) for i in range(2)]
        Brc = [Buf("rc%d" % i) for i in range(2)]
        Bstat = [Buf("stat%d" % i) for i in range(64)]

        ident = cst[:, 0, :]
        blk64 = cst[:, 1, :]
        ones = cst[:, 2, :]
        CcM = cst[:, 3, :]
        ScM = cst[:, 4, :]

        ctr = {"ps": 0, "tf": 0, "tb": 0, "xt": 0, "xs": 0, "st": 0, "ev": 0, "st2": 0}

        def nxt(key, n):
            i = ctr[key] % n
            ctr[key] += 1
            return i

        def next_ps():
            i = nxt("ps", 8)
            return ps[i], Bps[i]

        def next_tf():
            i = nxt("tf", 5)
            return tmpf[i], Btf[i]

        def next_tb():
            i = nxt("tb", 4)
            return tmpb[i], Btb[i]

        def next_stat():
            i = nxt("st", 32)
            return i, Bstat[i]

        S.dma("sp", cst[:], cst_d, writes=[Bcst])
        S.dma("sp", vec[:], vecs_d, writes=[Bvec])
        S.op("dve", lambda e: e.tensor_scalar(out=vec2[:, 0:1], in0=vec[:, V_QG:V_QG + 1], scalar1=0.125, scalar2=None, op0=ALU.mult),
             reads=[Bvec], writes=[Bvec2])
        S.op("dve", lambda e: e.tensor_scalar(out=vec2[:, 1:2], in0=vec[:, V_MQG:V_MQG + 1], scalar1=float(128 ** -0.5), scalar2=None, op0=ALU.mult),
             reads=[Bvec], writes=[Bvec2])

        junk = PT[0][:, 0:D]

        def tok_norm_a(src_ap, srcBs):
            si, sB = next_stat()
            xi = nxt("xs", 4)
            S.op("act", lambda e: e.activation(out=junk, in_=src_ap, func=AF.Square, accum_out=stat[:, si:si + 1]),
                 reads=srcBs, writes=[sB, BPT[0]])
            S.op("act", lambda e: e.activation(out=stat[:, si:si + 1], in_=stat[:, si:si + 1], func=AF.Sqrt, scale=1.0 / D, bias=EPS),
                 reads=[sB], writes=[sB])
            S.op("dve", lambda e: e.reciprocal(out=rstd[:, si:si + 1], in_=stat[:, si:si + 1]), reads=[sB], writes=[sB])
            S.op("dve", lambda e: e.scalar_tensor_tensor(out=xs[xi][:], in0=src_ap, scalar=rstd[:, si:si + 1], in1=gbc[:],
                                                          op0=ALU.mult, op1=ALU.mult),
                 reads=list(srcBs) + [sB, Bgbc], writes=[Bxs[xi]])
            return xi

        def tok_norm_b(xi, gcol, dst_fn, dstB, bank=None, ev=None):
            p, pB = next_ps() if bank is None else (ps[bank], Bps[bank])
            pb = p[:].bitcast(BF16)
            for k in range(8):
                S.op("pe", lambda e, k=k: e.transpose(out=pb[:, k * 128:(k + 1) * 128], in_=xs[xi][:, k * 128:(k + 1) * 128], identity=ident),
                     reads=[Bxs[xi], Bcst], writes=[pB], inc=(k == 7))
            if ev is None:
                ev = nxt("ev", 2)
            if ev == 0:
                S.op("act", lambda e: e.activation(out=dst_fn, in_=pb.rearrange("p (k t) -> p k t", k=8), func=AF.Copy),
                     reads=[pB], writes=[dstB])
            else:
                S.op("dve", lambda e: e.tensor_copy(out=dst_fn, in_=pb.rearrange("p (k t) -> p k t", k=8)), reads=[pB], writes=[dstB])

        def tok_norm_a2(srcs):
            c = 32 + 2 * nxt("st2", 16)
            sB = Bstat[c]
            for j, (src_ap, srcBs) in enumerate(srcs):
                S.op("act", lambda e, src_ap=src_ap, j=j: e.activation(out=junk, in_=src_ap, func=AF.Square, accum_out=stat[:, c + j:c + j + 1]),
                     reads=srcBs, writes=[sB, BPT[0]])
            S.op("act", lambda e: e.activation(out=stat[:, c:c + 2], in_=stat[:, c:c + 2], func=AF.Sqrt, scale=1.0 / D, bias=EPS),
                 reads=[sB], writes=[sB])
            S.op("dve", lambda e: e.reciprocal(out=rstd[:, c:c + 2], in_=stat[:, c:c + 2]), reads=[sB], writes=[sB])
            xis = []
            for j, (src_ap, srcBs) in enumerate(srcs):
                xi = nxt("xs", 4)
                S.op("dve", lambda e, src_ap=src_ap, j=j, xi=xi: e.scalar_tensor_tensor(out=xs[xi][:], in0=src_ap, scalar=rstd[:, c + j:c + j + 1], in1=gbc[:],
                                                                                     op0=ALU.mult, op1=ALU.mult),
                     reads=list(srcBs) + [sB, Bgbc], writes=[Bxs[xi]])
                xis.append(xi)
            return xis

        def tok_norm_transpose(src_ap, srcB, gcol, dst_fn, dstB):
            xi = tok_norm_a(src_ap, [srcB])
            tok_norm_b(xi, gcol, dst_fn, dstB)

        def fm_norm(p, pB, ncols, mat, inv_d, gain_ap, gainB, out_ap, outB, split=None, split3=False):
            sq, sqB = next_tb()
            S.op("act", lambda e: e.activation(out=sq[:, :ncols], in_=p[:, :ncols], func=AF.Square), reads=[pB], writes=[sqB])
            p2, p2B = next_ps()
            S.op("pe", lambda e: e.matmul(p2[:, :ncols], lhsT=mat, rhs=sq[:, :ncols], start=True, stop=True),
                 reads=[sqB, Bcst], writes=[p2B])
            t, tB = next_tf()
            S.op("act", lambda e: e.activation(out=t[:, :ncols], in_=p2[:, :ncols], func=AF.Ln, scale=inv_d, bias=EPS),
                 reads=[p2B], writes=[tB])
            S.op("act", lambda e: e.activation(out=t[:, :ncols], in_=t[:, :ncols], func=AF.Exp, scale=-0.5), reads=[tB], writes=[tB])
            if split is None:
                S.op("dve", lambda e: e.scalar_tensor_tensor(out=out_ap, in0=p[:, :ncols], scalar=gain_ap, in1=t[:, :ncols],
                                                              op0=ALU.mult, op1=ALU.mult),
                     reads=[pB, tB, gainB], writes=[outB])
            else:
                for (pr, oap) in split:
                    i0 = p[pr, :ncols]
                    i1 = t[pr, :ncols]
                    if split3:
                        i0 = i0.rearrange("p (t q) -> p t q", q=128)
                        i1 = i1.rearrange("p (t q) -> p t q", q=128)
                    S.op("dve", lambda e, pr=pr, oap=oap, i0=i0, i1=i1: e.scalar_tensor_tensor(out=oap, in0=i0, scalar=gain_ap[pr], in1=i1,
                                                                                             op0=ALU.mult, op1=ALU.mult),
                         reads=[pB, tB, gainB], writes=[outB])

        def proj_fm(wt, wB, ncolchunk, rhs_fn, rhsBs, p, pB, ncols):
            for k in range(8):
                S.op("pe", lambda e, k=k: e.matmul(p[:, :ncols], lhsT=wt[:, k, ncolchunk], rhs=rhs_fn(k), start=(k == 0), stop=(k == 7)),
                     reads=[wB] + list(rhsBs), writes=[pB], inc=(k == 7))

        def proj_norm_batch(items):
            n = len(items)
            P = []
            for it in items:
                p, pB = next_ps()
                proj_fm(it["w"], it["wB"], slice(0, 128), it["rhs_fn"], it["rhsBs"], p, pB, 512)
                P.append((p, pB))
            SQ = []
            for it, (p, pB) in zip(items, P):
                sq, sqB = next_tb()
                S.op("act", lambda e, sq=sq, p=p: e.activation(out=sq[:], in_=p[:], func=AF.Square), reads=[pB], writes=[sqB])
                SQ.append((sq, sqB))
            P2 = []
            for it, (sq, sqB) in zip(items, SQ):
                p2, p2B = next_ps()
                S.op("pe", lambda e, p2=p2, sq=sq, it=it: e.matmul(p2[:], lhsT=it["mat"], rhs=sq[:], start=True, stop=True),
                     reads=[sqB, Bcst], writes=[p2B])
                P2.append((p2, p2B))
            T = []
            for it, (p2, p2B) in zip(items, P2):
                t, tB = next_tf()
                S.op("act", lambda e, t=t, p2=p2, it=it: e.activation(out=t[:], in_=p2[:], func=AF.Ln, scale=it["inv_d"], bias=EPS),
                     reads=[p2B], writes=[tB])
                S.op("act", lambda e, t=t: e.activation(out=t[:], in_=t[:], func=AF.Exp, scale=-0.5), reads=[tB], writes=[tB])
                T.append((t, tB))
            for it, (p, pB), (t, tB) in zip(items, P, T):
                if it.get("split") is None:
                    S.op("dve", lambda e, it=it, p=p, t=t: e.scalar_tensor_tensor(out=it["out"], in0=p[:], scalar=it["gain"], in1=t[:],
                                                                                  op0=ALU.mult, op1=ALU.mult),
                         reads=[pB, tB, it["gainB"]], writes=[it["outB"]])
                else:
                    for (pr, oap) in it["split"]:
                        i0 = p[pr, :].rearrange("p (t q) -> p t q", q=128)
                        i1 = t[pr, :].rearrange("p (t q) -> p t q", q=128)
                        S.op("dve", lambda e, it=it, pr=pr, oap=oap, i0=i0, i1=i1: e.scalar_tensor_tensor(
                            out=oap, in0=i0, scalar=it["gain"][pr], in1=i1, op0=ALU.mult, op1=ALU.mult),
                            reads=[pB, tB, it["gainB"]], writes=[it["outB"]])

        dbg_off = [0]

        def dump(ap, B, n):
            for c0 in range(0, n, 512):
                w = min(512, n - c0)
                t, tB = next_tf()
                S.op("act", lambda e, c0=c0, w=w, t=t: e.activation(out=t[:, :w], in_=ap[:, c0:c0 + w], func=AF.Copy), reads=B, writes=[tB])
                o = dbg_off[0]
                S.dma("sp", dbg_d[:, o:o + w], t[:, :w], reads=[tB], writes=[Bdbg])
                dbg_off[0] += w

        Bdbg = Buf("dbg")
        By = [Buf("y%d" % i) for i in range(16)]

        def finish():
            S.wait_all("sp", By + [Bdbg])
            S.emit()

        wkv = reg(C_OFF, 8192).rearrange("p (k c) -> p k c", k=8)
        Bwkv = Buf("wkv")
        S.dma("pool", wkv, wview(w_mem_kv), writes=[Bwkv])
        S.dma("sp", gbc[:], gbc_d[2], writes=[Bgbc])
        memT = reg(BZ_OFF + 12288, 2048).rearrange("p (k t) -> p k t", k=8)
        BmemT = [Buf("memT%d" % i) for i in range(2)]
        for mt in range(2):
            xi = nxt("xt", NXA)
            S.dma("sp", xt[xi], mem[mt * 128:(mt + 1) * 128, :], writes=[Bxt[xi]])
            tok_norm_transpose(xt[xi], Bxt[xi], V_GM, memT[:, :, mt * 128:(mt + 1) * 128], BmemT[mt])
        for h in range(4):
            p, pB = next_ps()
            proj_fm(wkv, Bwkv, slice(h * 128, (h + 1) * 128), lambda k: memT[:, k, :], BmemT, p, pB, 256)
            fm_norm(p, pB, 256, ones, 1.0 / 128, vec[:, V_MKG:V_MKG + 1], Bvec, kmT[:, h, :], BkmT)
        for c in range(2):
            p, pB = next_ps()
            for k in range(8):
                S.op("pe", lambda e, k=k, c=c, p=p: e.matmul(p[:], lhsT=memT[:, k, c * 128:(c + 1) * 128], rhs=wkv[:, k, 512:1024],
                                                          start=(k == 0), stop=(k == 7)),
                     reads=[BmemT[c], Bwkv], writes=[pB], inc=(k == 7))
            S.op("act", lambda e, c=c, p=p: e.activation(out=vm[:, c, :], in_=p[:], func=AF.Copy), reads=[pB], writes=[Bvm])

        hTf = reg(A_OFF, A_SZ).rearrange("p (k t) -> p k t", k=8)
        hTo = reg(BZ_OFF, 12288).rearrange("p (k t) -> p k t", k=8)
        BhT = [Buf("hT%d" % i) for i in range(32)]

        def hT_tile(lt):
            f = (lt + 2) % 32
            if f < NFR:
                return hTf[:, :, f * 128:(f + 1) * 128]
            return hTo[:, :, (lt - 18) * 128:(lt - 17) * 128]

        BhTf = [BhT[(f - 2) % 32] for f in range(NFR)]

        S.dma("sp", gbc[:], gbc_d[0], writes=[Bgbc])
        wu = reg(G_OFF, 4096).rearrange("p (k c) -> p k c", k=8)
        Bwu = Buf("wu")
        S.dma("pool", wu, wview(w_in)[:, :, 1536:2048], writes=[Bwu])

        var = reg(C_OFF, C_SZ).rearrange("p (v i c) -> p v i c", v=5, i=8)
        Bvar = [[Buf("var%d_%d" % (v, i)) for i in range(8)] for v in range(5)]
        S.claim([b for r in Bvar for b in r], [Bwkv])

        order = [8 * e4 + i for i in range(8) for e4 in range(4)]
        pend = {}

        def emitN(n):
            lt = order[n]
            xi = nxt("xt", NXA)
            S.dma("sp", xt[xi], xr[lt * 128:(lt + 1) * 128, :], writes=[Bxt[xi]])
            pend[n] = tok_norm_a(xt[xi], [Bxt[xi]])

        def emitT(n, bank=None):
            lt = order[n]
            tok_norm_b(pend.pop(n), V_G1, hT_tile(lt), BhT[lt], bank=bank)

        def emit_combos(i, U):
            t0, t0B = next_tf()
            t1, t1B = next_tf()
            S.op("act", lambda e: e.activation(out=t0[:], in_=U[0][0][:], func=AF.Copy), reads=[U[0][1]], writes=[t0B])
            S.op("act", lambda e: e.activation(out=t1[:], in_=U[1][0][:], func=AF.Copy), reads=[U[1][1]], writes=[t1B])
            fa, faB = next_tf()
            fb, fbB = next_tf()
            S.op("dve", lambda e: e.tensor_tensor(out=fa[:], in0=t0[:], in1=U[2][0][:], op=ALU.add), reads=[t0B, U[2][1]], writes=[faB])
            S.op("dve", lambda e: e.tensor_tensor(out=var[:, 2, i, :], in0=t0[:], in1=U[2][0][:], op=ALU.subtract),
                 reads=[t0B, U[2][1]], writes=[Bvar[2][i]])
            S.op("dve", lambda e: e.tensor_tensor(out=fb[:], in0=t1[:], in1=U[3][0][:], op=ALU.add), reads=[t1B, U[3][1]], writes=[fbB])
            S.op("dve", lambda e: e.tensor_tensor(out=var[:, 3, i, :], in0=t1[:], in1=U[3][0][:], op=ALU.subtract),
                 reads=[t1B, U[3][1]], writes=[Bvar[3][i]])
            S.op("dve", lambda e: e.tensor_tensor(out=var[:, 0, i, :], in0=fa[:], in1=fb[:], op=ALU.add), reads=[faB, fbB], writes=[Bvar[0][i]])
            S.op("dve", lambda e: e.tensor_tensor(out=var[:, 1, i, :], in0=fa[:], in1=fb[:], op=ALU.subtract), reads=[faB, fbB], writes=[Bvar[1][i]])
            S.op("dve", lambda e: e.tensor_scalar(out=var[:, 4, i, :], in0=var[:, 3, i, :], scalar1=-1.0, scalar2=None, op0=ALU.mult),
                 reads=[Bvar[3][i]], writes=[Bvar[4][i]])

        Upend = []

        def emit_uproj(n):
            i, e4 = n // 4, n % 4
            lt = order[n]
            bk = 4 * (i % 2) + e4
            p, pB = ps[bk], Bps[bk]
            hv = hT_tile(lt)
            for k in range(8):
                S.op("pe", lambda e, k=k, p=p, hv=hv: e.matmul(p[:], lhsT=hv[:, k, :], rhs=wu[:, k, :], start=(k == 0), stop=(k == 7)),
                     reads=[BhT[lt], Bwu], writes=[pB], inc=(k == 7))
            if e4 == 3:
                Upend.append((i, [(ps[4 * (i % 2) + j], Bps[4 * (i % 2) + j]) for j in range(4)]))

        def emitN2(j):
            srcs = []
            for n in (2 * j, 2 * j + 1):
                lt = order[n]
                xi = nxt("xt", NXA)
                S.dma("sp", xt[xi], xr[lt * 128:(lt + 1) * 128, :], writes=[Bxt[xi]])
                srcs.append((xt[xi], [Bxt[xi]]))
            xis = tok_norm_a2(srcs)
            pend[2 * j], pend[2 * j + 1] = xis

        def emitT2p(j):
            for q_, n in enumerate((2 * j, 2 * j + 1)):
                i, e4 = n // 4, n % 4
                lt = order[n]
                tok_norm_b(pend.pop(n), V_G1, hT_tile(lt), BhT[lt], bank=4 * (i % 2) + e4, ev=q_)

        emitN2(0)
        for j in range(16):
            if j + 1 < 16:
                emitN2(j + 1)
            emitT2p(j)
            if stop >= 2:
                if j >= 1:
                    emit_uproj(2 * (j - 1))
                    emit_uproj(2 * (j - 1) + 1)
                if Upend and j >= 2 * Upend[0][0] + 3:
                    emit_combos(*Upend.pop(0))
        if stop >= 2:
            emit_uproj(30)
            emit_uproj(31)
        while Upend:
            emit_combos(*Upend.pop(0))

        if stop <= 2:
            if dbg:
                dump(hTf[:, 0, :], BhTf, 2560)
                if stop == 2:
                    for v in range(5):
                        dump(var[:, v, 0, :], [Bvar[v][0]], 512)
            finish()
            return nc

        Zt = reg(BZ_OFF, BZ_SZ).rearrange("p (r g t) -> p r g t", r=2, g=4)
        BZ = [[[Buf("Z%d_%d_%d" % (r, g, v)) for v in range(4)] for g in range(4)] for r in range(2)]
        S.claim([b for r in BZ for gg in r for b in gg], BhT[18:30] + BmemT)
        tabs = [reg(D_OFF + s * 8192, 8192).rearrange("p (a i w) -> p a i w", a=2, i=8) for s in range(2)]
        Btab = [Buf("ftab%d" % s) for s in range(2)]
        S.claim(Btab, Bxt)
        CLS = {
            0: ([(0, 0)], [(0, 1)]),
            2: ([(1, 0)], [(1, 1)]),
            1: ([(2, 0), (3, 1)], [(4, 0), (2, 1)]),
            3: ([(2, 0), (4, 1)], [(3, 0), (2, 1)]),
        }
        for ci, v in enumerate([0, 2, 1, 3]):
            s = ci % 2
            S.dma("sp", tabs[s], fft_d[v], writes=[Btab[s]])
            for g in range(4):
                for ri in range(2):
                    terms = CLS[v][ri]
                    p, pB = next_ps()
                    n = len(terms) * 8
                    j = 0
                    for (vi, ab) in terms:
                        for i in range(8):
                            S.op("pe", lambda e, vi=vi, ab=ab, i=i, g=g, p=p, j=j, n=n, s=s: e.matmul(
                                p[:], lhsT=var[:, vi, i, g * 128:(g + 1) * 128], rhs=tabs[s][:, ab, i, :], start=(j == 0), stop=(j == n - 1)),
                                reads=[Bvar[vi][i], Btab[s]], writes=[pB], inc=(j == n - 1))
                            j += 1
                    zo = Zt[:, ri, g, :].rearrange("p (w v) -> p v w", v=4)[:, v, :]
                    S.op("act", lambda e, zo=zo, p=p: e.activation(out=zo, in_=p[:], func=AF.Copy), reads=[pB], writes=[BZ[ri][g][v]])
        BZg = [[BZ[r][g] for g in range(4)] for r in range(2)]

        if stop <= 3:
            if dbg:
                dump(Zt[:, 0, 0, :], BZ[0][0], 2048)
                dump(Zt[:, 1, 1, :], BZ[1][1], 2048)
            finish()
            return nc

        Va = reg(C_OFF, 10400).rearrange("p (f h e) -> p f h e", f=NFR, h=8)
        BVa = [Buf("Va%d" % f) for f in range(NFR)]
        qzz = reg(C_OFF + 10400, 4096).rearrange("p (t h q) -> p t h q", t=16, h=2)
        kT = reg(C_OFF + 10400 + 4096, 2560)
        BqT = [Buf("qT%d" % t) for t in range(4)]
        BkT = [Buf("kT%d" % t) for t in range(5)]
        allvar = [b for r in Bvar for b in r]
        S.claim(BVa + BqT + BkT, allvar)
        wv = reg(G_OFF, 4096).rearrange("p (k c) -> p k c", k=8)
        Bwv = Buf("wv")
        S.claim([Bwv], [Bwu])
        S.dma("pool", wv, wview(w_in)[:, :, 1024:1536], writes=[Bwv])
        nab = reg(E_OFF, 2 * NA_NCH * 128).rearrange("p (c h q) -> p c h q", h=2, c=NA_NCH)
        Bnab = Buf("nab")
        oT = reg(D_OFF, 8192).rearrange("p (k t) -> p k t", k=4)
        BoT = [[Buf("oT%d_%d" % (k, t)) for t in range(16)] for k in range(4)]
        omT = reg(D_OFF + 8192, 8192).rearrange("p (k t) -> p k t", k=4)
        BomT = [[Buf("omT%d_%d" % (k, t)) for t in range(4)] for k in range(4)]
        S.claim([b for r in BoT for b in r], [Btab[0]])
        S.claim([b for r in BomT for b in r], [Btab[1]])
        wf = reg(D_OFF + 8192, 4096).rearrange("p (g c) -> p g c", g=4)
        Bwf = Buf("wf")
        S.claim([Bwf], [Btab[1]])
        S.dma("pool", wf, w_f.rearrange("(g p) c -> p g c", p=128), writes=[Bwf])

        S.op("dve", lambda e: e.memset(Va[:, :, :, 64:65], 1.0), reads=[], writes=BVa)
        S.op("dve", lambda e: e.memset(reg(C_OFF + 10400, 4096), 0.0), reads=[], writes=BqT)
        for f in range(NFR):
            p, pB = next_ps()
            for k in range(8):
                S.op("pe", lambda e, k=k, f=f, p=p: e.matmul(p[:], lhsT=hTf[:, k, f * 128:(f + 1) * 128], rhs=wv[:, k, :], start=(k == 0), stop=(k == 7)),
                     reads=[BhTf[f], Bwv], writes=[pB], inc=(k == 7))
            S.op("act", lambda e, f=f, p=p: e.activation(out=Va[:, f, :, 0:64], in_=p[:].rearrange("p (h d) -> p h d", h=8), func=AF.Copy),
                 reads=[pB], writes=[BVa[f]])

        wsl = [reg(G_OFF + i * 1024, 1024).rearrange("p (k c) -> p k c", k=8) for i in range(4)]
        Bwsl = [Buf("wsl%d" % i) for i in range(4)]
        S.claim(Bwsl, [Bwv])

        H0, H1 = slice(0, 64), slice(64, 128)
        for hp in range(4):
            s = hp % 2
            wq, wk = wsl[2 * s], wsl[2 * s + 1]
            BwqB, BwkB = Bwsl[2 * s], Bwsl[2 * s + 1]
            S.dma("pool", wq, wview(w_in)[:, :, hp * 128:(hp + 1) * 128], writes=[BwqB])
            S.dma("pool", wk, wview(w_in)[:, :, 512 + hp * 128:512 + (hp + 1) * 128], writes=[BwkB])
            S.dma("pool", nab.rearrange("p c h q -> p (c h) q"), nab_d[hp], writes=[Bnab])
            qitems = []
            for tb in range(4):
                qitems.append(dict(w=wq, wB=BwqB, rhs_fn=(lambda k, tb=tb: hTf[:, k, 256 + tb * 512:256 + (tb + 1) * 512]),
                                   rhsBs=BhTf[2 + 4 * tb:6 + 4 * tb], mat=blk64, inv_d=1.0 / 64, gain=vec2[:, 0:1], gainB=Bvec2, outB=BqT[tb],
                                   split=[(H0, qzz[H0, 4 * tb:4 * tb + 4, 0, :]), (H1, qzz[H1, 4 * tb:4 * tb + 4, 1, :])]))
            kitems = []
            for fb in range(5):
                kitems.append(dict(w=wk, wB=BwkB, rhs_fn=(lambda k, fb=fb: hTf[:, k, fb * 512:(fb + 1) * 512]),
                                   rhsBs=BhTf[4 * fb:4 * fb + 4], mat=blk64, inv_d=1.0 / 64, gain=vec[:, V_KG:V_KG + 1], gainB=Bvec,
                                   out=kT[:, fb * 512:(fb + 1) * 512], outB=BkT[fb]))
            proj_norm_batch(kitems[0:3])
            proj_norm_batch(kitems[3:5] + qitems[0:1])
            proj_norm_batch(qitems[1:4])

            def emitS(tl):
                b0, nb_, voff = na_band(tl)
                par = tl % 2
                for j in range(nb_):
                    bk = 3 * par + j // 2
                    pp, ppB = ps[bk], Bps[bk]
                    col = (j % 2) * 256
                    f = b0 + j
                    last = (j % 2 == 1) or (j == nb_ - 1)
                    S.op("pe", lambda e, pp=pp, col=col, f=f, tl=tl: e.matmul(
                        pp[:, col:col + 256], lhsT=kT[:, f * 128:(f + 1) * 128], rhs=qzz[:, tl].rearrange("p h q -> p (h q)"), start=True, stop=False),
                        reads=[BkT[f // 4], BqT[tl // 4]], writes=[ppB], inc=False)
                    S.op("pe", lambda e, pp=pp, col=col, j=j, voff=voff: e.matmul(
                        pp[:, col:col + 256], lhsT=ident, rhs=nab[:, voff + j].rearrange("p h q -> p (h q)"), start=False, stop=True),
                        reads=[Bcst, Bnab], writes=[ppB], inc=last)
                Pt, PtB = PT[par], BPT[par]
                for bi in range((nb_ + 1) // 2):
                    bk = 3 * par + bi
                    ncol = min(512, (nb_ - 2 * bi) * 256)
                    S.op("act", lambda e, Pt=Pt, bk=bk, bi=bi, ncol=ncol: e.activation(out=Pt[:, bi * 512:bi * 512 + ncol], in_=ps[bk][:, 0:ncol], func=AF.Exp),
                         reads=[Bps[bk]], writes=[PtB])

            def emitPV(tl):
                b0, nb_, voff = na_band(tl)
                par = tl % 2
                po, poB = ps[6 + par], Bps[6 + par]
                Pt, PtB = PT[par], BPT[par]
                for hh in range(2):
                    h = 2 * hp + hh
                    for j in range(nb_):
                        f = b0 + j
                        S.op("pe", lambda e, po=po, hh=hh, j=j, f=f, h=h, Pt=Pt, nb_=nb_: e.matmul(
                            po[:, hh * 65:(hh + 1) * 65], lhsT=Pt[:, j * 256 + hh * 128:j * 256 + (hh + 1) * 128], rhs=Va[:, f, h, :],
                            start=(j == 0), stop=(j == nb_ - 1)),
                            reads=[PtB, BVa[f]], writes=[poB], inc=(j == nb_ - 1))

            def emitFin(tl, hp=hp):
                par = tl % 2
                po, poB = ps[6 + par], Bps[6 + par]
                pov = po[:, 0:130].rearrange("p (h e) -> p h e", h=2)
                S.op("dve", lambda e: e.reciprocal(out=rc[:, 2 * par:2 * par + 2], in_=pov[:, :, 64]), reads=[poB], writes=[Brc[par]])
                S.op("dve", lambda e: e.tensor_tensor(
                    out=onb[par][:].rearrange("p (h d) -> p h d", h=2), in0=pov[:, :, 0:64],
                    in1=rc[:, 2 * par:2 * par + 2].unsqueeze(2).to_broadcast([128, 2, 64]), op=ALU.mult),
                    reads=[poB, Brc[par]], writes=[Bonb[par]])
                ptb = po[:].bitcast(BF16)[:, 512:640]
                S.op("pe", lambda e: e.transpose(out=ptb, in_=onb[par][:], identity=ident), reads=[Bonb[par], Bcst], writes=[poB])
                S.op("dve", lambda e: e.tensor_copy(out=oT[:, hp, tl * 128:(tl + 1) * 128], in_=ptb), reads=[poB], writes=[BoT[hp][tl]])

            for i in range(16 + 2):
                if i < 16:
                    emitS(i)
                if 1 <= i <= 16:
                    emitPV(i - 1)
                if i >= 2:
                    emitFin(i - 2)

        if stop <= 4:
            if dbg:
                dump(oT[:, 0, :], BoT[0], 2048)
                dump(oT[:, 3, :], BoT[3], 2048)
            finish()
            return nc

        Wf2 = reg(E_OFF, 8192).rearrange("p (a g c) -> p a g c", a=2, g=4)
        BWf2 = Buf("Wf2")
        S.claim([BWf2], [Bnab])
        for a, M in enumerate([CcM, ScM]):
            for g in range(4):
                for half in range(2):
                    p, pB = next_ps()
                    S.op("pe", lambda e, p=p, M=M, g=g, half=half: e.matmul(p[:], lhsT=M, rhs=wf[:, g, half * 512:(half + 1) * 512], start=True, stop=True),
                         reads=[Bcst, Bwf], writes=[pB])
                    S.op("dve", lambda e, p=p, a=a, g=g, half=half: e.tensor_copy(out=Wf2[:, a, g, half * 512:(half + 1) * 512], in_=p[:]),
                         reads=[pB], writes=[BWf2])
        S.claim([b for r in BomT for b in r], [Bwf])
        f1s = [reg(G_OFF, 4096), reg(C_OFF + 16384, 4096)]
        Bf1 = [Buf("f1s0"), Buf("f1s1")]
        S.claim([Bf1[1]], BVa + BqT + BkT)

        def load_f1(dc):
            s_ = (dc + 1) % 2
            base = f1s[s_]
            gw = base[:, 0:3072].rearrange("p (b k c) -> p b k c", b=3, k=8)
            wo = base[:, 3072:4096].rearrange("p (b k c) -> p b k c", b=2, k=4)
            for br in range(3):
                S.dma("pool", gw[:, br], wview(w_in)[:, :, 2560 + br * 1024 + dc * 128:2560 + br * 1024 + (dc + 1) * 128],
                      writes=[Bf1[s_]] if br == 0 else [], reads=[], key=Bf1[s_])
            S.dma("pool", wo[:, 0], w_na_o.rearrange("(k p) c -> p k c", p=128)[:, :, dc * 128:(dc + 1) * 128], key=Bf1[s_])
            tokw = S.dma("pool", wo[:, 1], w_mem_o.rearrange("(k p) c -> p k c", p=128)[:, :, dc * 128:(dc + 1) * 128], key=Bf1[s_])
            Bf1[s_].lw = tokw

        load_f1(0)

        mqs = [[reg(C_OFF + (hs * 4 + tb) * 512, 512) for tb in range(4)] for hs in range(2)]
        Bmqs = [[Buf("mq%d_%d" % (hs, tb)) for tb in range(4)] for hs in range(2)]
        S.claim([b for r in Bmqs for b in r], BVa + BqT + BkT)

        def emitDPN(h):
            wmq, BwmqB = wsl[h % 4], Bwsl[h % 4]
            S.dma("pool", wmq, wview(w_in)[:, :, 2048 + h * 128:2048 + (h + 1) * 128], writes=[BwmqB])
            items = []
            for tb in range(4):
                items.append(dict(w=wmq, wB=BwmqB, rhs_fn=(lambda k, tb=tb: hTf[:, k, 256 + tb * 512:256 + (tb + 1) * 512]),
                                  rhsBs=BhTf[2 + 4 * tb:6 + 4 * tb], mat=ones, inv_d=1.0 / 128, gain=vec2[:, 1:2], gainB=Bvec2,
                                  out=mqs[h % 2][tb], outB=Bmqs[h % 2][tb]))
            proj_norm_batch(items)

        def emitDattn(h, tb):
            mq, mqB = mqs[h % 2][tb], Bmqs[h % 2][tb]
            pts = []
            for c in range(2):
                pS, pSB = next_ps()
                S.op("pe", lambda e, pS=pS, c=c: e.matmul(pS[:], lhsT=kmT[:, h, c * 128:(c + 1) * 128], rhs=mq, start=True, stop=True),
                     reads=[BkmT, mqB], writes=[pSB])
                pt_, ptB_ = next_tb()
                S.op("act", lambda e, pS=pS, pt_=pt_: e.activation(out=pt_[:], in_=pS[:], func=AF.Exp), reads=[pSB], writes=[ptB_])
                pts.append((pt_, ptB_))
            po, poB = next_ps()
            pq, pqB = next_ps()
            for c in range(2):
                S.op("pe", lambda e, c=c: e.matmul(po[:], lhsT=vm[:, c, h * 128:(h + 1) * 128], rhs=pts[c][0][:], start=(c == 0), stop=(c == 1)),
                     reads=[Bvm, pts[c][1]], writes=[poB], inc=(c == 1))
            for c in range(2):
                S.op("pe", lambda e, c=c: e.matmul(pq[:], lhsT=ones, rhs=pts[c][0][:], start=(c == 0), stop=(c == 1)),
                     reads=[Bcst, pts[c][1]], writes=[pqB], inc=(c == 1))
            t, tB = next_tf()
            if True:
                S.op("act", lambda e: e.activation(out=t[:], in_=pq[:], func=AF.Ln), reads=[pqB], writes=[tB])
                S.op("act", lambda e: e.activation(out=t[:], in_=t[:], func=AF.Exp, scale=-1.0), reads=[tB], writes=[tB])
            else:
                S.op("dve", lambda e: e.reciprocal(out=t[:], in_=pq[:]), reads=[pqB], writes=[tB])
            S.op("dve", lambda e: e.tensor_tensor(out=omT[:, h, tb * 512:(tb + 1) * 512], in0=po[:], in1=t[:], op=ALU.mult),
                 reads=[poB, tB], writes=[BomT[h][tb]])

        emitDPN(0)
        for h in range(4):
            if h + 1 < 4:
                emitDPN(h + 1)
            for tb in range(4):
                emitDattn(h, tb)

        if stop <= 5:
            if dbg:
                dump(omT[:, 0, :], BomT[0], 2048)
                dump(omT[:, 3, :], BomT[3], 2048)
            finish()
            return nc

        mT = reg(C_OFF, 16384).rearrange("p (k t) -> p k t", k=8)
        BmT = [[Buf("mT%d_%d" % (k, t)) for t in range(4)] for k in range(8)]
        ncbufs = BVa + BqT + BkT + [b for r in Bmqs for b in r]
        S.claim([b for r in BmT for b in r], ncbufs)
        S.claim([Bf1[0]], [Bwv] + Bwsl)
        for dc in range(8):
            s = (dc + 1) % 2
            base = f1s[s]
            gw = base[:, 0:3072].rearrange("p (b k c) -> p b k c", b=3, k=8)
            wo = base[:, 3072:4096].rearrange("p (b k c) -> p b k c", b=2, k=4)
            if dc + 1 < 8:
                load_f1(dc + 1)
            for tb in range(4):
                tsl = slice(tb * 512, (tb + 1) * 512)
                acc = None
                for br in range(3):
                    pg, pgB = next_ps()
                    for k in range(8):
                        S.op("pe", lambda e, pg=pg, gw=gw, br=br, k=k, tb=tb: e.matmul(
                            pg[:], lhsT=gw[:, br, k, :], rhs=hTf[:, k, 256 + tb * 512:256 + (tb + 1) * 512], start=(k == 0), stop=(k == 7)),
                            reads=[Bf1[s]] + BhTf[2 + 4 * tb:6 + 4 * tb], writes=[pgB], inc=(k == 7))
                    py, pyB = next_ps()
                    if br == 0:
                        for k in range(4):
                            S.op("pe", lambda e, py=py, wo=wo, k=k, tsl=tsl: e.matmul(py[:], lhsT=wo[:, 0, k, :], rhs=oT[:, k, tsl], start=(k == 0), stop=(k == 3)),
                                 reads=[Bf1[s]] + BoT[k][4 * tb:4 * tb + 4], writes=[pyB], inc=(k == 3))
                    elif br == 1:
                        j = 0
                        for a in range(2):
                            for g in range(4):
                                S.op("pe", lambda e, py=py, a=a, g=g, dc=dc, tsl=tsl, j=j: e.matmul(
                                    py[:], lhsT=Wf2[:, a, g, dc * 128:(dc + 1) * 128], rhs=Zt[:, a, g, tsl], start=(j == 0), stop=(j == 7)),
                                    reads=[BWf2] + BZ[a][g], writes=[pyB], inc=(j == 7))
                                j += 1
                    else:
                        for k in range(4):
                            S.op("pe", lambda e, py=py, wo=wo, k=k, tsl=tsl: e.matmul(py[:], lhsT=wo[:, 1, k, :], rhs=omT[:, k, tsl], start=(k == 0), stop=(k == 3)),
                                 reads=[Bf1[s], BomT[k][tb]], writes=[pyB], inc=(k == 3))
                    sg, sgB = next_tf()
                    bcol = V_BG + br * 8 + dc
                    S.op("act", lambda e, sg=sg, pg=pg, bcol=bcol: e.activation(out=sg[:], in_=pg[:], func=AF.Sigmoid, bias=vec[:, bcol:bcol + 1]),
                         reads=[pgB, Bvec], writes=[sgB])
                    if br == 0:
                        S.op("dve", lambda e, sg=sg, py=py: e.tensor_tensor(out=sg[:], in0=sg[:], in1=py[:], op=ALU.mult), reads=[sgB, pyB], writes=[sgB])
                        acc, accB = sg, sgB
                    elif br == 1:
                        S.op("dve", lambda e, sg=sg, py=py: e.tensor_tensor(out=sg[:], in0=sg[:], in1=py[:], op=ALU.mult), reads=[sgB, pyB], writes=[sgB])
                        S.op("dve", lambda e, sg=sg, acc=acc: e.tensor_tensor(out=acc[:], in0=acc[:], in1=sg[:], op=ALU.add), reads=[sgB, accB], writes=[accB])
                    else:
                        S.op("dve", lambda e, sg=sg, py=py: e.tensor_tensor(out=sg[:], in0=sg[:], in1=py[:], op=ALU.mult), reads=[sgB, pyB], writes=[sgB])
                        S.op("dve", lambda e, sg=sg, acc=acc, dc=dc, tsl=tsl: e.tensor_tensor(out=mT[:, dc, tsl], in0=acc[:], in1=sg[:], op=ALU.add),
                             reads=[sgB, accB], writes=[BmT[dc][tb]])

        if stop <= 6:
            if dbg:
                dump(mT[:, 0, :], BmT[0], 2048)
                dump(mT[:, 7, :], BmT[7], 2048)
            finish()
            return nc

        wout = reg(C_OFF + 16384, 8192).rearrange("p (k c) -> p k c", k=8)
        Bwout = Buf("wout")
        S.claim([Bwout], [Bf1[1], Bf1[0]])
        S.dma("pool", wout, wview(w_out), writes=[Bwout])
        x1 = reg(BZ_OFF, BZ_SZ + D_SZ).bitcast(F32).rearrange("p (t c) -> p t c", t=16)
        Bx1 = [[Buf("x1_%d_%d" % (t, hf)) for hf in range(2)] for t in range(16)]
        oldz = [b for r in BZ for gg in r for b in gg] + [b for r in BoT for b in r] + [b for r in BomT for b in r]
        S.claim([b for r in Bx1 for b in r], oldz)
        h2T = reg(A_OFF, 16384).rearrange("p (k t) -> p k t", k=8)
        Bh2 = [Buf("h2T%d" % t) for t in range(16)]
        S.claim(Bh2, BhTf)
        S.dma("sp", gbc[:], gbc_d[1], writes=[Bgbc])
        pend2 = {}
        NXF = 4
        xtf = [reg(E_OFF + i * 2048, 2048).bitcast(F32) for i in range(NXF)]
        Bxtf = [Buf("xtf%d" % i) for i in range(NXF)]
        S.claim(Bxtf, [BWf2])
        ctr["xtf"] = 0

        def emitX1p(j):
            srcs = []
            for t in (2 * j, 2 * j + 1):
                xi = nxt("xtf", NXF)
                S.dma("sp", xtf[xi], xr[t * 128:(t + 1) * 128, :], writes=[Bxtf[xi]])
                for hf in range(2):
                    p, pB = next_ps()
                    for k in range(8):
                        S.op("pe", lambda e, p=p, k=k, hf=hf, t=t: e.matmul(p[:], lhsT=mT[:, k, t * 128:(t + 1) * 128], rhs=wout[:, k, hf * 512:(hf + 1) * 512],
                                                                         start=(k == 0), stop=(k == 7)),
                             reads=[BmT[k][t // 4], Bwout], writes=[pB], inc=(k == 7))
                    S.op("dve", lambda e, p=p, hf=hf, xi=xi, t=t: e.tensor_tensor(out=x1[:, t, hf * 512:(hf + 1) * 512], in0=p[:], in1=xtf[xi][:, hf * 512:(hf + 1) * 512], op=ALU.add),
                         reads=[pB, Bxtf[xi]], writes=[Bx1[t][hf]])
                srcs.append((x1[:, t, :], Bx1[t]))
            xis = tok_norm_a2(srcs)
            pend2[2 * j], pend2[2 * j + 1] = xis

        def emitT2p2(j):
            for q_, t in enumerate((2 * j, 2 * j + 1)):
                tok_norm_b(pend2.pop(t), V_G2, h2T[:, :, t * 128:(t + 1) * 128], Bh2[t], ev=q_)

        emitX1p(0)
        for j in range(8):
            if j + 1 < 8:
                emitX1p(j + 1)
            emitT2p2(j)

        if stop <= 7:
            if dbg:
                dump(x1[:, 0, :], Bx1[0], 1024)
                dump(h2T[:, 0, :], Bh2, 2048)
            finish()
            return nc

        w1s = [reg(C_OFF + s * 8192, 4096).rearrange("p (k c) -> p k c", k=8) for s in range(2)]
        w2s = [reg(C_OFF + s * 8192 + 4096, 4096).rearrange("p (k c) -> p k c", k=4) for s in range(2)]
        Bw1 = [Buf("w1_%d" % s) for s in range(2)]
        Bw2 = [Buf("w2_%d" % s) for s in range(2)]
        S.claim(Bw1 + Bw2, [b for r in BmT for b in r])
        aTs = [reg(C_OFF + 16384, 8192).rearrange("p (k t) -> p k t", k=4), reg(E_OFF, 8192).rearrange("p (k t) -> p k t", k=4)]
        BaT = [[[Buf("aT%d_%d_%d" % (s, k, t)) for t in range(4)] for k in range(4)] for s in range(2)]
        S.claim([b for r in BaT[0] for b in r], [Bwout])
        S.claim([b for r in BaT[1] for b in r], [BWf2] + Bxtf)
        NG = 8
        for grp in range(NG):
            s = grp % 2
            S.dma("pool", w1s[s], wview(w_ff1)[:, :, grp * 512:(grp + 1) * 512], writes=[Bw1[s]])
            S.dma("pool", w2s[s], w_ff2[grp * 512:(grp + 1) * 512, :].rearrange("(k p) c -> p k c", p=128), writes=[Bw2[s]])
            aT = aTs[s]
            for fc in range(4):
                for tb in range(4):
                    p, pB = next_ps()
                    for k in range(8):
                        S.op("pe", lambda e, p=p, k=k, fc=fc, tb=tb, s=s: e.matmul(p[:], lhsT=w1s[s][:, k, fc * 128:(fc + 1) * 128],
                                                                                 rhs=h2T[:, k, tb * 512:(tb + 1) * 512], start=(k == 0), stop=(k == 7)),
                             reads=[Bw1[s]] + Bh2[4 * tb:4 * tb + 4], writes=[pB], inc=(k == 7))
                    t, tB = next_tf()
                    S.op("act", lambda e, t=t, p=p: e.activation(out=t[:], in_=p[:], func=AF.Relu), reads=[pB], writes=[tB])
                    S.op("act", lambda e, t=t, aT=aT, fc=fc, tb=tb: e.activation(out=aT[:, fc, tb * 512:(tb + 1) * 512], in_=t[:], func=AF.Square),
                         reads=[tB], writes=[BaT[s][fc][tb]])
            for t in range(16):
                for hf in range(2):
                    p, pB = next_ps()
                    for fc in range(4):
                        S.op("pe", lambda e, p=p, fc=fc, t=t, hf=hf, s=s, aT=aT: e.matmul(p[:], lhsT=aT[:, fc, t * 128:(t + 1) * 128],
                                                                                      rhs=w2s[s][:, fc, hf * 512:(hf + 1) * 512], start=(fc == 0), stop=(fc == 3)),
                             reads=[BaT[s][fc][t // 4], Bw2[s]], writes=[pB], inc=(fc == 3))
                    S.op("dve", lambda e, p=p, t=t, hf=hf: e.tensor_tensor(out=x1[:, t, hf * 512:(hf + 1) * 512], in0=p[:], in1=x1[:, t, hf * 512:(hf + 1) * 512], op=ALU.add),
                         reads=[pB, Bx1[t][hf]], writes=[Bx1[t][hf]])
                if grp == NG - 1:
                    S.dma("sp", y[t * 128:(t + 1) * 128, :], x1[:, t, :], reads=Bx1[t], writes=[By[t]])
        finish()
    return nc


def _na_table(rpb, hf):
    rpb = np.asarray(rpb, np.float32)
    tab = np.full((8, NA_NCH, 128, 128), NEG, np.float32)
    kr2 = np.arange(128) // 64
    kc = np.arange(128) % 64
    qr2 = np.arange(128) // 64
    qc = np.arange(128) % 64
    cs = np.clip(qc - 8, 0, 48)
    tls = [0, 1, 2, 14, 15]
    for vi, tl in enumerate(tls):
        b0, nb_, voff = na_band(tl)
        t = 16 * hf + tl
        qrow = 2 * t + qr2
        rs = np.clip(qrow - 4, 0, 56)
        for j in range(nb_):
            f = b0 + j
            lt = (f - 2) % 32
            real = (2 <= f < 18) or (f < 2 and hf == 1) or (f >= 18 and hf == 0)
            if not real:
                continue
            gt = (lt + 16 * hf) % 32
            krow = 2 * gt + kr2
            dr = krow[:, None] - qrow[None, :]
            rowok = (krow[:, None] >= rs[None, :]) & (krow[:, None] < rs[None, :] + 8)
            colok = (kc[:, None] >= cs[None, :]) & (kc[:, None] < cs[None, :] + 16)
            ok = rowok & colok
            dri = np.clip(dr + 7, 0, 14)
            dci = np.clip(kc[:, None] - qc[None, :], -15, 15) + 15
            g = rpb[:, dri, dci]
            tab[:, voff + j] = np.where(ok[None], g, NEG)
    tab = tab.reshape(4, 2, NA_NCH, 128, 128).transpose(0, 3, 2, 1, 4).reshape(4, 128, 2 * NA_NCH, 128)
    return np.ascontiguousarray(tab)


def _fft_tables(hf):
    m = np.arange(1024, dtype=np.int64)
    out = np.zeros((4, 2, 1024, 512), np.float64)
    w = np.arange(512, dtype=np.int64)
    for v in range(4):
        sp = 2048 * hf + 4 * w + v
        ph = (m[:, None] * sp[None, :]) % 4096
        ang = 2.0 * np.pi * ph / 4096.0
        sign = -1.0 if (hf == 1 and v % 2 == 1) else 1.0
        out[v, 0] = sign * np.cos(ang)
        out[v, 1] = -sign * np.sin(ang)
    out = out.reshape(4, 2, 8, 128, 512).transpose(0, 3, 1, 2, 4)
    return np.ascontiguousarray(out.astype(np.float32)).astype(ml_dtypes.bfloat16)


def _consts():
    c = np.zeros((128, 5, 128), np.float64)
    c[:, 0] = np.eye(128)
    blk = np.zeros((128, 128))
    blk[:64, :64] = 1.0
    blk[64:, 64:] = 1.0
    c[:, 1] = blk
    c[:, 2] = 1.0
    i = np.arange(128)
    ang = 2.0 * np.pi * ((i[:, None] * i[None, :]) % 128) / 128.0
    nrm = 1.0 / np.sqrt(4096.0 * 128.0)
    c[:, 3] = np.cos(ang) * nrm
    c[:, 4] = np.sin(ang) * nrm
    return c.astype(np.float32).astype(ml_dtypes.bfloat16)


def _vecs(norm1_g, norm2_g, mem_norm_g, b_gate, na_q_g, na_k_g, mem_q_g, mem_k_g):
    v = np.zeros((128, NVEC), np.float32)
    v[:, V_G1:V_G1 + 8] = np.asarray(norm1_g, np.float32).reshape(8, 128).T
    v[:, V_G2:V_G2 + 8] = np.asarray(norm2_g, np.float32).reshape(8, 128).T
    v[:, V_GM:V_GM + 8] = np.asarray(mem_norm_g, np.float32).reshape(8, 128).T
    v[:, V_BG:V_BG + 24] = np.asarray(b_gate, np.float32).reshape(24, 128).T
    v[:, V_QG] = np.tile(np.asarray(na_q_g, np.float32), 2)
    v[:, V_KG] = np.tile(np.asarray(na_k_g, np.float32), 2)
    v[:, V_MQG] = np.asarray(mem_q_g, np.float32)
    v[:, V_MKG] = np.asarray(mem_k_g, np.float32)
    return v


def make_in_maps(x, mem, norm1_g, w_in, b_gate, na_q_g, na_k_g, na_rpb, w_na_o, w_f,
                 mem_norm_g, w_mem_kv, mem_q_g, mem_k_g, w_mem_o, w_out, norm2_g, w_ff1, w_ff2):
    f = lambda a: np.ascontiguousarray(np.asarray(a, np.float32))
    x = f(x)
    mem = f(mem)
    shared = {
        "w_in": f(w_in), "w_na_o": f(w_na_o), "w_f": f(w_f), "w_mem_o": f(w_mem_o), "w_mem_kv": f(w_mem_kv),
        "w_out": f(w_out), "w_ff1": f(w_ff1), "w_ff2": f(w_ff2),
        "vecs": _vecs(norm1_g, norm2_g, mem_norm_g, b_gate, na_q_g, na_k_g, mem_q_g, mem_k_g),
        "consts": _consts(),
        "gbc": np.ascontiguousarray(np.stack([np.broadcast_to(np.asarray(g, np.float32)[None, :], (128, D))
                                              for g in (norm1_g, norm2_g, mem_norm_g)])),
    }
    nat = [_na_table(na_rpb, hf) for hf in range(2)]
    fft = [_fft_tables(hf) for hf in range(2)]
    maps = []
    for c in range(8):
        b, hf = c // 2, c % 2
        m = dict(shared)
        m["xr"] = np.ascontiguousarray(np.roll(x[b], -NOWN * hf, axis=0))
        m["mem"] = mem[b]
        m["nabias"] = nat[hf]
        m["fftab"] = fft[hf]
        maps.append(m)
    return maps


_NC_CACHE = {}


def kernel(**inputs):
    maps = make_in_maps(**inputs)
    if "nc" not in _NC_CACHE:
        _NC_CACHE["nc"] = build()
    nc = _NC_CACHE["nc"]
    res = run_bass_kernel_spmd(nc, maps, core_ids=list(range(8)))
    out = np.zeros((4, SEQ, D), np.float32)
    for c in range(8):
        b, hf = c // 2, c % 2
        out[b, NOWN * hf:NOWN * (hf + 1)] = res.results[c]["y"]
    return out
```

```python
from contextlib import ExitStack

import numpy as np
import ml_dtypes

import concourse.bass as bass
import concourse.mybir as mybir
from concourse.bass_utils import run_bass_kernel_spmd

F32 = mybir.dt.float32
BF16 = mybir.dt.bfloat16
AF = mybir.ActivationFunctionType
ALU = mybir.AluOpType

SAME_ENG_SYNC = True
EPS = 1e-6
NEG = -1e30


class Buf:
    __slots__ = ("name", "lw", "rds", "dsem", "dcnt")

    def __init__(self, name):
        self.name = name
        self.lw = None
        self.rds = {}
        self.dsem = None
        self.dcnt = 0


class EngQ:
    def __init__(self, name, sem, is_pe=False):
        self.name = name
        self.sem = sem
        self.n = 0
        self.waited = {}
        self.ops = []
        self.is_pe = is_pe


class Sched:
    def __init__(self, nc, stack):
        self.nc = nc
        self.stack = stack
        self.q = {}
        for name in ("pe", "act", "dve", "pool", "sp"):
            sem = stack.enter_context(nc.semaphore("q_" + name))
            self.q[name] = EngQ(name, sem, is_pe=(name == "pe"))
        self.nsem = 5
        self.free_sems = []

    def newsem(self, name):
        self.nsem += 1
        return self.stack.enter_context(self.nc.semaphore(name))

    def _collect(self, q, reads, writes):
        deps = {}

        def add(t):
            if t is None:
                return
            sem, val = t
            if sem is q.sem and (q.is_pe or not SAME_ENG_SYNC):
                return
            k = id(sem)
            if q.waited.get(k, 0) >= val:
                return
            if k not in deps or deps[k][1] < val:
                deps[k] = (sem, val)

        for b in reads:
            add(b.lw)
        for b in writes:
            add(b.lw)
            for t in b.rds.values():
                add(t)
        out = list(deps.values())
        for sem, val in out:
            q.waited[id(sem)] = val
        return out

    def _update(self, tok, reads, writes):
        for b in writes:
            b.lw = tok
            b.rds = {}
        k = id(tok[0])
        for b in reads:
            if k not in b.rds or b.rds[k][1] < tok[1]:
                b.rds[k] = tok

    def op(self, qn, fn, reads=(), writes=(), inc=True):
        q = self.q[qn]
        waits = self._collect(q, reads, writes)
        tok = (q.sem, q.n + 1)
        if inc:
            q.n += 1
        else:
            assert q.is_pe
        q.ops.append((waits, fn, (q.sem, 1) if inc else None))
        self._update(tok, reads, writes)

    def dma(self, qn, out, in_, reads=(), writes=(), key=None, **kw):
        q = self.q[qn]
        waits = self._collect(q, reads, writes)
        kb = key if key is not None else (writes[0] if writes else reads[0])
        if kb.dsem is None:
            kb.dsem = self.newsem("d_" + kb.name)
        kb.dcnt += 16
        tok = (kb.dsem, kb.dcnt)

        def fn(e, out=out, in_=in_, kw=kw):
            return e.dma_start(out=out, in_=in_, **kw)

        q.ops.append((waits, fn, (kb.dsem, 16)))
        self._update(tok, reads, writes)
        return tok

    def claim(self, new_bufs, old_bufs):
        toks = {}
        for b in old_bufs:
            for t in ([b.lw] if b.lw else []) + list(b.rds.values()):
                k = id(t[0])
                if k not in toks or toks[k][1] < t[1]:
                    toks[k] = t
        for nb in new_bufs:
            for k, t in toks.items():
                if k not in nb.rds or nb.rds[k][1] < t[1]:
                    nb.rds[k] = t

    def wait_all(self, qn, bufs):
        q = self.q[qn]
        waits = self._collect(q, (), bufs)
        q.ops.append((waits, None, None))

    def emit(self):
        nc = self.nc
        qs = self.q

        def run(q, e):
            for waits, fn, inc in q.ops:
                for sem, val in waits:
                    e.wait_ge(sem, val)
                if fn is None:
                    continue
                ins = fn(e)
                if inc is not None:
                    ins.then_inc(inc[0], inc[1])

        with nc.Block() as block:
            @block.tensor
            def _(e):
                run(qs["pe"], e)

            @block.scalar
            def _(e):
                run(qs["act"], e)

            @block.vector
            def _(e):
                run(qs["dve"], e)

            @block.gpsimd
            def _(e):
                run(qs["pool"], e)

            @block.sync
            def _(e):
                run(qs["sp"], e)


D = 1024
SEQ = 4096
NOWN = 2048
NFR = 20
NA_VARIANTS = [(0, 6), (1, 5), (2, 5), (14, 5), (14, 6)]
NA_VOFF = [0, 6, 11, 16, 21]
NA_NCH = 27


def na_band(tl):
    if tl == 0:
        return 0, 6, NA_VOFF[0]
    if tl == 1:
        return 1, 5, NA_VOFF[1]
    if tl == 14:
        return 14, 5, NA_VOFF[3]
    if tl == 15:
        return 14, 6, NA_VOFF[4]
    return tl, 5, NA_VOFF[2]


V_G1, V_G2, V_GM, V_BG, V_QG, V_KG, V_MQG, V_MKG = 0, 8, 16, 24, 48, 49, 50, 51
NVEC = 64

A_OFF, A_SZ = 0, 20480
BZ_OFF, BZ_SZ = 20480, 16384
D_OFF, D_SZ = 36864, 16384
C_OFF, C_SZ = 53248, 20480
G_OFF, G_SZ = 73728, 4096
E_OFF, E_SZ = 77824, 8192
ARENA = 86016


def build(stop=99, dbg=False):
    nc = bass.Bass("TRN2", target_bir_lowering=False)

    def dram(n, s, d, kind="ExternalInput"):
        return nc.dram_tensor(n, s, d, kind=kind).ap()

    xr = dram("xr", [SEQ, D], F32)
    mem = dram("mem", [256, D], F32)
    w_in = dram("w_in", [D, 5632], F32)
    w_na_o = dram("w_na_o", [512, D], F32)
    w_f = dram("w_f", [512, D], F32)
    w_mem_o = dram("w_mem_o", [512, D], F32)
    w_mem_kv = dram("w_mem_kv", [D, D], F32)
    w_out = dram("w_out", [D, D], F32)
    w_ff1 = dram("w_ff1", [D, 4096], F32)
    w_ff2 = dram("w_ff2", [4096, D], F32)
    vecs_d = dram("vecs", [128, NVEC], F32)
    nab_d = dram("nabias", [4, 128, 2 * NA_NCH, 128], F32)
    fft_d = dram("fftab", [4, 128, 2, 8, 512], BF16)
    cst_d = dram("consts", [128, 5, 128], BF16)
    gbc_d = dram("gbc", [3, 128, D], F32)
    y = dram("y", [NOWN, D], F32, kind="ExternalOutput")
    if dbg:
        dbg_d = dram("dbg", [128, 8192], F32, kind="ExternalOutput")

    def wview(w):
        return w.rearrange("(k p) c -> p k c", p=128)

    st = ExitStack()
    with st:
        S = Sched(nc, st)

        def sb(n, s, d):
            return st.enter_context(nc.sbuf_tensor(n, s, d))

        ar = sb("arena", [128, ARENA], BF16)

        def reg(off, n):
            return ar[:, off:off + n]

        cst = sb("cst", [128, 5, 128], BF16)
        vec = sb("vec", [128, NVEC], F32)
        vec2 = sb("vec2", [128, 4], F32)
        stat = sb("stat", [128, 64], F32)
        rstd = sb("rstd", [128, 64], F32)
        gbc = sb("gbcsb", [128, D], F32)
        Bgbc = Buf("gbc")
        NXA = 6
        xt = [reg(D_OFF + i * 2048, 2048).bitcast(F32) for i in range(NXA)]
        xs = [sb("xs%d" % i, [128, D], BF16) for i in range(4)]
        tmpf = [sb("tmpf%d" % i, [128, 512], F32) for i in range(5)]
        tmpb = [sb("tmpb%d" % i, [128, 512], BF16) for i in range(4)]
        PT = [sb("PT%d" % i, [128, 1536], BF16) for i in range(2)]
        kmT = sb("kmT", [128, 4, 256], BF16)
        vm = sb("vm", [128, 2, 512], BF16)
        onb = [sb("onb%d" % i, [128, 128], BF16) for i in range(2)]
        rc = sb("rc", [128, 4], F32)
        ps = [st.enter_context(nc.psum_tensor("ps%d" % i, [128, 512], F32)) for i in range(8)]

        Bcst, Bvec, Bvec2 = Buf("cst"), Buf("vec"), Buf("vec2")
        Bxt = [Buf("xt%d" % i) for i in range(NXA)]
        Bxs = [Buf("xs%d" % i) for i in range(4)]
        Btf = [Buf("tmpf%d" % i) for i in range(5)]
        Btb = [Buf("tmpb%d" % i) for i in range(4)]
        BPT = [Buf("PT%d" % i) for i in range(2)]
        Bps = [Buf("ps%d" % i) for i in range(8)]
        BkmT, Bvm = Buf("kmT"), Buf("vm")
        Bonb = [Buf("onb%d" % i) for i in range(2)]
        Brc = [Buf("rc%d" % i) for i in range(2)]
        Bstat = [Buf("stat%d" % i) for i in range(64)]

        ident = cst[:, 0, :]
        blk64 = cst[:, 1, :]
        ones = cst[:, 2, :]
        CcM = cst[:, 3, :]
        ScM = cst[:, 4, :]

        ctr = {"ps": 0, "tf": 0, "tb": 0, "xt": 0, "xs": 0, "st": 0, "ev": 0, "st2": 0}

        def nxt(key, n):
            i = ctr[key] % n
            ctr[key] += 1
            return i

        def next_ps():
            i = nxt("ps", 8)
            return ps[i], Bps[i]

        def next_tf():
            i = nxt("tf", 5)
            return tmpf[i], Btf[i]

        def next_tb():
            i = nxt("tb", 4)
            return tmpb[i], Btb[i]

        def next_stat():
            i = nxt("st", 32)
            return i, Bstat[i]

        S.dma("sp", cst[:], cst_d, writes=[Bcst])
        S.dma("sp", vec[:], vecs_d, writes=[Bvec])
        S.op("dve", lambda e: e.tensor_scalar(out=vec2[:, 0:1], in0=vec[:, V_QG:V_QG + 1], scalar1=0.125, scalar2=None, op0=ALU.mult),
             reads=[Bvec], writes=[Bvec2])
        S.op("dve", lambda e: e.tensor_scalar(out=vec2[:, 1:2], in0=vec[:, V_MQG:V_MQG + 1], scalar1=float(128 ** -0.5), scalar2=None, op0=ALU.mult),
             reads=[Bvec], writes=[Bvec2])

        junk = PT[0][:, 0:D]

        def tok_norm_a(src_ap, srcBs):
            si, sB = next_stat()
            xi = nxt("xs", 4)
            S.op("act", lambda e: e.activation(out=junk, in_=src_ap, func=AF.Square, accum_out=stat[:, si:si + 1]),
                 reads=srcBs, writes=[sB, BPT[0]])
            S.op("act", lambda e: e.activation(out=stat[:, si:si + 1], in_=stat[:, si:si + 1], func=AF.Sqrt, scale=1.0 / D, bias=EPS),
                 reads=[sB], writes=[sB])
            S.op("dve", lambda e: e.reciprocal(out=rstd[:, si:si + 1], in_=stat[:, si:si + 1]), reads=[sB], writes=[sB])
            S.op("dve", lambda e: e.scalar_tensor_tensor(out=xs[xi][:], in0=src_ap, scalar=rstd[:, si:si + 1], in1=gbc[:],
                                                          op0=ALU.mult, op1=ALU.mult),
                 reads=list(srcBs) + [sB, Bgbc], writes=[Bxs[xi]])
            return xi

        def tok_norm_b(xi, gcol, dst_fn, dstB, bank=None, ev=None):
            p, pB = next_ps() if bank is None else (ps[bank], Bps[bank])
            pb = p[:].bitcast(BF16)
            for k in range(8):
                S.op("pe", lambda e, k=k: e.transpose(out=pb[:, k * 128:(k + 1) * 128], in_=xs[xi][:, k * 128:(k + 1) * 128], identity=ident),
                     reads=[Bxs[xi], Bcst], writes=[pB], inc=(k == 7))
            if ev is None:
                ev = nxt("ev", 2)
            if ev == 0:
                S.op("act", lambda e: e.activation(out=dst_fn, in_=pb.rearrange("p (k t) -> p k t", k=8), func=AF.Copy),
                     reads=[pB], writes=[dstB])
            else:
                S.op("dve", lambda e: e.tensor_copy(out=dst_fn, in_=pb.rearrange("p (k t) -> p k t", k=8)), reads=[pB], writes=[dstB])

        def tok_norm_a2(srcs):
            c = 32 + 2 * nxt("st2", 16)
            sB = Bstat[c]
            for j, (src_ap, srcBs) in enumerate(srcs):
                S.op("act", lambda e, src_ap=src_ap, j=j: e.activation(out=junk, in_=src_ap, func=AF.Square, accum_out=stat[:, c + j:c + j + 1]),
                     reads=srcBs, writes=[sB, BPT[0]])
            S.op("act", lambda e: e.activation(out=stat[:, c:c + 2], in_=stat[:, c:c + 2], func=AF.Sqrt, scale=1.0 / D, bias=EPS),
                 reads=[sB], writes=[sB])
            S.op("dve", lambda e: e.reciprocal(out=rstd[:, c:c + 2], in_=stat[:, c:c + 2]), reads=[sB], writes=[sB])
            xis = []
            for j, (src_ap, srcBs) in enumerate(srcs):
                xi = nxt("xs", 4)
                S.op("dve", lambda e, src_ap=src_ap, j=j, xi=xi: e.scalar_tensor_tensor(out=xs[xi][:], in0=src_ap, scalar=rstd[:, c + j:c + j + 1], in1=gbc[:],
                                                                                     op0=ALU.mult, op1=ALU.mult),
                     reads=list(srcBs) + [sB, Bgbc], writes=[Bxs[xi]])
                xis.append(xi)
            return xis

        def tok_norm_transpose(src_ap, srcB, gcol, dst_fn, dstB):
            xi = tok_norm_a(src_ap, [srcB])
            tok_norm_b(xi, gcol, dst_fn, dstB)

        def fm_norm(p, pB, ncols, mat, inv_d, gain_ap, gainB, out_ap, outB, split=None, split3=False):
            sq, sqB = next_tb()
            S.op("act", lambda e: e.activation(out=sq[:, :ncols], in_=p[:, :ncols], func=AF.Square), reads=[pB], writes=[sqB])
            p2, p2B = next_ps()
            S.op("pe", lambda e: e.matmul(p2[:, :ncols], lhsT=mat, rhs=sq[:, :ncols], start=True, stop=True),
                 reads=[sqB, Bcst], writes=[p2B])
            t, tB = next_tf()
            S.op("act", lambda e: e.activation(out=t[:, :ncols], in_=p2[:, :ncols], func=AF.Ln, scale=inv_d, bias=EPS),
                 reads=[p2B], writes=[tB])
            S.op("act", lambda e: e.activation(out=t[:, :ncols], in_=t[:, :ncols], func=AF.Exp, scale=-0.5), reads=[tB], writes=[tB])
            if split is None:
                S.op("dve", lambda e: e.scalar_tensor_tensor(out=out_ap, in0=p[:, :ncols], scalar=gain_ap, in1=t[:, :ncols],
                                                              op0=ALU.mult, op1=ALU.mult),
                     reads=[pB, tB, gainB], writes=[outB])
            else:
                for (pr, oap) in split:
                    i0 = p[pr, :ncols]
                    i1 = t[pr, :ncols]
                    if split3:
                        i0 = i0.rearrange("p (t q) -> p t q", q=128)
                        i1 = i1.rearrange("p (t q) -> p t q", q=128)
                    S.op("dve", lambda e, pr=pr, oap=oap, i0=i0, i1=i1: e.scalar_tensor_tensor(out=oap, in0=i0, scalar=gain_ap[pr], in1=i1,
                                                                                             op0=ALU.mult, op1=ALU.mult),
                         reads=[pB, tB, gainB], writes=[outB])

        def proj_fm(wt, wB, ncolchunk, rhs_fn, rhsBs, p, pB, ncols):
            for k in range(8):
                S.op("pe", lambda e, k=k: e.matmul(p[:, :ncols], lhsT=wt[:, k, ncolchunk], rhs=rhs_fn(k), start=(k == 0), stop=(k == 7)),
                     reads=[wB] + list(rhsBs), writes=[pB], inc=(k == 7))

        def proj_norm_batch(items):
            n = len(items)
            P = []
            for it in items:
                p, pB = next_ps()
                proj_fm(it["w"], it["wB"], slice(0, 128), it["rhs_fn"], it["rhsBs"], p, pB, 512)
                P.append((p, pB))
            SQ = []
            for it, (p, pB) in zip(items, P):
                sq, sqB = next_tb()
                S.op("act", lambda e, sq=sq, p=p: e.activation(out=sq[:], in_=p[:], func=AF.Square), reads=[pB], writes=[sqB])
                SQ.append((sq, sqB))
            P2 = []
            for it, (sq, sqB) in zip(items, SQ):
                p2, p2B = next_ps()
                S.op("pe", lambda e, p2=p2, sq=sq, it=it: e.matmul(p2[:], lhsT=it["mat"], rhs=sq[:], start=True, stop=True),
                     reads=[sqB, Bcst], writes=[p2B])
                P2.append((p2, p2B))
            T = []
            for it, (p2, p2B) in zip(items, P2):
                t, tB = next_tf()
                S.op("act", lambda e, t=t, p2=p2, it=it: e.activation(out=t[:], in_=p2[:], func=AF.Ln, scale=it["inv_d"], bias=EPS),
                     reads=[p2B], writes=[tB])
                S.op("act", lambda e, t=t: e.activation(out=t[:], in_=t[:], func=AF.Exp, scale=-0.5), reads=[tB], writes=[tB])
                T.append((t, tB))
            for it, (p, pB), (t, tB) in zip(items, P, T):
                if it.get("split") is None:
                    S.op("dve", lambda e, it=it, p=p, t=t: e.scalar_tensor_tensor(out=it["out"], in0=p[:], scalar=it["gain"], in1=t[:],
                                                                                  op0=ALU.mult, op1=ALU.mult),
                         reads=[pB, tB, it["gainB"]], writes=[it["outB"]])
                else:
                    for (pr, oap) in it["split"]:
                        i0 = p[pr, :].rearrange("p (t q) -> p t q", q=128)
                        i1 = t[pr, :].rearrange("p (t q) -> p t q", q=128)
                        S.op("dve", lambda e, it=it, pr=pr, oap=oap, i0=i0, i1=i1: e.scalar_tensor_tensor(
                            out=oap, in0=i0, scalar=it["gain"][pr], in1=i1, op0=ALU.mult, op1=ALU.mult),
                            reads=[pB, tB, it["gainB"]], writes=[it["outB"]])

        dbg_off = [0]

        def dump(ap, B, n):
            for c0 in range(0, n, 512):
                w = min(512, n - c0)
                t, tB = next_tf()
                S.op("act", lambda e, c0=c0, w=w, t=t: e.activation(out=t[:, :w], in_=ap[:, c0:c0 + w], func=AF.Copy), reads=B, writes=[tB])
                o = dbg_off[0]
                S.dma("sp", dbg_d[:, o:o + w], t[:, :w], reads=[tB], writes=[Bdbg])
                dbg_off[0] += w

        Bdbg = Buf("dbg")
        By = [Buf("y%d" % i) for i in range(16)]

        def finish():
            S.wait_all("sp", By + [Bdbg])
            S.emit()

        wkv = reg(C_OFF, 8192).rearrange("p (k c) -> p k c", k=8)
        Bwkv = Buf("wkv")
        S.dma("pool", wkv, wview(w_mem_kv), writes=[Bwkv])
        S.dma("sp", gbc[:], gbc_d[2], writes=[Bgbc])
        memT = reg(BZ_OFF + 12288, 2048).rearrange("p (k t) -> p k t", k=8)
        BmemT = [Buf("memT%d" % i) for i in range(2)]
        for mt in range(2):
            xi = nxt("xt", NXA)
            S.dma("sp", xt[xi], mem[mt * 128:(mt + 1) * 128, :], writes=[Bxt[xi]])
            tok_norm_transpose(xt[xi], Bxt[xi], V_GM, memT[:, :, mt * 128:(mt + 1) * 128], BmemT[mt])
        for h in range(4):
            p, pB = next_ps()
            proj_fm(wkv, Bwkv, slice(h * 128, (h + 1) * 128), lambda k: memT[:, k, :], BmemT, p, pB, 256)
            fm_norm(p, pB, 256, ones, 1.0 / 128, vec[:, V_MKG:V_MKG + 1], Bvec, kmT[:, h, :], BkmT)
        for c in range(2):
            p, pB = next_ps()
            for k in range(8):
                S.op("pe", lambda e, k=k, c=c, p=p: e.matmul(p[:], lhsT=memT[:, k, c * 128:(c + 1) * 128], rhs=wkv[:, k, 512:1024],
                                                          start=(k == 0), stop=(k == 7)),
                     reads=[BmemT[c], Bwkv], writes=[pB], inc=(k == 7))
            S.op("act", lambda e, c=c, p=p: e.activation(out=vm[:, c, :], in_=p[:], func=AF.Copy), reads=[pB], writes=[Bvm])

        hTf = reg(A_OFF, A_SZ).rearrange("p (k t) -> p k t", k=8)
        hTo = reg(BZ_OFF, 12288).rearrange("p (k t) -> p k t", k=8)
        BhT = [Buf("hT%d" % i) for i in range(32)]

        def hT_tile(lt):
            f = (lt + 2) % 32
            if f < NFR:
                return hTf[:, :, f * 128:(f + 1) * 128]
            return hTo[:, :, (lt - 18) * 128:(lt - 17) * 128]

        BhTf = [BhT[(f - 2) % 32] for f in range(NFR)]

        S.dma("sp", gbc[:], gbc_d[0], writes=[Bgbc])
        wu = reg(G_OFF, 4096).rearrange("p (k c) -> p k c", k=8)
        Bwu = Buf("wu")
        S.dma("pool", wu, wview(w_in)[:, :, 1536:2048], writes=[Bwu])

        var = reg(C_OFF, C_SZ).rearrange("p (v i c) -> p v i c", v=5, i=8)
        Bvar = [[Buf("var%d_%d" % (v, i)) for i in range(8)] for v in range(5)]
        S.claim([b for r in Bvar for b in r], [Bwkv])

        order = [8 * e4 + i for i in range(8) for e4 in range(4)]
        pend = {}

        def emitN(n):
            lt = order[n]
            xi = nxt("xt", NXA)
            S.dma("sp", xt[xi], xr[lt * 128:(lt + 1) * 128, :], writes=[Bxt[xi]])
            pend[n] = tok_norm_a(xt[xi], [Bxt[xi]])

        def emitT(n, bank=None):
            lt = order[n]
            tok_norm_b(pend.pop(n), V_G1, hT_tile(lt), BhT[lt], bank=bank)

        def emit_combos(i, U):
            t0, t0B = next_tf()
            t1, t1B = next_tf()
            S.op("act", lambda e: e.activation(out=t0[:], in_=U[0][0][:], func=AF.Copy), reads=[U[0][1]], writes=[t0B])
            S.op("act", lambda e: e.activation(out=t1[:], in_=U[1][0][:], func=AF.Copy), reads=[U[1][1]], writes=[t1B])
            fa, faB = next_tf()
            fb, fbB = next_tf()
            S.op("dve", lambda e: e.tensor_tensor(out=fa[:], in0=t0[:], in1=U[2][0][:], op=ALU.add), reads=[t0B, U[2][1]], writes=[faB])
            S.op("dve", lambda e: e.tensor_tensor(out=var[:, 2, i, :], in0=t0[:], in1=U[2][0][:], op=ALU.subtract),
                 reads=[t0B, U[2][1]], writes=[Bvar[2][i]])
            S.op("dve", lambda e: e.tensor_tensor(out=fb[:], in0=t1[:], in1=U[3][0][:], op=ALU.add), reads=[t1B, U[3][1]], writes=[fbB])
            S.op("dve", lambda e: e.tensor_tensor(out=var[:, 3, i, :], in0=t1[:], in1=U[3][0][:], op=ALU.subtract),
                 reads=[t1B, U[3][1]], writes=[Bvar[3][i]])
            S.op("dve", lambda e: e.tensor_tensor(out=var[:, 0, i, :], in0=fa[:], in1=fb[:], op=ALU.add), reads=[faB, fbB], writes=[Bvar[0][i]])
            S.op("dve", lambda e: e.tensor_tensor(out=var[:, 1, i, :], in0=fa[:], in1=fb[:], op=ALU.subtract), reads=[faB, fbB], writes=[Bvar[1][i]])
            S.op("dve", lambda e: e.tensor_scalar(out=var[:, 4, i, :], in0=var[:, 3, i, :], scalar1=-1.0, scalar2=None, op0=ALU.mult),
                 reads=[Bvar[3][i]], writes=[Bvar[4][i]])

        Upend = []

        def emit_uproj(n):
            i, e4 = n // 4, n % 4
            lt = order[n]
            bk = 4 * (i % 2) + e4
            p, pB = ps[bk], Bps[bk]
            hv = hT_tile(lt)
            for k in range(8):
                S.op("pe", lambda e, k=k, p=p, hv=hv: e.matmul(p[:], lhsT=hv[:, k, :], rhs=wu[:, k, :], start=(k == 0), stop=(k == 7)),
                     reads=[BhT[lt], Bwu], writes=[pB], inc=(k == 7))
            if e4 == 3:
                Upend.append((i, [(ps[4 * (i % 2) + j], Bps[4 * (i % 2) + j]) for j in range(4)]))

        def emitN2(j):
            srcs = []
            for n in (2 * j, 2 * j + 1):
                lt = order[n]
                xi = nxt("xt", NXA)
                S.dma("sp", xt[xi], xr[lt * 128:(lt + 1) * 128, :], writes=[Bxt[xi]])
                srcs.append((xt[xi], [Bxt[xi]]))
            xis = tok_norm_a2(srcs)
            pend[2 * j], pend[2 * j + 1] = xis

        def emitT2p(j):
            for q_, n in enumerate((2 * j, 2 * j + 1)):
                i, e4 = n // 4, n % 4
                lt = order[n]
                tok_norm_b(pend.pop(n), V_G1, hT_tile(lt), BhT[lt], bank=4 * (i % 2) + e4, ev=q_)

        emitN2(0)
        for j in range(16):
            if j + 1 < 16:
                emitN2(j + 1)
            emitT2p(j)
            if stop >= 2:
                if j >= 1:
                    emit_uproj(2 * (j - 1))
                    emit_uproj(2 * (j - 1) + 1)
                if Upend and j >= 2 * Upend[0][0] + 3:
                    emit_combos(*Upend.pop(0))
        if stop >= 2:
            emit_uproj(30)
            emit_uproj(31)
        while Upend:
            emit_combos(*Upend.pop(0))

        if stop <= 2:
            if dbg:
                dump(hTf[:, 0, :], BhTf, 2560)
                if stop == 2:
                    for v in range(5):
                        dump(var[:, v, 0, :], [Bvar[v][0]], 512)
            finish()
            return nc

        Zt = reg(BZ_OFF, BZ_SZ).rearrange("p (r g t) -> p r g t", r=2, g=4)
        BZ = [[[Buf("Z%d_%d_%d" % (r, g, v)) for v in range(4)] for g in range(4)] for r in range(2)]
        S.claim([b for r in BZ for gg in r for b in gg], BhT[18:30] + BmemT)
        tabs = [reg(D_OFF + s * 8192, 8192).rearrange("p (a i w) -> p a i w", a=2, i=8) for s in range(2)]
        Btab = [Buf("ftab%d" % s) for s in range(2)]
        S.claim(Btab, Bxt)
        CLS = {
            0: ([(0, 0)], [(0, 1)]),
            2: ([(1, 0)], [(1, 1)]),
            1: ([(2, 0), (3, 1)], [(4, 0), (2, 1)]),
            3: ([(2, 0), (4, 1)], [(3, 0), (2, 1)]),
        }
        for ci, v in enumerate([0, 2, 1, 3]):
            s = ci % 2
            S.dma("sp", tabs[s], fft_d[v], writes=[Btab[s]])
            for g in range(4):
                for ri in range(2):
                    terms = CLS[v][ri]
                    p, pB = next_ps()
                    n = len(terms) * 8
                    j = 0
                    for (vi, ab) in terms:
                        for i in range(8):
                            S.op("pe", lambda e, vi=vi, ab=ab, i=i, g=g, p=p, j=j, n=n, s=s: e.matmul(
                                p[:], lhsT=var[:, vi, i, g * 128:(g + 1) * 128], rhs=tabs[s][:, ab, i, :], start=(j == 0), stop=(j == n - 1)),
                                reads=[Bvar[vi][i], Btab[s]], writes=[pB], inc=(j == n - 1))
                            j += 1
                    zo = Zt[:, ri, g, :].rearrange("p (w v) -> p v w", v=4)[:, v, :]
                    S.op("act", lambda e, zo=zo, p=p: e.activation(out=zo, in_=p[:], func=AF.Copy), reads=[pB], writes=[BZ[ri][g][v]])
        BZg = [[BZ[r][g] for g in range(4)] for r in range(2)]

        if stop <= 3:
            if dbg:
                dump(Zt[:, 0, 0, :], BZ[0][0], 2048)
                dump(Zt[:, 1, 1, :], BZ[1][1], 2048)
            finish()
            return nc

        Va = reg(C_OFF, 10400).rearrange("p (f h e) -> p f h e", f=NFR, h=8)
        BVa = [Buf("Va%d" % f) for f in range(NFR)]
        qzz = reg(C_OFF + 10400, 4096).rearrange("p (t h q) -> p t h q", t=16, h=2)
        kT = reg(C_OFF + 10400 + 4096, 2560)
        BqT = [Buf("qT%d" % t) for t in range(4)]
        BkT = [Buf("kT%d" % t) for t in range(5)]
        allvar = [b for r in Bvar for b in r]
        S.claim(BVa + BqT + BkT, allvar)
        wv = reg(G_OFF, 4096).rearrange("p (k c) -> p k c", k=8)
        Bwv = Buf("wv")
        S.claim([Bwv], [Bwu])
        S.dma("pool", wv, wview(w_in)[:, :, 1024:1536], writes=[Bwv])
        nab = reg(E_OFF, 2 * NA_NCH * 128).rearrange("p (c h q) -> p c h q", h=2, c=NA_NCH)
        Bnab = Buf("nab")
        oT = reg(D_OFF, 8192).rearrange("p (k t) -> p k t", k=4)
        BoT = [[Buf("oT%d_%d" % (k, t)) for t in range(16)] for k in range(4)]
        omT = reg(D_OFF + 8192, 8192).rearrange("p (k t) -> p k t", k=4)
        BomT = [[Buf("omT%d_%d" % (k, t)) for t in range(4)] for k in range(4)]
        S.claim([b for r in BoT for b in r], [Btab[0]])
        S.claim([b for r in BomT for b in r], [Btab[1]])
        wf = reg(D_OFF + 8192, 4096).rearrange("p (g c) -> p g c", g=4)
        Bwf = Buf("wf")
        S.claim([Bwf], [Btab[1]])
        S.dma("pool", wf, w_f.rearrange("(g p) c -> p g c", p=128), writes=[Bwf])

        S.op("dve", lambda e: e.memset(Va[:, :, :, 64:65], 1.0), reads=[], writes=BVa)
        S.op("dve", lambda e: e.memset(reg(C_OFF + 10400, 4096), 0.0), reads=[], writes=BqT)
        for f in range(NFR):
            p, pB = next_ps()
            for k in range(8):
                S.op("pe", lambda e, k=k, f=f, p=p: e.matmul(p[:], lhsT=hTf[:, k, f * 128:(f + 1) * 128], rhs=wv[:, k, :], start=(k == 0), stop=(k == 7)),
                     reads=[BhTf[f], Bwv], writes=[pB], inc=(k == 7))
            S.op("act", lambda e, f=f, p=p: e.activation(out=Va[:, f, :, 0:64], in_=p[:].rearrange("p (h d) -> p h d", h=8), func=AF.Copy),
                 reads=[pB], writes=[BVa[f]])

        wsl = [reg(G_OFF + i * 1024, 1024).rearrange("p (k c) -> p k c", k=8) for i in range(4)]
        Bwsl = [Buf("wsl%d" % i) for i in range(4)]
        S.claim(Bwsl, [Bwv])

        H0, H1 = slice(0, 64), slice(64, 128)
        for hp in range(4):
            s = hp % 2
            wq, wk = wsl[2 * s], wsl[2 * s + 1]
            BwqB, BwkB = Bwsl[2 * s], Bwsl[2 * s + 1]
            S.dma("pool", wq, wview(w_in)[:, :, hp * 128:(hp + 1) * 128], writes=[BwqB])
            S.dma("pool", wk, wview(w_in)[:, :, 512 + hp * 128:512 + (hp + 1) * 128], writes=[BwkB])
            S.dma("pool", nab.rearrange("p c h q -> p (c h) q"), nab_d[hp], writes=[Bnab])
            qitems = []
            for tb in range(4):
                qitems.append(dict(w=wq, wB=BwqB, rhs_fn=(lambda k, tb=tb: hTf[:, k, 256 + tb * 512:256 + (tb + 1) * 512]),
                                   rhsBs=BhTf[2 + 4 * tb:6 + 4 * tb], mat=blk64, inv_d=1.0 / 64, gain=vec2[:, 0:1], gainB=Bvec2, outB=BqT[tb],
                                   split=[(H0, qzz[H0, 4 * tb:4 * tb + 4, 0, :]), (H1, qzz[H1, 4 * tb:4 * tb + 4, 1, :])]))
            kitems = []
            for fb in range(5):
                kitems.append(dict(w=wk, wB=BwkB, rhs_fn=(lambda k, fb=fb: hTf[:, k, fb * 512:(fb + 1) * 512]),
                                   rhsBs=BhTf[4 * fb:4 * fb + 4], mat=blk64, inv_d=1.0 / 64, gain=vec[:, V_KG:V_KG + 1], gainB=Bvec,
                                   out=kT[:, fb * 512:(fb + 1) * 512], outB=BkT[fb]))
            proj_norm_batch(kitems[0:3])
            proj_norm_batch(kitems[3:5] + qitems[0:1])
            proj_norm_batch(qitems[1:4])

            def emitS(tl):
                b0, nb_, voff = na_band(tl)
                par = tl % 2
                for j in range(nb_):
                    bk = 3 * par + j // 2
                    pp, ppB = ps[bk], Bps[bk]
                    col = (j % 2) * 256
                    f = b0 + j
                    last = (j % 2 == 1) or (j == nb_ - 1)
                    S.op("pe", lambda e, pp=pp, col=col, f=f, tl=tl: e.matmul(
                        pp[:, col:col + 256], lhsT=kT[:, f * 128:(f + 1) * 128], rhs=qzz[:, tl].rearrange("p h q -> p (h q)"), start=True, stop=False),
                        reads=[BkT[f // 4], BqT[tl // 4]], writes=[ppB], inc=False)
                    S.op("pe", lambda e, pp=pp, col=col, j=j, voff=voff: e.matmul(
                        pp[:, col:col + 256], lhsT=ident, rhs=nab[:, voff + j].rearrange("p h q -> p (h q)"), start=False, stop=True),
                        reads=[Bcst, Bnab], writes=[ppB], inc=last)
                Pt, PtB = PT[par], BPT[par]
                for bi in range((nb_ + 1) // 2):
                    bk = 3 * par + bi
                    ncol = min(512, (nb_ - 2 * bi) * 256)
                    S.op("act", lambda e, Pt=Pt, bk=bk, bi=bi, ncol=ncol: e.activation(out=Pt[:, bi * 512:bi * 512 + ncol], in_=ps[bk][:, 0:ncol], func=AF.Exp),
                         reads=[Bps[bk]], writes=[PtB])

            def emitPV(tl):
                b0, nb_, voff = na_band(tl)
                par = tl % 2
                po, poB = ps[6 + par], Bps[6 + par]
                Pt, PtB = PT[par], BPT[par]
                for hh in range(2):
                    h = 2 * hp + hh
                    for j in range(nb_):
                        f = b0 + j
                        S.op("pe", lambda e, po=po, hh=hh, j=j, f=f, h=h, Pt=Pt, nb_=nb_: e.matmul(
                            po[:, hh * 65:(hh + 1) * 65], lhsT=Pt[:, j * 256 + hh * 128:j * 256 + (hh + 1) * 128], rhs=Va[:, f, h, :],
                            start=(j == 0), stop=(j == nb_ - 1)),
                            reads=[PtB, BVa[f]], writes=[poB], inc=(j == nb_ - 1))

            def emitFin(tl, hp=hp):
                par = tl % 2
                po, poB = ps[6 + par], Bps[6 + par]
                pov = po[:, 0:130].rearrange("p (h e) -> p h e", h=2)
                S.op("dve", lambda e: e.reciprocal(out=rc[:, 2 * par:2 * par + 2], in_=pov[:, :, 64]), reads=[poB], writes=[Brc[par]])
                S.op("dve", lambda e: e.tensor_tensor(
                    out=onb[par][:].rearrange("p (h d) -> p h d", h=2), in0=pov[:, :, 0:64],
                    in1=rc[:, 2 * par:2 * par + 2].unsqueeze(2).to_broadcast([128, 2, 64]), op=ALU.mult),
                    reads=[poB, Brc[par]], writes=[Bonb[par]])
                ptb = po[:].bitcast(BF16)[:, 512:640]
                S.op("pe", lambda e: e.transpose(out=ptb, in_=onb[par][:], identity=ident), reads=[Bonb[par], Bcst], writes=[poB])
                S.op("dve", lambda e: e.tensor_copy(out=oT[:, hp, tl * 128:(tl + 1) * 128], in_=ptb), reads=[poB], writes=[BoT[hp][tl]])

            for i in range(16 + 2):
                if i < 16:
                    emitS(i)
                if 1 <= i <= 16:
                    emitPV(i - 1)
                if i >= 2:
                    emitFin(i - 2)

        if stop <= 4:
            if dbg:
                dump(oT[:, 0, :], BoT[0], 2048)
                dump(oT[:, 3, :], BoT[3], 2048)
            finish()
            return nc

        Wf2 = reg(E_OFF, 8192).rearrange("p (a g c) -> p a g c", a=2, g=4)
        BWf2 = Buf("Wf2")
        S.claim([BWf2], [Bnab])
        for a, M in enumerate([CcM, ScM]):
            for g in range(4):
                for half in range(2):
                    p, pB = next_ps()
                    S.op("pe", lambda e, p=p, M=M, g=g, half=half: e.matmul(p[:], lhsT=M, rhs=wf[:, g, half * 512:(half + 1) * 512], start=True, stop=True),
                         reads=[Bcst, Bwf], writes=[pB])
                    S.op("dve", lambda e, p=p, a=a, g=g, half=half: e.tensor_copy(out=Wf2[:, a, g, half * 512:(half + 1) * 512], in_=p[:]),
                         reads=[pB], writes=[BWf2])
        S.claim([b for r in BomT for b in r], [Bwf])
        f1s = [reg(G_OFF, 4096), reg(C_OFF + 16384, 4096)]
        Bf1 = [Buf("f1s0"), Buf("f1s1")]
        S.claim([Bf1[1]], BVa + BqT + BkT)

        def load_f1(dc):
            s_ = (dc + 1) % 2
            base = f1s[s_]
            gw = base[:, 0:3072].rearrange("p (b k c) -> p b k c", b=3, k=8)
            wo = base[:, 3072:4096].rearrange("p (b k c) -> p b k c", b=2, k=4)
            for br in range(3):
                S.dma("pool", gw[:, br], wview(w_in)[:, :, 2560 + br * 1024 + dc * 128:2560 + br * 1024 + (dc + 1) * 128],
                      writes=[Bf1[s_]] if br == 0 else [], reads=[], key=Bf1[s_])
            S.dma("pool", wo[:, 0], w_na_o.rearrange("(k p) c -> p k c", p=128)[:, :, dc * 128:(dc + 1) * 128], key=Bf1[s_])
            tokw = S.dma("pool", wo[:, 1], w_mem_o.rearrange("(k p) c -> p k c", p=128)[:, :, dc * 128:(dc + 1) * 128], key=Bf1[s_])
            Bf1[s_].lw = tokw

        load_f1(0)

        mqs = [[reg(C_OFF + (hs * 4 + tb) * 512, 512) for tb in range(4)] for hs in range(2)]
        Bmqs = [[Buf("mq%d_%d" % (hs, tb)) for tb in range(4)] for hs in range(2)]
        S.claim([b for r in Bmqs for b in r], BVa + BqT + BkT)

        def emitDPN(h):
            wmq, BwmqB = wsl[h % 4], Bwsl[h % 4]
            S.dma("pool", wmq, wview(w_in)[:, :, 2048 + h * 128:2048 + (h + 1) * 128], writes=[BwmqB])
            items = []
            for tb in range(4):
                items.append(dict(w=wmq, wB=BwmqB, rhs_fn=(lambda k, tb=tb: hTf[:, k, 256 + tb * 512:256 + (tb + 1) * 512]),
                                  rhsBs=BhTf[2 + 4 * tb:6 + 4 * tb], mat=ones, inv_d=1.0 / 128, gain=vec2[:, 1:2], gainB=Bvec2,
                                  out=mqs[h % 2][tb], outB=Bmqs[h % 2][tb]))
            proj_norm_batch(items)

        def emitDattn(h, tb):
            mq, mqB = mqs[h % 2][tb], Bmqs[h % 2][tb]
            pts = []
            for c in range(2):
                pS, pSB = next_ps()
                S.op("pe", lambda e, pS=pS, c=c: e.matmul(pS[:], lhsT=kmT[:, h, c * 128:(c + 1) * 128], rhs=mq, start=True, stop=True),
                     reads=[BkmT, mqB], writes=[pSB])
                pt_, ptB_ = next_tb()
                S.op("act", lambda e, pS=pS, pt_=pt_: e.activation(out=pt_[:], in_=pS[:], func=AF.Exp), reads=[pSB], writes=[ptB_])
                pts.append((pt_, ptB_))
            po, poB = next_ps()
            pq, pqB = next_ps()
            for c in range(2):
                S.op("pe", lambda e, c=c: e.matmul(po[:], lhsT=vm[:, c, h * 128:(h + 1) * 128], rhs=pts[c][0][:], start=(c == 0), stop=(c == 1)),
                     reads=[Bvm, pts[c][1]], writes=[poB], inc=(c == 1))
            for c in range(2):
                S.op("pe", lambda e, c=c: e.matmul(pq[:], lhsT=ones, rhs=pts[c][0][:], start=(c == 0), stop=(c == 1)),
                     reads=[Bcst, pts[c][1]], writes=[pqB], inc=(c == 1))
            t, tB = next_tf()
            if True:
                S.op("act", lambda e: e.activation(out=t[:], in_=pq[:], func=AF.Ln), reads=[pqB], writes=[tB])
                S.op("act", lambda e: e.activation(out=t[:], in_=t[:], func=AF.Exp, scale=-1.0), reads=[tB], writes=[tB])
            else:
                S.op("dve", lambda e: e.reciprocal(out=t[:], in_=pq[:]), reads=[pqB], writes=[tB])
            S.op("dve", lambda e: e.tensor_tensor(out=omT[:, h, tb * 512:(tb + 1) * 512], in0=po[:], in1=t[:], op=ALU.mult),
                 reads=[poB, tB], writes=[BomT[h][tb]])

        emitDPN(0)
        for h in range(4):
            if h + 1 < 4:
                emitDPN(h + 1)
            for tb in range(4):
                emitDattn(h, tb)

        if stop <= 5:
            if dbg:
                dump(omT[:, 0, :], BomT[0], 2048)
                dump(omT[:, 3, :], BomT[3], 2048)
            finish()
            return nc

        mT = reg(C_OFF, 16384).rearrange("p (k t) -> p k t", k=8)
        BmT = [[Buf("mT%d_%d" % (k, t)) for t in range(4)] for k in range(8)]
        ncbufs = BVa + BqT + BkT + [b for r in Bmqs for b in r]
        S.claim([b for r in BmT for b in r], ncbufs)
        S.claim([Bf1[0]], [Bwv] + Bwsl)
        for dc in range(8):
            s = (dc + 1) % 2
            base = f1s[s]
            gw = base[:, 0:3072].rearrange("p (b k c) -> p b k c", b=3, k=8)
            wo = base[:, 3072:4096].rearrange("p (b k c) -> p b k c", b=2, k=4)
            if dc + 1 < 8:
                load_f1(dc + 1)
            for tb in range(4):
                tsl = slice(tb * 512, (tb + 1) * 512)
                acc = None
                for br in range(3):
                    pg, pgB = next_ps()
                    for k in range(8):
                        S.op("pe", lambda e, pg=pg, gw=gw, br=br, k=k, tb=tb: e.matmul(
                            pg[:], lhsT=gw[:, br, k, :], rhs=hTf[:, k, 256 + tb * 512:256 + (tb + 1) * 512], start=(k == 0), stop=(k == 7)),
                            reads=[Bf1[s]] + BhTf[2 + 4 * tb:6 + 4 * tb], writes=[pgB], inc=(k == 7))
                    py, pyB = next_ps()
                    if br == 0:
                        for k in range(4):
                            S.op("pe", lambda e, py=py, wo=wo, k=k, tsl=tsl: e.matmul(py[:], lhsT=wo[:, 0, k, :], rhs=oT[:, k, tsl], start=(k == 0), stop=(k == 3)),
                                 reads=[Bf1[s]] + BoT[k][4 * tb:4 * tb + 4], writes=[pyB], inc=(k == 3))
                    elif br == 1:
                        j = 0
                        for a in range(2):
                            for g in range(4):
                                S.op("pe", lambda e, py=py, a=a, g=g, dc=dc, tsl=tsl, j=j: e.matmul(
                                    py[:], lhsT=Wf2[:, a, g, dc * 128:(dc + 1) * 128], rhs=Zt[:, a, g, tsl], start=(j == 0), stop=(j == 7)),
                                    reads=[BWf2] + BZ[a][g], writes=[pyB], inc=(j == 7))
                                j += 1
                    else:
                        for k in range(4):
                            S.op("pe", lambda e, py=py, wo=wo, k=k, tsl=tsl: e.matmul(py[:], lhsT=wo[:, 1, k, :], rhs=omT[:, k, tsl], start=(k == 0), stop=(k == 3)),
                                 reads=[Bf1[s], BomT[k][tb]], writes=[pyB], inc=(k == 3))
                    sg, sgB = next_tf()
                    bcol = V_BG + br * 8 + dc
                    S.op("act", lambda e, sg=sg, pg=pg, bcol=bcol: e.activation(out=sg[:], in_=pg[:], func=AF.Sigmoid, bias=vec[:, bcol:bcol + 1]),
                         reads=[pgB, Bvec], writes=[sgB])
                    if br == 0:
                        S.op("dve", lambda e, sg=sg, py=py: e.tensor_tensor(out=sg[:], in0=sg[:], in1=py[:], op=ALU.mult), reads=[sgB, pyB], writes=[sgB])
                        acc, accB = sg, sgB
                    elif br == 1:
                        S.op("dve", lambda e, sg=sg, py=py: e.tensor_tensor(out=sg[:], in0=sg[:], in1=py[:], op=ALU.mult), reads=[sgB, pyB], writes=[sgB])
                        S.op("dve", lambda e, sg=sg, acc=acc: e.tensor_tensor(out=acc[:], in0=acc[:], in1=sg[:], op=ALU.add), reads=[sgB, accB], writes=[accB])
                    else:
                        S.op("dve", lambda e, sg=sg, py=py: e.tensor_tensor(out=sg[:], in0=sg[:], in1=py[:], op=ALU.mult), reads=[sgB, pyB], writes=[sgB])
                        S.op("dve", lambda e, sg=sg, acc=acc, dc=dc, tsl=tsl: e.tensor_tensor(out=mT[:, dc, tsl], in0=acc[:], in1=sg[:], op=ALU.add),
                             reads=[sgB, accB], writes=[BmT[dc][tb]])

        if stop <= 6:
            if dbg:
                dump(mT[:, 0, :], BmT[0], 2048)
                dump(mT[:, 7, :], BmT[7], 2048)
            finish()
            return nc

        wout = reg(C_OFF + 16384, 8192).rearrange("p (k c) -> p k c", k=8)
        Bwout = [Buf("wout0"), Buf("wout1")]
        S.claim([Bwout[0]], [Bf1[1]])
        S.claim([Bwout[1]], [Bf1[0]])
        S.dma("pool", wout[:, 0:4, :], wview(w_out)[:, 0:4, :], writes=[Bwout[0]])
        S.dma("pool", wout[:, 4:8, :], wview(w_out)[:, 4:8, :], writes=[Bwout[1]])
        x1 = reg(BZ_OFF, BZ_SZ + D_SZ).bitcast(F32).rearrange("p (t c) -> p t c", t=16)
        Bx1 = [[Buf("x1_%d_%d" % (t, hf)) for hf in range(2)] for t in range(16)]
        oldz = [b for r in BZ for gg in r for b in gg] + [b for r in BoT for b in r] + [b for r in BomT for b in r]
        S.claim([b for r in Bx1 for b in r], oldz)
        h2T = reg(A_OFF, 16384).rearrange("p (k t) -> p k t", k=8)
        Bh2 = [Buf("h2T%d" % t) for t in range(16)]
        S.claim(Bh2, BhTf)
        S.dma("sp", gbc[:], gbc_d[1], writes=[Bgbc])
        pend2 = {}
        NXF = 4
        xtf = [reg(E_OFF + i * 2048, 2048).bitcast(F32) for i in range(NXF)]
        Bxtf = [Buf("xtf%d" % i) for i in range(NXF)]
        S.claim(Bxtf, [BWf2])
        ctr["xtf"] = 0

        def emitX1p(j):
            srcs = []
            for t in (2 * j, 2 * j + 1):
                xi = nxt("xtf", NXF)
                S.dma("sp", xtf[xi], xr[t * 128:(t + 1) * 128, :], writes=[Bxtf[xi]])
                for hf in range(2):
                    p, pB = next_ps()
                    for k in range(8):
                        S.op("pe", lambda e, p=p, k=k, hf=hf, t=t: e.matmul(p[:], lhsT=mT[:, k, t * 128:(t + 1) * 128], rhs=wout[:, k, hf * 512:(hf + 1) * 512],
                                                                         start=(k == 0), stop=(k == 7)),
                             reads=[BmT[k][t // 4], Bwout[k // 4]], writes=[pB], inc=(k == 7))
                    S.op("dve", lambda e, p=p, hf=hf, xi=xi, t=t: e.tensor_tensor(out=x1[:, t, hf * 512:(hf + 1) * 512], in0=p[:], in1=xtf[xi][:, hf * 512:(hf + 1) * 512], op=ALU.add),
                         reads=[pB, Bxtf[xi]], writes=[Bx1[t][hf]])
                srcs.append((x1[:, t, :], Bx1[t]))
            xis = tok_norm_a2(srcs)
            pend2[2 * j], pend2[2 * j + 1] = xis

        def emitT2p2(j):
            for q_, t in enumerate((2 * j, 2 * j + 1)):
                tok_norm_b(pend2.pop(t), V_G2, h2T[:, :, t * 128:(t + 1) * 128], Bh2[t], ev=q_)

        emitX1p(0)
        for j in range(8):
            if j + 1 < 8:
                emitX1p(j + 1)
            emitT2p2(j)

        if stop <= 7:
            if dbg:
                dump(x1[:, 0, :], Bx1[0], 1024)
                dump(h2T[:, 0, :], Bh2, 2048)
            finish()
            return nc

        w1s = [reg(C_OFF + s * 8192, 4096).rearrange("p (k c) -> p k c", k=8) for s in range(2)]
        w2s = [reg(C_OFF + s * 8192 + 4096, 4096).rearrange("p (k c) -> p k c", k=4) for s in range(2)]
        Bw1 = [Buf("w1_%d" % s) for s in range(2)]
        Bw2 = [Buf("w2_%d" % s) for s in range(2)]
        S.claim(Bw1 + Bw2, [b for r in BmT for b in r])
        aTs = [reg(C_OFF + 16384, 8192).rearrange("p (k t) -> p k t", k=4), reg(E_OFF, 8192).rearrange("p (k t) -> p k t", k=4)]
        BaT = [[[Buf("aT%d_%d_%d" % (s, k, t)) for t in range(4)] for k in range(4)] for s in range(2)]
        S.claim([b for r in BaT[0] for b in r], Bwout)
        S.claim([b for r in BaT[1] for b in r], [BWf2] + Bxtf)
        NG = 8
        for grp in range(NG):
            s = grp % 2
            S.dma("pool", w1s[s], wview(w_ff1)[:, :, grp * 512:(grp + 1) * 512], writes=[Bw1[s]])
            S.dma("pool", w2s[s], w_ff2[grp * 512:(grp + 1) * 512, :].rearrange("(k p) c -> p k c", p=128), writes=[Bw2[s]])
            aT = aTs[s]
            for fc in range(4):
                for tb in range(4):
                    p, pB = next_ps()
                    for k in range(8):
                        S.op("pe", lambda e, p=p, k=k, fc=fc, tb=tb, s=s: e.matmul(p[:], lhsT=w1s[s][:, k, fc * 128:(fc + 1) * 128],
                                                                                 rhs=h2T[:, k, tb * 512:(tb + 1) * 512], start=(k == 0), stop=(k == 7)),
                             reads=[Bw1[s]] + Bh2[4 * tb:4 * tb + 4], writes=[pB], inc=(k == 7))
                    t, tB = next_tf()
                    S.op("act", lambda e, t=t, p=p: e.activation(out=t[:], in_=p[:], func=AF.Relu), reads=[pB], writes=[tB])
                    S.op("act", lambda e, t=t, aT=aT, fc=fc, tb=tb: e.activation(out=aT[:, fc, tb * 512:(tb + 1) * 512], in_=t[:], func=AF.Square),
                         reads=[tB], writes=[BaT[s][fc][tb]])
            for t in range(16):
                for hf in range(2):
                    p, pB = next_ps()
                    for fc in range(4):
                        S.op("pe", lambda e, p=p, fc=fc, t=t, hf=hf, s=s, aT=aT: e.matmul(p[:], lhsT=aT[:, fc, t * 128:(t + 1) * 128],
                                                                                      rhs=w2s[s][:, fc, hf * 512:(hf + 1) * 512], start=(fc == 0), stop=(fc == 3)),
                             reads=[BaT[s][fc][t // 4], Bw2[s]], writes=[pB], inc=(fc == 3))
                    S.op("dve", lambda e, p=p, t=t, hf=hf: e.tensor_tensor(out=x1[:, t, hf * 512:(hf + 1) * 512], in0=p[:], in1=x1[:, t, hf * 512:(hf + 1) * 512], op=ALU.add),
                         reads=[pB, Bx1[t][hf]], writes=[Bx1[t][hf]])
                if grp == NG - 1:
                    S.dma("sp", y[t * 128:(t + 1) * 128, :], x1[:, t, :], reads=Bx1[t], writes=[By[t]])
        finish()
    return nc


def _na_table(rpb, hf):
    rpb = np.asarray(rpb, np.float32)
    tab = np.full((8, NA_NCH, 128, 128), NEG, np.float32)
    kr2 = np.arange(128) // 64
    kc = np.arange(128) % 64
    qr2 = np.arange(128) // 64
    qc = np.arange(128) % 64
    cs = np.clip(qc - 8, 0, 48)
    tls = [0, 1, 2, 14, 15]
    for vi, tl in enumerate(tls):
        b0, nb_, voff = na_band(tl)
        t = 16 * hf + tl
        qrow = 2 * t + qr2
        rs = np.clip(qrow - 4, 0, 56)
        for j in range(nb_):
            f = b0 + j
            lt = (f - 2) % 32
            real = (2 <= f < 18) or (f < 2 and hf == 1) or (f >= 18 and hf == 0)
            if not real:
                continue
            gt = (lt + 16 * hf) % 32
            krow = 2 * gt + kr2
            dr = krow[:, None] - qrow[None, :]
            rowok = (krow[:, None] >= rs[None, :]) & (krow[:, None] < rs[None, :] + 8)
            colok = (kc[:, None] >= cs[None, :]) & (kc[:, None] < cs[None, :] + 16)
            ok = rowok & colok
            dri = np.clip(dr + 7, 0, 14)
            dci = np.clip(kc[:, None] - qc[None, :], -15, 15) + 15
            g = rpb[:, dri, dci]
            tab[:, voff + j] = np.where(ok[None], g, NEG)
    tab = tab.reshape(4, 2, NA_NCH, 128, 128).transpose(0, 3, 2, 1, 4).reshape(4, 128, 2 * NA_NCH, 128)
    return np.ascontiguousarray(tab)


def _fft_tables(hf):
    m = np.arange(1024, dtype=np.int64)
    out = np.zeros((4, 2, 1024, 512), np.float64)
    w = np.arange(512, dtype=np.int64)
    for v in range(4):
        sp = 2048 * hf + 4 * w + v
        ph = (m[:, None] * sp[None, :]) % 4096
        ang = 2.0 * np.pi * ph / 4096.0
        sign = -1.0 if (hf == 1 and v % 2 == 1) else 1.0
        out[v, 0] = sign * np.cos(ang)
        out[v, 1] = -sign * np.sin(ang)
    out = out.reshape(4, 2, 8, 128, 512).transpose(0, 3, 1, 2, 4)
    return np.ascontiguousarray(out.astype(np.float32)).astype(ml_dtypes.bfloat16)


def _consts():
    c = np.zeros((128, 5, 128), np.float64)
    c[:, 0] = np.eye(128)
    blk = np.zeros((128, 128))
    blk[:64, :64] = 1.0
    blk[64:, 64:] = 1.0
    c[:, 1] = blk
    c[:, 2] = 1.0
    i = np.arange(128)
    ang = 2.0 * np.pi * ((i[:, None] * i[None, :]) % 128) / 128.0
    nrm = 1.0 / np.sqrt(4096.0 * 128.0)
    c[:, 3] = np.cos(ang) * nrm
    c[:, 4] = np.sin(ang) * nrm
    return c.astype(np.float32).astype(ml_dtypes.bfloat16)


def _vecs(norm1_g, norm2_g, mem_norm_g, b_gate, na_q_g, na_k_g, mem_q_g, mem_k_g):
    v = np.zeros((128, NVEC), np.float32)
    v[:, V_G1:V_G1 + 8] = np.asarray(norm1_g, np.float32).reshape(8, 128).T
    v[:, V_G2:V_G2 + 8] = np.asarray(norm2_g, np.float32).reshape(8, 128).T
    v[:, V_GM:V_GM + 8] = np.asarray(mem_norm_g, np.float32).reshape(8, 128).T
    v[:, V_BG:V_BG + 24] = np.asarray(b_gate, np.float32).reshape(24, 128).T
    v[:, V_QG] = np.tile(np.asarray(na_q_g, np.float32), 2)
    v[:, V_KG] = np.tile(np.asarray(na_k_g, np.float32), 2)
    v[:, V_MQG] = np.asarray(mem_q_g, np.float32)
    v[:, V_MKG] = np.asarray(mem_k_g, np.float32)
    return v


def make_in_maps(x, mem, norm1_g, w_in, b_gate, na_q_g, na_k_g, na_rpb, w_na_o, w_f,
                 mem_norm_g, w_mem_kv, mem_q_g, mem_k_g, w_mem_o, w_out, norm2_g, w_ff1, w_ff2):
    f = lambda a: np.ascontiguousarray(np.asarray(a, np.float32))
    x = f(x)
    mem = f(mem)
    shared = {
        "w_in": f(w_in), "w_na_o": f(w_na_o), "w_f": f(w_f), "w_mem_o": f(w_mem_o), "w_mem_kv": f(w_mem_kv),
        "w_out": f(w_out), "w_ff1": f(w_ff1), "w_ff2": f(w_ff2),
        "vecs": _vecs(norm1_g, norm2_g, mem_norm_g, b_gate, na_q_g, na_k_g, mem_q_g, mem_k_g),
        "consts": _consts(),
        "gbc": np.ascontiguousarray(np.stack([np.broadcast_to(np.asarray(g, np.float32)[None, :], (128, D))
                                              for g in (norm1_g, norm2_g, mem_norm_g)])),
    }
    nat = [_na_table(na_rpb, hf) for hf in range(2)]
    fft = [_fft_tables(hf) for hf in range(2)]
    maps = []
    for c in range(8):
        b, hf = c // 2, c % 2
        m = dict(shared)
        m["xr"] = np.ascontiguousarray(np.roll(x[b], -NOWN * hf, axis=0))
        m["mem"] = mem[b]
        m["nabias"] = nat[hf]
        m["fftab"] = fft[hf]
        maps.append(m)
    return maps


_NC_CACHE = {}


def kernel(**inputs):
    maps = make_in_maps(**inputs)
    if "nc" not in _NC_CACHE:
        _NC_CACHE["nc"] = build()
    nc = _NC_CACHE["nc"]
    res = run_bass_kernel_spmd(nc, maps, core_ids=list(range(8)))
    out = np.zeros((4, SEQ, D), np.float32)
    for c in range(8):
        b, hf = c // 2, c % 2
        out[b, NOWN * hf:NOWN * (hf + 1)] = res.results[c]["y"]
    return out
```

```python
from contextlib import ExitStack

import numpy as np
import ml_dtypes

import concourse.bass as bass
import concourse.mybir as mybir
from concourse.bass_utils import run_bass_kernel_spmd

F32 = mybir.dt.float32
BF16 = mybir.dt.bfloat16
AF = mybir.ActivationFunctionType
ALU = mybir.AluOpType

SAME_ENG_SYNC = True
EPS = 1e-6
NEG = -1e30


class Buf:
    __slots__ = ("name", "lw", "rds", "dsem", "dcnt")

    def __init__(self, name):
        self.name = name
        self.lw = None
        self.rds = {}
        self.dsem = None
        self.dcnt = 0


class EngQ:
    def __init__(self, name, sem, is_pe=False):
        self.name = name
        self.sem = sem
        self.n = 0
        self.waited = {}
        self.ops = []
        self.is_pe = is_pe


class Sched:
    def __init__(self, nc, stack):
        self.nc = nc
        self.stack = stack
        self.q = {}
        for name in ("pe", "act", "dve", "pool", "sp"):
            sem = stack.enter_context(nc.semaphore("q_" + name))
            self.q[name] = EngQ(name, sem, is_pe=(name == "pe"))
        self.nsem = 5
        self.free_sems = []

    def newsem(self, name):
        self.nsem += 1
        return self.stack.enter_context(self.nc.semaphore(name))

    def _collect(self, q, reads, writes):
        deps = {}

        def add(t):
            if t is None:
                return
            sem, val = t
            if sem is q.sem and (q.is_pe or not SAME_ENG_SYNC):
                return
            k = id(sem)
            if q.waited.get(k, 0) >= val:
                return
            if k not in deps or deps[k][1] < val:
                deps[k] = (sem, val)

        for b in reads:
            add(b.lw)
        for b in writes:
            add(b.lw)
            for t in b.rds.values():
                add(t)
        out = list(deps.values())
        for sem, val in out:
            q.waited[id(sem)] = val
        return out

    def _update(self, tok, reads, writes):
        for b in writes:
            b.lw = tok
            b.rds = {}
        k = id(tok[0])
        for b in reads:
            if k not in b.rds or b.rds[k][1] < tok[1]:
                b.rds[k] = tok

    def op(self, qn, fn, reads=(), writes=(), inc=True):
        q = self.q[qn]
        waits = self._collect(q, reads, writes)
        tok = (q.sem, q.n + 1)
        if inc:
            q.n += 1
        else:
            assert q.is_pe
        q.ops.append((waits, fn, (q.sem, 1) if inc else None))
        self._update(tok, reads, writes)

    def dma(self, qn, out, in_, reads=(), writes=(), key=None, **kw):
        q = self.q[qn]
        waits = self._collect(q, reads, writes)
        kb = key if key is not None else (writes[0] if writes else reads[0])
        if kb.dsem is None:
            kb.dsem = self.newsem("d_" + kb.name)
        kb.dcnt += 16
        tok = (kb.dsem, kb.dcnt)

        def fn(e, out=out, in_=in_, kw=kw):
            return e.dma_start(out=out, in_=in_, **kw)

        q.ops.append((waits, fn, (kb.dsem, 16)))
        self._update(tok, reads, writes)
        return tok

    def claim(self, new_bufs, old_bufs):
        toks = {}
        for b in old_bufs:
            for t in ([b.lw] if b.lw else []) + list(b.rds.values()):
                k = id(t[0])
                if k not in toks or toks[k][1] < t[1]:
                    toks[k] = t
        for nb in new_bufs:
            for k, t in toks.items():
                if k not in nb.rds or nb.rds[k][1] < t[1]:
                    nb.rds[k] = t

    def wait_all(self, qn, bufs):
        q = self.q[qn]
        waits = self._collect(q, (), bufs)
        q.ops.append((waits, None, None))

    def emit(self):
        nc = self.nc
        qs = self.q

        def run(q, e):
            for waits, fn, inc in q.ops:
                for sem, val in waits:
                    e.wait_ge(sem, val)
                if fn is None:
                    continue
                ins = fn(e)
                if inc is not None:
                    ins.then_inc(inc[0], inc[1])

        with nc.Block() as block:
            @block.tensor
            def _(e):
                run(qs["pe"], e)

            @block.scalar
            def _(e):
                run(qs["act"], e)

            @block.vector
            def _(e):
                run(qs["dve"], e)

            @block.gpsimd
            def _(e):
                run(qs["pool"], e)

            @block.sync
            def _(e):
                run(qs["sp"], e)


D = 1024
SEQ = 4096
NOWN = 2048
NFR = 20
NA_VARIANTS = [(0, 6), (1, 5), (2, 5), (14, 5), (14, 6)]
NA_VOFF = [0, 6, 11, 16, 21]
NA_NCH = 27


def na_band(tl):
    if tl == 0:
        return 0, 6, NA_VOFF[0]
    if tl == 1:
        return 1, 5, NA_VOFF[1]
    if tl == 14:
        return 14, 5, NA_VOFF[3]
    if tl == 15:
        return 14, 6, NA_VOFF[4]
    return tl, 5, NA_VOFF[2]


V_G1, V_G2, V_GM, V_BG, V_QG, V_KG, V_MQG, V_MKG = 0, 8, 16, 24, 48, 49, 50, 51
NVEC = 64

A_OFF, A_SZ = 0, 20480
BZ_OFF, BZ_SZ = 20480, 16384
D_OFF, D_SZ = 36864, 16384
C_OFF, C_SZ = 53248, 20480
G_OFF, G_SZ = 73728, 4096
E_OFF, E_SZ = 77824, 8192
ARENA = 86016


def build(stop=99, dbg=False):
    nc = bass.Bass("TRN2", target_bir_lowering=False)

    def dram(n, s, d, kind="ExternalInput"):
        return nc.dram_tensor(n, s, d, kind=kind).ap()

    xr = dram("xr", [SEQ, D], F32)
    mem = dram("mem", [256, D], F32)
    w_in = dram("w_in", [D, 5632], F32)
    w_na_o = dram("w_na_o", [512, D], F32)
    w_f = dram("w_f", [512, D], F32)
    w_mem_o = dram("w_mem_o", [512, D], F32)
    w_mem_kv = dram("w_mem_kv", [D, D], F32)
    w_out = dram("w_out", [D, D], F32)
    w_ff1 = dram("w_ff1", [D, 4096], F32)
    w_ff2 = dram("w_ff2", [4096, D], F32)
    vecs_d = dram("vecs", [128, NVEC], F32)
    nab_d = dram("nabias", [4, 128, 2 * NA_NCH, 128], F32)
    fft_d = dram("fftab", [4, 128, 2, 8, 512], BF16)
    cst_d = dram("consts", [128, 5, 128], BF16)
    gbc_d = dram("gbc", [3, 128, D], F32)
    y = dram("y", [NOWN, D], F32, kind="ExternalOutput")
    if dbg:
        dbg_d = dram("dbg", [128, 8192], F32, kind="ExternalOutput")

    def wview(w):
        return w.rearrange("(k p) c -> p k c", p=128)

    st = ExitStack()
    with st:
        S = Sched(nc, st)

        def sb(n, s, d):
            return st.enter_context(nc.sbuf_tensor(n, s, d))

        ar = sb("arena", [128, ARENA], BF16)

        def reg(off, n):
            return ar[:, off:off + n]

        cst = sb("cst", [128, 5, 128], BF16)
        vec = sb("vec", [128, NVEC], F32)
        vec2 = sb("vec2", [128, 4], F32)
        stat = sb("stat", [128, 64], F32)
        rstd = sb("rstd", [128, 64], F32)
        gbc = sb("gbcsb", [128, D], F32)
        Bgbc = Buf("gbc")
        NXA = 6
        xt = [reg(D_OFF + i * 2048, 2048).bitcast(F32) for i in range(NXA)]
        xs = [sb("xs%d" % i, [128, D], BF16) for i in range(4)]
        tmpf = [sb("tmpf%d" % i, [128, 512], F32) for i in range(5)]
        tmpb = [sb("tmpb%d" % i, [128, 512], BF16) for i in range(4)]
        PT = [sb("PT%d" % i, [128, 1536], BF16) for i in range(2)]
        kmT = sb("kmT", [128, 4, 256], BF16)
        vm = sb("vm", [128, 2, 512], BF16)
        onb = [sb("onb%d" % i, [128, 128], BF16) for i in range(2)]
        rc = sb("rc", [128, 4], F32)
        ps = [st.enter_context(nc.psum_tensor("ps%d" % i, [128, 512], F32)) for i in range(8)]

        Bcst, Bvec, Bvec2 = Buf("cst"), Buf("vec"), Buf("vec2")
        Bxt = [Buf("xt%d" % i) for i in range(NXA)]
        Bxs = [Buf("xs%d" % i) for i in range(4)]
        Btf = [Buf("tmpf%d" % i) for i in range(5)]
        Btb = [Buf("tmpb%d" % i) for i in range(4)]
        BPT = [Buf("PT%d" % i) for i in range(2)]
        Bps = [Buf("ps%d" % i) for i in range(8)]
        BkmT, Bvm = Buf("kmT"), Buf("vm")
        Bonb = [Buf("onb%d" % i) for i in range(2)]
        Brc = [Buf("rc%d" % i) for i in range(2)]
        Bstat = [Buf("stat%d" % i) for i in range(64)]

        ident = cst[:, 0, :]
        blk64 = cst[:, 1, :]
        ones = cst[:, 2, :]
        CcM = cst[:, 3, :]
        ScM = cst[:, 4, :]

        ctr = {"ps": 0, "tf": 0, "tb": 0, "xt": 0, "xs": 0, "st": 0, "ev": 0, "st2": 0}

        def nxt(key, n):
            i = ctr[key] % n
            ctr[key] += 1
            return i

        def next_ps():
            i = nxt("ps", 8)
            return ps[i], Bps[i]

        def next_tf():
            i = nxt("tf", 5)
            return tmpf[i], Btf[i]

        def next_tb():
            i = nxt("tb", 4)
            return tmpb[i], Btb[i]

        def next_stat():
            i = nxt("st", 32)
            return i, Bstat[i]

        S.dma("sp", cst[:], cst_d, writes=[Bcst])
        S.dma("sp", vec[:], vecs_d, writes=[Bvec])
        S.op("dve", lambda e: e.tensor_scalar(out=vec2[:, 0:1], in0=vec[:, V_QG:V_QG + 1], scalar1=0.125, scalar2=None, op0=ALU.mult),
             reads=[Bvec], writes=[Bvec2])
        S.op("dve", lambda e: e.tensor_scalar(out=vec2[:, 1:2], in0=vec[:, V_MQG:V_MQG + 1], scalar1=float(128 ** -0.5), scalar2=None, op0=ALU.mult),
             reads=[Bvec], writes=[Bvec2])

        junk = PT[0][:, 0:D]

        def tok_norm_a(src_ap, srcBs):
            si, sB = next_stat()
            xi = nxt("xs", 4)
            S.op("act", lambda e: e.activation(out=junk, in_=src_ap, func=AF.Square, accum_out=stat[:, si:si + 1]),
                 reads=srcBs, writes=[sB, BPT[0]])
            S.op("act", lambda e: e.activation(out=stat[:, si:si + 1], in_=stat[:, si:si + 1], func=AF.Sqrt, scale=1.0 / D, bias=EPS),
                 reads=[sB], writes=[sB])
            S.op("dve", lambda e: e.reciprocal(out=rstd[:, si:si + 1], in_=stat[:, si:si + 1]), reads=[sB], writes=[sB])
            S.op("dve", lambda e: e.scalar_tensor_tensor(out=xs[xi][:], in0=src_ap, scalar=rstd[:, si:si + 1], in1=gbc[:],
                                                          op0=ALU.mult, op1=ALU.mult),
                 reads=list(srcBs) + [sB, Bgbc], writes=[Bxs[xi]])
            return xi

        def tok_norm_b(xi, gcol, dst_fn, dstB, bank=None, ev=None):
            p, pB = next_ps() if bank is None else (ps[bank], Bps[bank])
            pb = p[:].bitcast(BF16)
            for k in range(8):
                S.op("pe", lambda e, k=k: e.transpose(out=pb[:, k * 128:(k + 1) * 128], in_=xs[xi][:, k * 128:(k + 1) * 128], identity=ident),
                     reads=[Bxs[xi], Bcst], writes=[pB], inc=(k == 7))
            if ev is None:
                ev = nxt("ev", 2)
            if ev == 0:
                S.op("act", lambda e: e.activation(out=dst_fn, in_=pb.rearrange("p (k t) -> p k t", k=8), func=AF.Copy),
                     reads=[pB], writes=[dstB])
            else:
                S.op("dve", lambda e: e.tensor_copy(out=dst_fn, in_=pb.rearrange("p (k t) -> p k t", k=8)), reads=[pB], writes=[dstB])

        def tok_norm_a2(srcs):
            c = 32 + 2 * nxt("st2", 16)
            sB = Bstat[c]
            for j, (src_ap, srcBs) in enumerate(srcs):
                S.op("act", lambda e, src_ap=src_ap, j=j: e.activation(out=junk, in_=src_ap, func=AF.Square, accum_out=stat[:, c + j:c + j + 1]),
                     reads=srcBs, writes=[sB, BPT[0]])
            S.op("act", lambda e: e.activation(out=stat[:, c:c + 2], in_=stat[:, c:c + 2], func=AF.Sqrt, scale=1.0 / D, bias=EPS),
                 reads=[sB], writes=[sB])
            S.op("dve", lambda e: e.reciprocal(out=rstd[:, c:c + 2], in_=stat[:, c:c + 2]), reads=[sB], writes=[sB])
            xis = []
            for j, (src_ap, srcBs) in enumerate(srcs):
                xi = nxt("xs", 4)
                S.op("dve", lambda e, src_ap=src_ap, j=j, xi=xi: e.scalar_tensor_tensor(out=xs[xi][:], in0=src_ap, scalar=rstd[:, c + j:c + j + 1], in1=gbc[:],
                                                                                     op0=ALU.mult, op1=ALU.mult),
                     reads=list(srcBs) + [sB, Bgbc], writes=[Bxs[xi]])
                xis.append(xi)
            return xis

        def tok_norm_transpose(src_ap, srcB, gcol, dst_fn, dstB):
            xi = tok_norm_a(src_ap, [srcB])
            tok_norm_b(xi, gcol, dst_fn, dstB)

        def fm_norm(p, pB, ncols, mat, inv_d, gain_ap, gainB, out_ap, outB, split=None, split3=False):
            sq, sqB = next_tb()
            S.op("act", lambda e: e.activation(out=sq[:, :ncols], in_=p[:, :ncols], func=AF.Square), reads=[pB], writes=[sqB])
            p2, p2B = next_ps()
            S.op("pe", lambda e: e.matmul(p2[:, :ncols], lhsT=mat, rhs=sq[:, :ncols], start=True, stop=True),
                 reads=[sqB, Bcst], writes=[p2B])
            t, tB = next_tf()
            S.op("act", lambda e: e.activation(out=t[:, :ncols], in_=p2[:, :ncols], func=AF.Ln, scale=inv_d, bias=EPS),
                 reads=[p2B], writes=[tB])
            S.op("act", lambda e: e.activation(out=t[:, :ncols], in_=t[:, :ncols], func=AF.Exp, scale=-0.5), reads=[tB], writes=[tB])
            if split is None:
                S.op("dve", lambda e: e.scalar_tensor_tensor(out=out_ap, in0=p[:, :ncols], scalar=gain_ap, in1=t[:, :ncols],
                                                              op0=ALU.mult, op1=ALU.mult),
                     reads=[pB, tB, gainB], writes=[outB])
            else:
                for (pr, oap) in split:
                    i0 = p[pr, :ncols]
                    i1 = t[pr, :ncols]
                    if split3:
                        i0 = i0.rearrange("p (t q) -> p t q", q=128)
                        i1 = i1.rearrange("p (t q) -> p t q", q=128)
                    S.op("dve", lambda e, pr=pr, oap=oap, i0=i0, i1=i1: e.scalar_tensor_tensor(out=oap, in0=i0, scalar=gain_ap[pr], in1=i1,
                                                                                             op0=ALU.mult, op1=ALU.mult),
                         reads=[pB, tB, gainB], writes=[outB])

        def proj_fm(wt, wB, ncolchunk, rhs_fn, rhsBs, p, pB, ncols):
            for k in range(8):
                S.op("pe", lambda e, k=k: e.matmul(p[:, :ncols], lhsT=wt[:, k, ncolchunk], rhs=rhs_fn(k), start=(k == 0), stop=(k == 7)),
                     reads=[wB] + list(rhsBs), writes=[pB], inc=(k == 7))

        def proj_norm_batch(items):
            n = len(items)
            P = []
            for it in items:
                p, pB = next_ps()
                proj_fm(it["w"], it["wB"], slice(0, 128), it["rhs_fn"], it["rhsBs"], p, pB, 512)
                P.append((p, pB))
            SQ = []
            for it, (p, pB) in zip(items, P):
                sq, sqB = next_tb()
                S.op("act", lambda e, sq=sq, p=p: e.activation(out=sq[:], in_=p[:], func=AF.Square), reads=[pB], writes=[sqB])
                SQ.append((sq, sqB))
            P2 = []
            for it, (sq, sqB) in zip(items, SQ):
                p2, p2B = next_ps()
                S.op("pe", lambda e, p2=p2, sq=sq, it=it: e.matmul(p2[:], lhsT=it["mat"], rhs=sq[:], start=True, stop=True),
                     reads=[sqB, Bcst], writes=[p2B])
                P2.append((p2, p2B))
            T = []
            for it, (p2, p2B) in zip(items, P2):
                t, tB = next_tf()
                S.op("act", lambda e, t=t, p2=p2, it=it: e.activation(out=t[:], in_=p2[:], func=AF.Ln, scale=it["inv_d"], bias=EPS),
                     reads=[p2B], writes=[tB])
                S.op("act", lambda e, t=t: e.activation(out=t[:], in_=t[:], func=AF.Exp, scale=-0.5), reads=[tB], writes=[tB])
                T.append((t, tB))
            for it, (p, pB), (t, tB) in zip(items, P, T):
                if it.get("split") is None:
                    S.op("dve", lambda e, it=it, p=p, t=t: e.scalar_tensor_tensor(out=it["out"], in0=p[:], scalar=it["gain"], in1=t[:],
                                                                                  op0=ALU.mult, op1=ALU.mult),
                         reads=[pB, tB, it["gainB"]], writes=[it["outB"]])
                else:
                    for (pr, oap) in it["split"]:
                        i0 = p[pr, :].rearrange("p (t q) -> p t q", q=128)
                        i1 = t[pr, :].rearrange("p (t q) -> p t q", q=128)
                        S.op("dve", lambda e, it=it, pr=pr, oap=oap, i0=i0, i1=i1: e.scalar_tensor_tensor(
                            out=oap, in0=i0, scalar=it["gain"][pr], in1=i1, op0=ALU.mult, op1=ALU.mult),
                            reads=[pB, tB, it["gainB"]], writes=[it["outB"]])

        dbg_off = [0]

        def dump(ap, B, n):
            for c0 in range(0, n, 512):
                w = min(512, n - c0)
                t, tB = next_tf()
                S.op("act", lambda e, c0=c0, w=w, t=t: e.activation(out=t[:, :w], in_=ap[:, c0:c0 + w], func=AF.Copy), reads=B, writes=[tB])
                o = dbg_off[0]
                S.dma("sp", dbg_d[:, o:o + w], t[:, :w], reads=[tB], writes=[Bdbg])
                dbg_off[0] += w

        Bdbg = Buf("dbg")
        By = [Buf("y%d" % i) for i in range(16)]

        def finish():
            S.wait_all("sp", By + [Bdbg])
            S.emit()

        wkv = reg(C_OFF, 8192).rearrange("p (k c) -> p k c", k=8)
        Bwkv = Buf("wkv")
        S.dma("pool", wkv, wview(w_mem_kv), writes=[Bwkv])
        S.dma("sp", gbc[:], gbc_d[2], writes=[Bgbc])
        memT = reg(BZ_OFF + 12288, 2048).rearrange("p (k t) -> p k t", k=8)
        BmemT = [Buf("memT%d" % i) for i in range(2)]
        for mt in range(2):
            xi = nxt("xt", NXA)
            S.dma("sp", xt[xi], mem[mt * 128:(mt + 1) * 128, :], writes=[Bxt[xi]])
            tok_norm_transpose(xt[xi], Bxt[xi], V_GM, memT[:, :, mt * 128:(mt + 1) * 128], BmemT[mt])
        for h in range(4):
            p, pB = next_ps()
            proj_fm(wkv, Bwkv, slice(h * 128, (h + 1) * 128), lambda k: memT[:, k, :], BmemT, p, pB, 256)
            fm_norm(p, pB, 256, ones, 1.0 / 128, vec[:, V_MKG:V_MKG + 1], Bvec, kmT[:, h, :], BkmT)
        for c in range(2):
            p, pB = next_ps()
            for k in range(8):
                S.op("pe", lambda e, k=k, c=c, p=p: e.matmul(p[:], lhsT=memT[:, k, c * 128:(c + 1) * 128], rhs=wkv[:, k, 512:1024],
                                                          start=(k == 0), stop=(k == 7)),
                     reads=[BmemT[c], Bwkv], writes=[pB], inc=(k == 7))
            S.op("act", lambda e, c=c, p=p: e.activation(out=vm[:, c, :], in_=p[:], func=AF.Copy), reads=[pB], writes=[Bvm])

        hTf = reg(A_OFF, A_SZ).rearrange("p (k t) -> p k t", k=8)
        hTo = reg(BZ_OFF, 12288).rearrange("p (k t) -> p k t", k=8)
        BhT = [Buf("hT%d" % i) for i in range(32)]

        def hT_tile(lt):
            f = (lt + 2) % 32
            if f < NFR:
                return hTf[:, :, f * 128:(f + 1) * 128]
            return hTo[:, :, (lt - 18) * 128:(lt - 17) * 128]

        BhTf = [BhT[(f - 2) % 32] for f in range(NFR)]

        S.dma("sp", gbc[:], gbc_d[0], writes=[Bgbc])
        wu = reg(G_OFF, 4096).rearrange("p (k c) -> p k c", k=8)
        Bwu = Buf("wu")
        S.dma("pool", wu, wview(w_in)[:, :, 1536:2048], writes=[Bwu])

        var = reg(C_OFF, C_SZ).rearrange("p (v i c) -> p v i c", v=5, i=8)
        Bvar = [[Buf("var%d_%d" % (v, i)) for i in range(8)] for v in range(5)]
        S.claim([b for r in Bvar for b in r], [Bwkv])

        order = [8 * e4 + i for i in range(8) for e4 in range(4)]
        pend = {}

        def emitN(n):
            lt = order[n]
            xi = nxt("xt", NXA)
            S.dma("sp", xt[xi], xr[lt * 128:(lt + 1) * 128, :], writes=[Bxt[xi]])
            pend[n] = tok_norm_a(xt[xi], [Bxt[xi]])

        def emitT(n, bank=None):
            lt = order[n]
            tok_norm_b(pend.pop(n), V_G1, hT_tile(lt), BhT[lt], bank=bank)

        def emit_combos(i, U):
            t0, t0B = next_tf()
            t1, t1B = next_tf()
            S.op("act", lambda e: e.activation(out=t0[:], in_=U[0][0][:], func=AF.Copy), reads=[U[0][1]], writes=[t0B])
            S.op("act", lambda e: e.activation(out=t1[:], in_=U[1][0][:], func=AF.Copy), reads=[U[1][1]], writes=[t1B])
            fa, faB = next_tf()
            fb, fbB = next_tf()
            S.op("dve", lambda e: e.tensor_tensor(out=fa[:], in0=t0[:], in1=U[2][0][:], op=ALU.add), reads=[t0B, U[2][1]], writes=[faB])
            S.op("dve", lambda e: e.tensor_tensor(out=var[:, 2, i, :], in0=t0[:], in1=U[2][0][:], op=ALU.subtract),
                 reads=[t0B, U[2][1]], writes=[Bvar[2][i]])
            S.op("dve", lambda e: e.tensor_tensor(out=fb[:], in0=t1[:], in1=U[3][0][:], op=ALU.add), reads=[t1B, U[3][1]], writes=[fbB])
            S.op("dve", lambda e: e.tensor_tensor(out=var[:, 3, i, :], in0=t1[:], in1=U[3][0][:], op=ALU.subtract),
                 reads=[t1B, U[3][1]], writes=[Bvar[3][i]])
            S.op("dve", lambda e: e.tensor_tensor(out=var[:, 0, i, :], in0=fa[:], in1=fb[:], op=ALU.add), reads=[faB, fbB], writes=[Bvar[0][i]])
            S.op("dve", lambda e: e.tensor_tensor(out=var[:, 1, i, :], in0=fa[:], in1=fb[:], op=ALU.subtract), reads=[faB, fbB], writes=[Bvar[1][i]])
            S.op("dve", lambda e: e.tensor_scalar(out=var[:, 4, i, :], in0=var[:, 3, i, :], scalar1=-1.0, scalar2=None, op0=ALU.mult),
                 reads=[Bvar[3][i]], writes=[Bvar[4][i]])

        Upend = []

        def emit_uproj(n):
            i, e4 = n // 4, n % 4
            lt = order[n]
            bk = 4 * (i % 2) + e4
            p, pB = ps[bk], Bps[bk]
            hv = hT_tile(lt)
            for k in range(8):
                S.op("pe", lambda e, k=k, p=p, hv=hv: e.matmul(p[:], lhsT=hv[:, k, :], rhs=wu[:, k, :], start=(k == 0), stop=(k == 7)),
                     reads=[BhT[lt], Bwu], writes=[pB], inc=(k == 7))
            if e4 == 3:
                Upend.append((i, [(ps[4 * (i % 2) + j], Bps[4 * (i % 2) + j]) for j in range(4)]))

        def emitN2(j):
            srcs = []
            for n in (2 * j, 2 * j + 1):
                lt = order[n]
                xi = nxt("xt", NXA)
                S.dma("sp", xt[xi], xr[lt * 128:(lt + 1) * 128, :], writes=[Bxt[xi]])
                srcs.append((xt[xi], [Bxt[xi]]))
            xis = tok_norm_a2(srcs)
            pend[2 * j], pend[2 * j + 1] = xis

        def emitT2p(j):
            for q_, n in enumerate((2 * j, 2 * j + 1)):
                i, e4 = n // 4, n % 4
                lt = order[n]
                tok_norm_b(pend.pop(n), V_G1, hT_tile(lt), BhT[lt], bank=4 * (i % 2) + e4, ev=q_)

        emitN2(0)
        for j in range(16):
            if j + 1 < 16:
                emitN2(j + 1)
            emitT2p(j)
            if stop >= 2:
                if j >= 1:
                    emit_uproj(2 * (j - 1))
                    emit_uproj(2 * (j - 1) + 1)
                if Upend and j >= 2 * Upend[0][0] + 3:
                    emit_combos(*Upend.pop(0))
        if stop >= 2:
            emit_uproj(30)
            emit_uproj(31)
        while Upend:
            emit_combos(*Upend.pop(0))

        if stop <= 2:
            if dbg:
                dump(hTf[:, 0, :], BhTf, 2560)
                if stop == 2:
                    for v in range(5):
                        dump(var[:, v, 0, :], [Bvar[v][0]], 512)
            finish()
            return nc

        Zt = reg(BZ_OFF, BZ_SZ).rearrange("p (r g t) -> p r g t", r=2, g=4)
        BZ = [[[Buf("Z%d_%d_%d" % (r, g, v)) for v in range(4)] for g in range(4)] for r in range(2)]
        S.claim([b for r in BZ for gg in r for b in gg], BhT[18:30] + BmemT)
        tabs = [reg(D_OFF + s * 8192, 8192).rearrange("p (a i w) -> p a i w", a=2, i=8) for s in range(2)]
        Btab = [Buf("ftab%d" % s) for s in range(2)]
        S.claim(Btab, Bxt)
        CLS = {
            0: ([(0, 0)], [(0, 1)]),
            2: ([(1, 0)], [(1, 1)]),
            1: ([(2, 0), (3, 1)], [(4, 0), (2, 1)]),
            3: ([(2, 0), (4, 1)], [(3, 0), (2, 1)]),
        }
        for ci, v in enumerate([0, 2, 1, 3]):
            s = ci % 2
            S.dma("sp", tabs[s], fft_d[v], writes=[Btab[s]])
            for g in range(4):
                for ri in range(2):
                    terms = CLS[v][ri]
                    p, pB = next_ps()
                    n = len(terms) * 8
                    j = 0
                    for (vi, ab) in terms:
                        for i in range(8):
                            S.op("pe", lambda e, vi=vi, ab=ab, i=i, g=g, p=p, j=j, n=n, s=s: e.matmul(
                                p[:], lhsT=var[:, vi, i, g * 128:(g + 1) * 128], rhs=tabs[s][:, ab, i, :], start=(j == 0), stop=(j == n - 1)),
                                reads=[Bvar[vi][i], Btab[s]], writes=[pB], inc=(j == n - 1))
                            j += 1
                    zo = Zt[:, ri, g, :].rearrange("p (w v) -> p v w", v=4)[:, v, :]
                    S.op("act", lambda e, zo=zo, p=p: e.activation(out=zo, in_=p[:], func=AF.Copy), reads=[pB], writes=[BZ[ri][g][v]])
        BZg = [[BZ[r][g] for g in range(4)] for r in range(2)]

        if stop <= 3:
            if dbg:
                dump(Zt[:, 0, 0, :], BZ[0][0], 2048)
                dump(Zt[:, 1, 1, :], BZ[1][1], 2048)
            finish()
            return nc

        Va = reg(C_OFF, 10400).rearrange("p (f h e) -> p f h e", f=NFR, h=8)
        BVa = [Buf("Va%d" % f) for f in range(NFR)]
        qzz = reg(C_OFF + 10400, 4096).rearrange("p (t h q) -> p t h q", t=16, h=2)
        kT = reg(C_OFF + 10400 + 4096, 2560)
        BqT = [Buf("qT%d" % t) for t in range(4)]
        BkT = [Buf("kT%d" % t) for t in range(5)]
        allvar = [b for r in Bvar for b in r]
        S.claim(BVa + BqT + BkT, allvar)
        wv = reg(G_OFF, 4096).rearrange("p (k c) -> p k c", k=8)
        Bwv = Buf("wv")
        S.claim([Bwv], [Bwu])
        S.dma("pool", wv, wview(w_in)[:, :, 1024:1536], writes=[Bwv])
        nab = reg(E_OFF, 2 * NA_NCH * 128).rearrange("p (c h q) -> p c h q", h=2, c=NA_NCH)
        Bnab = Buf("nab")
        oT = reg(D_OFF, 8192).rearrange("p (k t) -> p k t", k=4)
        BoT = [[Buf("oT%d_%d" % (k, t)) for t in range(16)] for k in range(4)]
        omT = reg(D_OFF + 8192, 8192).rearrange("p (k t) -> p k t", k=4)
        BomT = [[Buf("omT%d_%d" % (k, t)) for t in range(4)] for k in range(4)]
        S.claim([b for r in BoT for b in r], [Btab[0]])
        S.claim([b for r in BomT for b in r], [Btab[1]])
        wf = reg(D_OFF + 8192, 4096).rearrange("p (g c) -> p g c", g=4)
        Bwf = Buf("wf")
        S.claim([Bwf], [Btab[1]])
        S.dma("pool", wf, w_f.rearrange("(g p) c -> p g c", p=128), writes=[Bwf])

        S.op("dve", lambda e: e.memset(Va[:, :, :, 64:65], 1.0), reads=[], writes=BVa)
        S.op("dve", lambda e: e.memset(reg(C_OFF + 10400, 4096), 0.0), reads=[], writes=BqT)
        for f in range(NFR):
            p, pB = next_ps()
            for k in range(8):
                S.op("pe", lambda e, k=k, f=f, p=p: e.matmul(p[:], lhsT=hTf[:, k, f * 128:(f + 1) * 128], rhs=wv[:, k, :], start=(k == 0), stop=(k == 7)),
                     reads=[BhTf[f], Bwv], writes=[pB], inc=(k == 7))
            S.op("act", lambda e, f=f, p=p: e.activation(out=Va[:, f, :, 0:64], in_=p[:].rearrange("p (h d) -> p h d", h=8), func=AF.Copy),
                 reads=[pB], writes=[BVa[f]])

        wsl = [reg(G_OFF + i * 1024, 1024).rearrange("p (k c) -> p k c", k=8) for i in range(4)]
        Bwsl = [Buf("wsl%d" % i) for i in range(4)]
        S.claim(Bwsl, [Bwv])

        H0, H1 = slice(0, 64), slice(64, 128)
        for hp in range(4):
            s = hp % 2
            wq, wk = wsl[2 * s], wsl[2 * s + 1]
            BwqB, BwkB = Bwsl[2 * s], Bwsl[2 * s + 1]
            S.dma("pool", wq, wview(w_in)[:, :, hp * 128:(hp + 1) * 128], writes=[BwqB])
            S.dma("pool", wk, wview(w_in)[:, :, 512 + hp * 128:512 + (hp + 1) * 128], writes=[BwkB])
            S.dma("pool", nab.rearrange("p c h q -> p (c h) q"), nab_d[hp], writes=[Bnab])
            qitems = []
            for tb in range(4):
                qitems.append(dict(w=wq, wB=BwqB, rhs_fn=(lambda k, tb=tb: hTf[:, k, 256 + tb * 512:256 + (tb + 1) * 512]),
                                   rhsBs=BhTf[2 + 4 * tb:6 + 4 * tb], mat=blk64, inv_d=1.0 / 64, gain=vec2[:, 0:1], gainB=Bvec2, outB=BqT[tb],
                                   split=[(H0, qzz[H0, 4 * tb:4 * tb + 4, 0, :]), (H1, qzz[H1, 4 * tb:4 * tb + 4, 1, :])]))
            kitems = []
            for fb in range(5):
                kitems.append(dict(w=wk, wB=BwkB, rhs_fn=(lambda k, fb=fb: hTf[:, k, fb * 512:(fb + 1) * 512]),
                                   rhsBs=BhTf[4 * fb:4 * fb + 4], mat=blk64, inv_d=1.0 / 64, gain=vec[:, V_KG:V_KG + 1], gainB=Bvec,
                                   out=kT[:, fb * 512:(fb + 1) * 512], outB=BkT[fb]))
            proj_norm_batch(kitems[0:3])
            proj_norm_batch(kitems[3:5] + qitems[0:1])
            proj_norm_batch(qitems[1:4])

            def emitS(tl):
                b0, nb_, voff = na_band(tl)
                par = tl % 2
                for j in range(nb_):
                    bk = 3 * par + j // 2
                    pp, ppB = ps[bk], Bps[bk]
                    col = (j % 2) * 256
                    f = b0 + j
                    last = (j % 2 == 1) or (j == nb_ - 1)
                    S.op("pe", lambda e, pp=pp, col=col, f=f, tl=tl: e.matmul(
                        pp[:, col:col + 256], lhsT=kT[:, f * 128:(f + 1) * 128], rhs=qzz[:, tl].rearrange("p h q -> p (h q)"), start=True, stop=False),
                        reads=[BkT[f // 4], BqT[tl // 4]], writes=[ppB], inc=False)
                    S.op("pe", lambda e, pp=pp, col=col, j=j, voff=voff: e.matmul(
                        pp[:, col:col + 256], lhsT=ident, rhs=nab[:, voff + j].rearrange("p h q -> p (h q)"), start=False, stop=True),
                        reads=[Bcst, Bnab], writes=[ppB], inc=last)
                Pt, PtB = PT[par], BPT[par]
                for bi in range((nb_ + 1) // 2):
                    bk = 3 * par + bi
                    ncol = min(512, (nb_ - 2 * bi) * 256)
                    S.op("act", lambda e, Pt=Pt, bk=bk, bi=bi, ncol=ncol: e.activation(out=Pt[:, bi * 512:bi * 512 + ncol], in_=ps[bk][:, 0:ncol], func=AF.Exp),
                         reads=[Bps[bk]], writes=[PtB])

            def emitPV(tl):
                b0, nb_, voff = na_band(tl)
                par = tl % 2
                po, poB = ps[6 + par], Bps[6 + par]
                Pt, PtB = PT[par], BPT[par]
                for hh in range(2):
                    h = 2 * hp + hh
                    for j in range(nb_):
                        f = b0 + j
                        S.op("pe", lambda e, po=po, hh=hh, j=j, f=f, h=h, Pt=Pt, nb_=nb_: e.matmul(
                            po[:, hh * 65:(hh + 1) * 65], lhsT=Pt[:, j * 256 + hh * 128:j * 256 + (hh + 1) * 128], rhs=Va[:, f, h, :],
                            start=(j == 0), stop=(j == nb_ - 1)),
                            reads=[PtB, BVa[f]], writes=[poB], inc=(j == nb_ - 1))

            def emitFin(tl, hp=hp):
                par = tl % 2
                po, poB = ps[6 + par], Bps[6 + par]
                pov = po[:, 0:130].rearrange("p (h e) -> p h e", h=2)
                S.op("dve", lambda e: e.reciprocal(out=rc[:, 2 * par:2 * par + 2], in_=pov[:, :, 64]), reads=[poB], writes=[Brc[par]])
                S.op("dve", lambda e: e.tensor_tensor(
                    out=onb[par][:].rearrange("p (h d) -> p h d", h=2), in0=pov[:, :, 0:64],
                    in1=rc[:, 2 * par:2 * par + 2].unsqueeze(2).to_broadcast([128, 2, 64]), op=ALU.mult),
                    reads=[poB, Brc[par]], writes=[Bonb[par]])
                ptb = po[:].bitcast(BF16)[:, 512:640]
                S.op("pe", lambda e: e.transpose(out=ptb, in_=onb[par][:], identity=ident), reads=[Bonb[par], Bcst], writes=[poB])
                S.op("dve", lambda e: e.tensor_copy(out=oT[:, hp, tl * 128:(tl + 1) * 128], in_=ptb), reads=[poB], writes=[BoT[hp][tl]])

            for i in range(16 + 2):
                if i < 16:
                    emitS(i)
                if 1 <= i <= 16:
                    emitPV(i - 1)
                if i >= 2:
                    emitFin(i - 2)

        if stop <= 4:
            if dbg:
                dump(oT[:, 0, :], BoT[0], 2048)
                dump(oT[:, 3, :], BoT[3], 2048)
            finish()
            return nc

        Wf2 = reg(E_OFF, 8192).rearrange("p (a g c) -> p a g c", a=2, g=4)
        BWf2 = Buf("Wf2")
        S.claim([BWf2], [Bnab])
        for a, M in enumerate([CcM, ScM]):
            for g in range(4):
                for half in range(2):
                    p, pB = next_ps()
                    S.op("pe", lambda e, p=p, M=M, g=g, half=half: e.matmul(p[:], lhsT=M, rhs=wf[:, g, half * 512:(half + 1) * 512], start=True, stop=True),
                         reads=[Bcst, Bwf], writes=[pB])
                    S.op("dve", lambda e, p=p, a=a, g=g, half=half: e.tensor_copy(out=Wf2[:, a, g, half * 512:(half + 1) * 512], in_=p[:]),
                         reads=[pB], writes=[BWf2])
        S.claim([b for r in BomT for b in r], [Bwf])
        f1s = [reg(G_OFF, 4096), reg(C_OFF + 16384, 4096)]
        Bf1 = [Buf("f1s0"), Buf("f1s1")]
        S.claim([Bf1[1]], BVa + BqT + BkT)

        def load_f1(dc):
            s_ = (dc + 1) % 2
            base = f1s[s_]
            gw = base[:, 0:3072].rearrange("p (b k c) -> p b k c", b=3, k=8)
            wo = base[:, 3072:4096].rearrange("p (b k c) -> p b k c", b=2, k=4)
            for br in range(3):
                S.dma("pool", gw[:, br], wview(w_in)[:, :, 2560 + br * 1024 + dc * 128:2560 + br * 1024 + (dc + 1) * 128],
                      writes=[Bf1[s_]] if br == 0 else [], reads=[], key=Bf1[s_])
            S.dma("pool", wo[:, 0], w_na_o.rearrange("(k p) c -> p k c", p=128)[:, :, dc * 128:(dc + 1) * 128], key=Bf1[s_])
            tokw = S.dma("pool", wo[:, 1], w_mem_o.rearrange("(k p) c -> p k c", p=128)[:, :, dc * 128:(dc + 1) * 128], key=Bf1[s_])
            Bf1[s_].lw = tokw

        load_f1(0)

        mqs = [[reg(C_OFF + (hs * 4 + tb) * 512, 512) for tb in range(4)] for hs in range(2)]
        Bmqs = [[Buf("mq%d_%d" % (hs, tb)) for tb in range(4)] for hs in range(2)]
        S.claim([b for r in Bmqs for b in r], BVa + BqT + BkT)

        def emitDPN(h):
            wmq, BwmqB = wsl[h % 4], Bwsl[h % 4]
            S.dma("pool", wmq, wview(w_in)[:, :, 2048 + h * 128:2048 + (h + 1) * 128], writes=[BwmqB])
            items = []
            for tb in range(4):
                items.append(dict(w=wmq, wB=BwmqB, rhs_fn=(lambda k, tb=tb: hTf[:, k, 256 + tb * 512:256 + (tb + 1) * 512]),
                                  rhsBs=BhTf[2 + 4 * tb:6 + 4 * tb], mat=ones, inv_d=1.0 / 128, gain=vec2[:, 1:2], gainB=Bvec2,
                                  out=mqs[h % 2][tb], outB=Bmqs[h % 2][tb]))
            proj_norm_batch(items)

        def emitDattn(h, tbs):
            st = []
            for tb in tbs:
                mq, mqB = mqs[h % 2][tb], Bmqs[h % 2][tb]
                pSs = []
                for c in range(2):
                    pS, pSB = next_ps()
                    S.op("pe", lambda e, pS=pS, c=c, mq=mq: e.matmul(pS[:], lhsT=kmT[:, h, c * 128:(c + 1) * 128], rhs=mq, start=True, stop=True),
                         reads=[BkmT, mqB], writes=[pSB])
                    pSs.append((pS, pSB))
                st.append(dict(tb=tb, pSs=pSs))
            for d_ in st:
                pts = []
                for (pS, pSB) in d_["pSs"]:
                    pt_, ptB_ = next_tb()
                    S.op("act", lambda e, pS=pS, pt_=pt_: e.activation(out=pt_[:], in_=pS[:], func=AF.Exp), reads=[pSB], writes=[ptB_])
                    pts.append((pt_, ptB_))
                d_["pts"] = pts
            for d_ in st:
                pts = d_["pts"]
                po, poB = next_ps()
                pq, pqB = next_ps()
                for c in range(2):
                    S.op("pe", lambda e, c=c, po=po, pts=pts: e.matmul(po[:], lhsT=vm[:, c, h * 128:(h + 1) * 128], rhs=pts[c][0][:], start=(c == 0), stop=(c == 1)),
                         reads=[Bvm, pts[c][1]], writes=[poB], inc=(c == 1))
                for c in range(2):
                    S.op("pe", lambda e, c=c, pq=pq, pts=pts: e.matmul(pq[:], lhsT=ones, rhs=pts[c][0][:], start=(c == 0), stop=(c == 1)),
                         reads=[Bcst, pts[c][1]], writes=[pqB], inc=(c == 1))
                d_["po"], d_["pq"] = (po, poB), (pq, pqB)
            for d_ in st:
                t, tB = next_tf()
                pq, pqB = d_["pq"]
                S.op("act", lambda e, t=t, pq=pq: e.activation(out=t[:], in_=pq[:], func=AF.Ln), reads=[pqB], writes=[tB])
                S.op("act", lambda e, t=t: e.activation(out=t[:], in_=t[:], func=AF.Exp, scale=-1.0), reads=[tB], writes=[tB])
                d_["t"] = (t, tB)
            for d_ in st:
                t, tB = d_["t"]
                po, poB = d_["po"]
                tb = d_["tb"]
                S.op("dve", lambda e, t=t, po=po, tb=tb: e.tensor_tensor(out=omT[:, h, tb * 512:(tb + 1) * 512], in0=po[:], in1=t[:], op=ALU.mult),
                     reads=[poB, tB], writes=[BomT[h][tb]])

        emitDPN(0)
        for h in range(4):
            if h + 1 < 4:
                emitDPN(h + 1)
            emitDattn(h, [0, 1])
            emitDattn(h, [2, 3])

        if stop <= 5:
            if dbg:
                dump(omT[:, 0, :], BomT[0], 2048)
                dump(omT[:, 3, :], BomT[3], 2048)
            finish()
            return nc

        mT = reg(C_OFF, 16384).rearrange("p (k t) -> p k t", k=8)
        BmT = [[Buf("mT%d_%d" % (k, t)) for t in range(4)] for k in range(8)]
        ncbufs = BVa + BqT + BkT + [b for r in Bmqs for b in r]
        S.claim([b for r in BmT for b in r], ncbufs)
        S.claim([Bf1[0]], [Bwv] + Bwsl)
        for dc in range(8):
            s = (dc + 1) % 2
            base = f1s[s]
            gw = base[:, 0:3072].rearrange("p (b k c) -> p b k c", b=3, k=8)
            wo = base[:, 3072:4096].rearrange("p (b k c) -> p b k c", b=2, k=4)
            if dc + 1 < 8:
                load_f1(dc + 1)
            for tb in range(4):
                tsl = slice(tb * 512, (tb + 1) * 512)
                acc = None
                for br in range(3):
                    pg, pgB = next_ps()
                    for k in range(8):
                        S.op("pe", lambda e, pg=pg, gw=gw, br=br, k=k, tb=tb: e.matmul(
                            pg[:], lhsT=gw[:, br, k, :], rhs=hTf[:, k, 256 + tb * 512:256 + (tb + 1) * 512], start=(k == 0), stop=(k == 7)),
                            reads=[Bf1[s]] + BhTf[2 + 4 * tb:6 + 4 * tb], writes=[pgB], inc=(k == 7))
                    py, pyB = next_ps()
                    if br == 0:
                        for k in range(4):
                            S.op("pe", lambda e, py=py, wo=wo, k=k, tsl=tsl: e.matmul(py[:], lhsT=wo[:, 0, k, :], rhs=oT[:, k, tsl], start=(k == 0), stop=(k == 3)),
                                 reads=[Bf1[s]] + BoT[k][4 * tb:4 * tb + 4], writes=[pyB], inc=(k == 3))
                    elif br == 1:
                        j = 0
                        for a in range(2):
                            for g in range(4):
                                S.op("pe", lambda e, py=py, a=a, g=g, dc=dc, tsl=tsl, j=j: e.matmul(
                                    py[:], lhsT=Wf2[:, a, g, dc * 128:(dc + 1) * 128], rhs=Zt[:, a, g, tsl], start=(j == 0), stop=(j == 7)),
                                    reads=[BWf2] + BZ[a][g], writes=[pyB], inc=(j == 7))
                                j += 1
                    else:
                        for k in range(4):
                            S.op("pe", lambda e, py=py, wo=wo, k=k, tsl=tsl: e.matmul(py[:], lhsT=wo[:, 1, k, :], rhs=omT[:, k, tsl], start=(k == 0), stop=(k == 3)),
                                 reads=[Bf1[s], BomT[k][tb]], writes=[pyB], inc=(k == 3))
                    sg, sgB = next_tf()
                    bcol = V_BG + br * 8 + dc
                    S.op("act", lambda e, sg=sg, pg=pg, bcol=bcol: e.activation(out=sg[:], in_=pg[:], func=AF.Sigmoid, bias=vec[:, bcol:bcol + 1]),
                         reads=[pgB, Bvec], writes=[sgB])
                    if br == 0:
                        S.op("dve", lambda e, sg=sg, py=py: e.tensor_tensor(out=sg[:], in0=sg[:], in1=py[:], op=ALU.mult), reads=[sgB, pyB], writes=[sgB])
                        acc, accB = sg, sgB
                    elif br == 1:
                        S.op("dve", lambda e, sg=sg, py=py: e.tensor_tensor(out=sg[:], in0=sg[:], in1=py[:], op=ALU.mult), reads=[sgB, pyB], writes=[sgB])
                        S.op("dve", lambda e, sg=sg, acc=acc: e.tensor_tensor(out=acc[:], in0=acc[:], in1=sg[:], op=ALU.add), reads=[sgB, accB], writes=[accB])
                    else:
                        S.op("dve", lambda e, sg=sg, py=py: e.tensor_tensor(out=sg[:], in0=sg[:], in1=py[:], op=ALU.mult), reads=[sgB, pyB], writes=[sgB])
                        S.op("dve", lambda e, sg=sg, acc=acc, dc=dc, tsl=tsl: e.tensor_tensor(out=mT[:, dc, tsl], in0=acc[:], in1=sg[:], op=ALU.add),
                             reads=[sgB, accB], writes=[BmT[dc][tb]])

        if stop <= 6:
            if dbg:
                dump(mT[:, 0, :], BmT[0], 2048)
                dump(mT[:, 7, :], BmT[7], 2048)
            finish()
            return nc

        wout = reg(C_OFF + 16384, 8192).rearrange("p (k c) -> p k c", k=8)
        Bwout = [Buf("wout0"), Buf("wout1")]
        S.claim([Bwout[0]], [Bf1[1]])
        S.claim([Bwout[1]], [Bf1[0]])
        S.dma("pool", wout[:, 0:4, :], wview(w_out)[:, 0:4, :], writes=[Bwout[0]])
        S.dma("pool", wout[:, 4:8, :], wview(w_out)[:, 4:8, :], writes=[Bwout[1]])
        x1 = reg(BZ_OFF, BZ_SZ + D_SZ).bitcast(F32).rearrange("p (t c) -> p t c", t=16)
        Bx1 = [[Buf("x1_%d_%d" % (t, hf)) for hf in range(2)] for t in range(16)]
        oldz = [b for r in BZ for gg in r for b in gg] + [b for r in BoT for b in r] + [b for r in BomT for b in r]
        S.claim([b for r in Bx1 for b in r], oldz)
        h2T = reg(A_OFF, 16384).rearrange("p (k t) -> p k t", k=8)
        Bh2 = [Buf("h2T%d" % t) for t in range(16)]
        S.claim(Bh2, BhTf)
        S.dma("sp", gbc[:], gbc_d[1], writes=[Bgbc])
        pend2 = {}
        NXF = 4
        xtf = [reg(E_OFF + i * 2048, 2048).bitcast(F32) for i in range(NXF)]
        Bxtf = [Buf("xtf%d" % i) for i in range(NXF)]
        S.claim(Bxtf, [BWf2])
        ctr["xtf"] = 0

        def emitX1p(j):
            srcs = []
            for t in (2 * j, 2 * j + 1):
                xi = nxt("xtf", NXF)
                S.dma("sp", xtf[xi], xr[t * 128:(t + 1) * 128, :], writes=[Bxtf[xi]])
                for hf in range(2):
                    p, pB = next_ps()
                    for k in range(8):
                        S.op("pe", lambda e, p=p, k=k, hf=hf, t=t: e.matmul(p[:], lhsT=mT[:, k, t * 128:(t + 1) * 128], rhs=wout[:, k, hf * 512:(hf + 1) * 512],
                                                                         start=(k == 0), stop=(k == 7)),
                             reads=[BmT[k][t // 4], Bwout[k // 4]], writes=[pB], inc=(k == 7))
                    S.op("dve", lambda e, p=p, hf=hf, xi=xi, t=t: e.tensor_tensor(out=x1[:, t, hf * 512:(hf + 1) * 512], in0=p[:], in1=xtf[xi][:, hf * 512:(hf + 1) * 512], op=ALU.add),
                         reads=[pB, Bxtf[xi]], writes=[Bx1[t][hf]])
                srcs.append((x1[:, t, :], Bx1[t]))
            xis = tok_norm_a2(srcs)
            pend2[2 * j], pend2[2 * j + 1] = xis

        def emitT2p2(j):
            for q_, t in enumerate((2 * j, 2 * j + 1)):
                tok_norm_b(pend2.pop(t), V_G2, h2T[:, :, t * 128:(t + 1) * 128], Bh2[t], ev=q_)

        emitX1p(0)
        for j in range(8):
            if j + 1 < 8:
                emitX1p(j + 1)
            emitT2p2(j)

        if stop <= 7:
            if dbg:
                dump(x1[:, 0, :], Bx1[0], 1024)
                dump(h2T[:, 0, :], Bh2, 2048)
            finish()
            return nc

        w1s = [reg(C_OFF + s * 8192, 4096).rearrange("p (k c) -> p k c", k=8) for s in range(2)]
        w2s = [reg(C_OFF + s * 8192 + 4096, 4096).rearrange("p (k c) -> p k c", k=4) for s in range(2)]
        Bw1 = [Buf("w1_%d" % s) for s in range(2)]
        Bw2 = [Buf("w2_%d" % s) for s in range(2)]
        S.claim(Bw1 + Bw2, [b for r in BmT for b in r])
        aTs = [reg(C_OFF + 16384, 8192).rearrange("p (k t) -> p k t", k=4), reg(E_OFF, 8192).rearrange("p (k t) -> p k t", k=4)]
        BaT = [[[Buf("aT%d_%d_%d" % (s, k, t)) for t in range(4)] for k in range(4)] for s in range(2)]
        S.claim([b for r in BaT[0] for b in r], Bwout)
        S.claim([b for r in BaT[1] for b in r], [BWf2] + Bxtf)
        NG = 8
        for grp in range(NG):
            s = grp % 2
            S.dma("pool", w1s[s], wview(w_ff1)[:, :, grp * 512:(grp + 1) * 512], writes=[Bw1[s]])
            S.dma("pool", w2s[s], w_ff2[grp * 512:(grp + 1) * 512, :].rearrange("(k p) c -> p k c", p=128), writes=[Bw2[s]])
            aT = aTs[s]
            for fc in range(4):
                for tb in range(4):
                    p, pB = next_ps()
                    for k in range(8):
                        S.op("pe", lambda e, p=p, k=k, fc=fc, tb=tb, s=s: e.matmul(p[:], lhsT=w1s[s][:, k, fc * 128:(fc + 1) * 128],
                                                                                 rhs=h2T[:, k, tb * 512:(tb + 1) * 512], start=(k == 0), stop=(k == 7)),
                             reads=[Bw1[s]] + Bh2[4 * tb:4 * tb + 4], writes=[pB], inc=(k == 7))
                    t, tB = next_tf()
                    S.op("act", lambda e, t=t, p=p: e.activation(out=t[:], in_=p[:], func=AF.Relu), reads=[pB], writes=[tB])
                    S.op("act", lambda e, t=t, aT=aT, fc=fc, tb=tb: e.activation(out=aT[:, fc, tb * 512:(tb + 1) * 512], in_=t[:], func=AF.Square),
                         reads=[tB], writes=[BaT[s][fc][tb]])
            for t in range(16):
                for hf in range(2):
                    p, pB = next_ps()
                    for fc in range(4):
                        S.op("pe", lambda e, p=p, fc=fc, t=t, hf=hf, s=s, aT=aT: e.matmul(p[:], lhsT=aT[:, fc, t * 128:(t + 1) * 128],
                                                                                      rhs=w2s[s][:, fc, hf * 512:(hf + 1) * 512], start=(fc == 0), stop=(fc == 3)),
                             reads=[BaT[s][fc][t // 4], Bw2[s]], writes=[pB], inc=(fc == 3))
                    S.op("dve", lambda e, p=p, t=t, hf=hf: e.tensor_tensor(out=x1[:, t, hf * 512:(hf + 1) * 512], in0=p[:], in1=x1[:, t, hf * 512:(hf + 1) * 512], op=ALU.add),
                         reads=[pB, Bx1[t][hf]], writes=[Bx1[t][hf]])
                if grp == NG - 1:
                    S.dma("sp", y[t * 128:(t + 1) * 128, :], x1[:, t, :], reads=Bx1[t], writes=[By[t]])
        finish()
    return nc


def _na_table(rpb, hf):
    rpb = np.asarray(rpb, np.float32)
    tab = np.full((8, NA_NCH, 128, 128), NEG, np.float32)
    kr2 = np.arange(128) // 64
    kc = np.arange(128) % 64
    qr2 = np.arange(128) // 64
    qc = np.arange(128) % 64
    cs = np.clip(qc - 8, 0, 48)
    tls = [0, 1, 2, 14, 15]
    for vi, tl in enumerate(tls):
        b0, nb_, voff = na_band(tl)
        t = 16 * hf + tl
        qrow = 2 * t + qr2
        rs = np.clip(qrow - 4, 0, 56)
        for j in range(nb_):
            f = b0 + j
            lt = (f - 2) % 32
            real = (2 <= f < 18) or (f < 2 and hf == 1) or (f >= 18 and hf == 0)
            if not real:
                continue
            gt = (lt + 16 * hf) % 32
            krow = 2 * gt + kr2
            dr = krow[:, None] - qrow[None, :]
            rowok = (krow[:, None] >= rs[None, :]) & (krow[:, None] < rs[None, :] + 8)
            colok = (kc[:, None] >= cs[None, :]) & (kc[:, None] < cs[None, :] + 16)
            ok = rowok & colok
            dri = np.clip(dr + 7, 0, 14)
            dci = np.clip(kc[:, None] - qc[None, :], -15, 15) + 15
            g = rpb[:, dri, dci]
            tab[:, voff + j] = np.where(ok[None], g, NEG)
    tab = tab.reshape(4, 2, NA_NCH, 128, 128).transpose(0, 3, 2, 1, 4).reshape(4, 128, 2 * NA_NCH, 128)
    return np.ascontiguousarray(tab)


def _fft_tables(hf):
    m = np.arange(1024, dtype=np.int64)
    out = np.zeros((4, 2, 1024, 512), np.float64)
    w = np.arange(512, dtype=np.int64)
    for v in range(4):
        sp = 2048 * hf + 4 * w + v
        ph = (m[:, None] * sp[None, :]) % 4096
        ang = 2.0 * np.pi * ph / 4096.0
        sign = -1.0 if (hf == 1 and v % 2 == 1) else 1.0
        out[v, 0] = sign * np.cos(ang)
        out[v, 1] = -sign * np.sin(ang)
    out = out.reshape(4, 2, 8, 128, 512).transpose(0, 3, 1, 2, 4)
    return np.ascontiguousarray(out.astype(np.float32)).astype(ml_dtypes.bfloat16)


def _consts():
    c = np.zeros((128, 5, 128), np.float64)
    c[:, 0] = np.eye(128)
    blk = np.zeros((128, 128))
    blk[:64, :64] = 1.0
    blk[64:, 64:] = 1.0
    c[:, 1] = blk
    c[:, 2] = 1.0
    i = np.arange(128)
    ang = 2.0 * np.pi * ((i[:, None] * i[None, :]) % 128) / 128.0
    nrm = 1.0 / np.sqrt(4096.0 * 128.0)
    c[:, 3] = np.cos(ang) * nrm
    c[:, 4] = np.sin(ang) * nrm
    return c.astype(np.float32).astype(ml_dtypes.bfloat16)


def _vecs(norm1_g, norm2_g, mem_norm_g, b_gate, na_q_g, na_k_g, mem_q_g, mem_k_g):
    v = np.zeros((128, NVEC), np.float32)
    v[:, V_G1:V_G1 + 8] = np.asarray(norm1_g, np.float32).reshape(8, 128).T
    v[:, V_G2:V_G2 + 8] = np.asarray(norm2_g, np.float32).reshape(8, 128).T
    v[:, V_GM:V_GM + 8] = np.asarray(mem_norm_g, np.float32).reshape(8, 128).T
    v[:, V_BG:V_BG + 24] = np.asarray(b_gate, np.float32).reshape(24, 128).T
    v[:, V_QG] = np.tile(np.asarray(na_q_g, np.float32), 2)
    v[:, V_KG] = np.tile(np.asarray(na_k_g, np.float32), 2)
    v[:, V_MQG] = np.asarray(mem_q_g, np.float32)
    v[:, V_MKG] = np.asarray(mem_k_g, np.float32)
    return v


def make_in_maps(x, mem, norm1_g, w_in, b_gate, na_q_g, na_k_g, na_rpb, w_na_o, w_f,
                 mem_norm_g, w_mem_kv, mem_q_g, mem_k_g, w_mem_o, w_out, norm2_g, w_ff1, w_ff2):
    f = lambda a: np.ascontiguousarray(np.asarray(a, np.float32))
    x = f(x)
    mem = f(mem)
    shared = {
        "w_in": f(w_in), "w_na_o": f(w_na_o), "w_f": f(w_f), "w_mem_o": f(w_mem_o), "w_mem_kv": f(w_mem_kv),
        "w_out": f(w_out), "w_ff1": f(w_ff1), "w_ff2": f(w_ff2),
        "vecs": _vecs(norm1_g, norm2_g, mem_norm_g, b_gate, na_q_g, na_k_g, mem_q_g, mem_k_g),
        "consts": _consts(),
        "gbc": np.ascontiguousarray(np.stack([np.broadcast_to(np.asarray(g, np.float32)[None, :], (128, D))
                                              for g in (norm1_g, norm2_g, mem_norm_g)])),
    }
    nat = [_na_table(na_rpb, hf) for hf in range(2)]
    fft = [_fft_tables(hf) for hf in range(2)]
    maps = []
    for c in range(8):
        b, hf = c // 2, c % 2
        m = dict(shared)
        m["xr"] = np.ascontiguousarray(np.roll(x[b], -NOWN * hf, axis=0))
        m["mem"] = mem[b]
        m["nabias"] = nat[hf]
        m["fftab"] = fft[hf]
        maps.append(m)
    return maps


_NC_CACHE = {}


def kernel(**inputs):
    maps = make_in_maps(**inputs)
    if "nc" not in _NC_CACHE:
        _NC_CACHE["nc"] = build()
    nc = _NC_CACHE["nc"]
    res = run_bass_kernel_spmd(nc, maps, core_ids=list(range(8)))
    out = np.zeros((4, SEQ, D), np.float32)
    for c in range(8):
        b, hf = c // 2, c % 2
        out[b, NOWN * hf:NOWN * (hf + 1)] = res.results[c]["y"]
    return out
```
